# Optimizing a Trainium2 kernel written in Bass

```python
import jax, jax.numpy as jnp
from jax import lax
import numpy as np

D_MODEL = 2048
BATCH = 4
SEQ = 4096
DEPTH = 1

HEAD_DIM = 128
ATTN_HEADS = 8
MLSTM_HEADS = 8
ATTN_WIDTH = ATTN_HEADS * HEAD_DIM
MLSTM_WIDTH = MLSTM_HEADS * HEAD_DIM
MIX_WIDTH = ATTN_WIDTH + MLSTM_WIDTH
ATTN_PATTERNS = ((128, 1), (512, 4), (2048, 16))
ATTN_BLOCK = 128
MLSTM_CHUNK = 128
CONV_WIDTH = 4
FFN_HIDDEN = ((8 * D_MODEL + 3 * 256 - 1) // (3 * 256)) * 256
IN_WIDTH = 3 * ATTN_WIDTH + 4 * MLSTM_WIDTH + 2 * MLSTM_HEADS
EPS = 1e-6
MASK_VALUE = -1e30
M_INIT = -1e30

kernel_name = "hybrid_dilated_attn_mlstm_block"


def rmsnorm(x, g):
    x32 = x.astype(jnp.float32)
    y = x32 * lax.rsqrt(jnp.mean(x32 * x32, axis=-1, keepdims=True) + EPS)
    return (y * g.astype(jnp.float32)).astype(x.dtype)


def causal_depthwise_conv(u, w, b):
    c = u.shape[-1]
    y = lax.conv_general_dilated(u, w[:, None, :].astype(u.dtype), window_strides=(1,),
                                 padding=[(CONV_WIDTH - 1, 0)],
                                 dimension_numbers=("NWC", "WIO", "NWC"),
                                 feature_group_count=c)
    return y + b.astype(u.dtype)


def strided_window_attention(q, k, v, span, dil):
    assert span <= ATTN_BLOCK
    B, S, H, Dh = q.shape
    n = S // dil
    nb = -(-n // ATTN_BLOCK)
    n_pad = nb * ATTN_BLOCK

    def to_blocks(t):
        t = t.reshape(B, n, dil, H, Dh)
        t = jnp.pad(t, ((0, 0), (0, n_pad - n), (0, 0), (0, 0), (0, 0)))
        return t.reshape(B, nb, ATTN_BLOCK, dil, H, Dh)

    def with_prev(t):
        prev = jnp.pad(t, ((0, 0), (1, 0), (0, 0), (0, 0), (0, 0), (0, 0)))[:, :-1]
        return jnp.concatenate([prev, t], axis=2)

    qb = to_blocks(q)
    kw = with_prev(to_blocks(k))
    vw = with_prev(to_blocks(v))
    scores = jnp.einsum("bnqrhd,bnkrhd->bnrhqk", qb, kw).astype(jnp.float32)
    q_idx = jnp.arange(ATTN_BLOCK)[:, None]
    k_idx = jnp.arange(2 * ATTN_BLOCK)[None, :] - ATTN_BLOCK
    dist = q_idx - k_idx
    blk = jnp.arange(nb)[:, None, None]
    valid = (dist >= 0) & (dist <= span) & (blk * ATTN_BLOCK + k_idx >= 0)
    scores = jnp.where(valid[None, :, None, None], scores, MASK_VALUE)
    m = jnp.max(scores, axis=-1, keepdims=True)
    p = jnp.exp(scores - m)
    s = jnp.sum(p, axis=-1, keepdims=True)
    o = jnp.einsum("bnrhqk,bnkrhd->bnqrhd", p / s, vw.astype(jnp.float32))
    lse = (m + jnp.log(s))[..., 0]
    o = o.reshape(B, n_pad, dil, H, Dh)[:, :n].reshape(B, S, H, Dh)
    lse = lse.transpose(0, 1, 4, 2, 3).reshape(B, n_pad, dil, H)[:, :n].reshape(B, S, H)
    return o, lse


def dilated_window_attention(q, k, v):
    outs, lses = [], []
    for window, dil in ATTN_PATTERNS:
        o, lse = strided_window_attention(q, k, v, window // dil, dil)
        outs.append(o)
        lses.append(lse)
    alpha = jax.nn.softmax(jnp.stack(lses, axis=0), axis=0)
    out = jnp.sum(alpha[..., None] * jnp.stack(outs, axis=0), axis=0)
    return out.astype(q.dtype)


def mlstm_chunkwise(q, k, v, i_pre, f_pre):
    B, S, H, Dh = q.shape
    L = MLSTM_CHUNK
    nc = S // L

    def chunks(t):
        return t.astype(jnp.float32).transpose(0, 2, 1, 3).reshape(B, H, nc, L, Dh)

    def gchunks(t):
        return t.astype(jnp.float32).transpose(0, 2, 1).reshape(B, H, nc, L)

    qc, kc, vc = chunks(q), chunks(k) * (Dh ** -0.5), chunks(v)
    ic = gchunks(i_pre)
    b = jnp.cumsum(jax.nn.log_sigmoid(gchunks(f_pre)), axis=-1)

    b_last = b[..., -1]
    a = b_last[..., None] - b + ic
    m_chunk = jnp.max(a, axis=-1)
    wa = jnp.exp(a - m_chunk[..., None])
    kv_chunk = jnp.einsum("bhcl,bhcld,bhcle->bhcde", wa, kc, vc)
    n_chunk = jnp.einsum("bhcl,bhcld->bhcd", wa, kc)

    def step(carry, xs):
        C, nvec, m = carry
        bl, mc, kvc, nch = xs
        m_new = jnp.maximum(bl + m, mc)
        decay = jnp.exp(bl + m - m_new)
        scale = jnp.exp(mc - m_new)
        C_new = decay[..., None, None] * C + scale[..., None, None] * kvc
        n_new = decay[..., None] * nvec + scale[..., None] * nch
        return (C_new, n_new, m_new), (C, nvec, m)

    init = (jnp.zeros((B, H, Dh, Dh), jnp.float32), jnp.zeros((B, H, Dh), jnp.float32),
            jnp.full((B, H), M_INIT, jnp.float32))
    xs = (jnp.moveaxis(b_last, 2, 0), jnp.moveaxis(m_chunk, 2, 0),
          jnp.moveaxis(kv_chunk, 2, 0), jnp.moveaxis(n_chunk, 2, 0))
    _, (C_prev, n_prev, m_prev) = lax.scan(step, init, xs)
    C_prev = jnp.moveaxis(C_prev, 0, 2)
    n_prev = jnp.moveaxis(n_prev, 0, 2)
    m_prev = jnp.moveaxis(m_prev, 0, 2)

    causal = jnp.tril(jnp.ones((L, L), dtype=bool))
    log_d = jnp.where(causal, b[..., :, None] - b[..., None, :] + ic[..., None, :], MASK_VALUE)
    inter = b + m_prev[..., None]
    m_t = jnp.maximum(inter, jnp.max(log_d, axis=-1))
    w = jnp.exp(log_d - m_t[..., None]) * jnp.einsum("bhcld,bhcsd->bhcls", qc, kc)
    g = jnp.exp(inter - m_t)
    num = g[..., None] * jnp.einsum("bhcld,bhcde->bhcle", qc, C_prev) + jnp.einsum("bhcls,bhcse->bhcle", w, vc)
    den = g * jnp.einsum("bhcld,bhcd->bhcl", qc, n_prev) + jnp.sum(w, axis=-1)
    h = num / jnp.maximum(jnp.abs(den), jnp.exp(-m_t))[..., None]
    return h.reshape(B, H, S, Dh).transpose(0, 2, 1, 3).astype(q.dtype)


def setup_inputs(seed: int = 0) -> dict:
    key = jax.random.key(seed)
    ks = jax.random.split(key, 16)

    def normal(k, shape, scale):
        return jax.random.normal(k, shape, jnp.float32) * scale

    x = normal(ks[0], (BATCH, SEQ, D_MODEL), 1.0)
    norm_mix_g = 1.0 + normal(ks[1], (DEPTH, D_MODEL), 0.02)
    w_in = normal(ks[2], (DEPTH, D_MODEL, IN_WIDTH), D_MODEL ** -0.5)
    conv_w = normal(ks[3], (DEPTH, CONV_WIDTH, 2 * MLSTM_WIDTH), CONV_WIDTH ** -0.5)
    conv_b = normal(ks[4], (DEPTH, 2 * MLSTM_WIDTH), 0.02)
    i_bias = normal(ks[5], (DEPTH, MLSTM_HEADS), 0.1)
    f_bias = jnp.linspace(3.0, 6.0, MLSTM_HEADS, dtype=jnp.float32)[None] + normal(ks[6], (DEPTH, MLSTM_HEADS), 0.1)
    gate_b = jnp.concatenate([i_bias, f_bias], axis=-1)
    q_norm_g = 1.0 + normal(ks[7], (DEPTH, HEAD_DIM), 0.02)
    k_norm_g = 1.0 + normal(ks[8], (DEPTH, HEAD_DIM), 0.02)
    mlstm_norm_g = 1.0 + normal(ks[9], (DEPTH, MLSTM_HEADS, HEAD_DIM), 0.02)
    w_out = normal(ks[10], (DEPTH, MIX_WIDTH, D_MODEL), MIX_WIDTH ** -0.5)
    norm_ffn_g = 1.0 + normal(ks[11], (DEPTH, D_MODEL), 0.02)
    w_gate = normal(ks[12], (DEPTH, D_MODEL, FFN_HIDDEN), D_MODEL ** -0.5)
    w_up = normal(ks[13], (DEPTH, D_MODEL, FFN_HIDDEN), D_MODEL ** -0.5)
    w_down = normal(ks[14], (DEPTH, FFN_HIDDEN, D_MODEL), FFN_HIDDEN ** -0.5)
    return {"x": x, "norm_mix_g": norm_mix_g, "w_in": w_in, "conv_w": conv_w, "conv_b": conv_b,
            "gate_b": gate_b, "q_norm_g": q_norm_g, "k_norm_g": k_norm_g, "mlstm_norm_g": mlstm_norm_g,
            "w_out": w_out, "norm_ffn_g": norm_ffn_g, "w_gate": w_gate, "w_up": w_up, "w_down": w_down}


def reference(x, norm_mix_g, w_in, conv_w, conv_b, gate_b, q_norm_g, k_norm_g, mlstm_norm_g,
              w_out, norm_ffn_g, w_gate, w_up, w_down):
    B, S, _ = x.shape
    offsets = [ATTN_WIDTH * i for i in (1, 2, 3)] + [3 * ATTN_WIDTH + MLSTM_WIDTH * i for i in (1, 2, 3, 4)]

    def heads(t, n_heads):
        return t.reshape(B, S, n_heads, HEAD_DIM)

    for layer in range(DEPTH):
        h = rmsnorm(x, norm_mix_g[layer])
        proj = h @ w_in[layer]
        aq, ak, av, mq, mk, mv, mo, gates = jnp.split(proj, offsets, axis=-1)

        aq = rmsnorm(heads(aq, ATTN_HEADS), q_norm_g[layer]) * (HEAD_DIM ** -0.5)
        ak = rmsnorm(heads(ak, ATTN_HEADS), k_norm_g[layer])
        attn_out = dilated_window_attention(aq, ak, heads(av, ATTN_HEADS)).reshape(B, S, ATTN_WIDTH)

        mqk = jax.nn.silu(causal_depthwise_conv(jnp.concatenate([mq, mk], axis=-1), conv_w[layer], conv_b[layer]))
        mq, mk = jnp.split(mqk, 2, axis=-1)
        gates = gates + gate_b[layer]
        cell = mlstm_chunkwise(heads(mq, MLSTM_HEADS), heads(mk, MLSTM_HEADS), heads(mv, MLSTM_HEADS),
                               gates[..., :MLSTM_HEADS], gates[..., MLSTM_HEADS:])
        mlstm_out = jax.nn.sigmoid(mo) * rmsnorm(cell, mlstm_norm_g[layer]).reshape(B, S, MLSTM_WIDTH)

        x = x + jnp.concatenate([attn_out, mlstm_out], axis=-1) @ w_out[layer]

        h = rmsnorm(x, norm_ffn_g[layer])
        x = x + (jax.nn.silu(h @ w_gate[layer]) * (h @ w_up[layer])) @ w_down[layer]
    return x
```

```python
import numpy as np
import ml_dtypes
import concourse.bass as bass
import concourse.mybir as mybir
from concourse.bass_utils import run_bass_kernel_spmd

F32 = mybir.dt.float32
BF16 = mybir.dt.bfloat16
AF = mybir.ActivationFunctionType
ALU = mybir.AluOpType
AX = mybir.AxisListType

P = 128
D = 2048
KC = 16
TO = 2048
TC = 2048
T = TO + TC
NH = 8
HD = 128
INW = 7184
FH = 5632
FKC = 44
EPS = 1e-6
MNEG = -30000.0
QS = HD ** -0.5

PP_CONVW = 0
PP_CONVB = 64
PP_QG = 80
PP_KG = 81
PP_CBIAS = 82
PP_EPS = 83
PP_IDENT = 84
PP_TRINEG = 212
PP_NEGONES = 340
PP_ONES = 468
PP_N = 596
RP_G1 = 0
RP_G2 = 2048
RP_MG = 4096
RP_GB = 5120
RP_QG = 5136
RP_KG = 5264
RP_N = 5392
CB_IDENT = 0
CB_ONES = 128
CB_MASKA = 256
CB_MASKC = 512
CB_MBA = 768
CB_MBC = 1024
CB_N = 1280


class Buf:
    def __init__(self, name, t=None, multi=False):
        self.name = name
        self.t = t
        self.w = []
        self.rs = []
        self.multi = multi
        self.dsem = None
        self.dcount = 0


class Eng:
    def __init__(self, name, h, sem):
        self.name = name
        self.h = h
        self.sem = sem
        self.n = 0
        self.seen = {}


class Sync:
    def __init__(self, nc):
        self.nc = nc
        self.eng = {}
        for name, h in (("pe", nc.tensor), ("act", nc.scalar), ("dve", nc.vector),
                        ("pool", nc.gpsimd), ("sp", nc.sync)):
            self.eng[name] = Eng(name, h, nc.alloc_semaphore("prog_" + name))
        self.nsem = 5
        self.ninst = 0
        self.dma_sems = {}

    def _wait(self, E, ev):
        sem, val, who = ev
        k = id(sem)
        if E.seen.get(k, 0) >= val:
            return
        E.h.wait_ge(sem, val)
        E.seen[k] = val
        self.ninst += 1

    def _deps(self, en, reads, writes):
        deps = []
        for b in reads:
            deps += b.w
        for b in writes:
            deps += b.rs
            if not b.multi:
                deps += b.w
        return deps

    def op(self, en, fn, reads=(), writes=(), inc=True):
        E = self.eng[en]
        for b in reads:
            for ev in b.w:
                if ev[2] == en and en == "pe":
                    continue
                self._wait(E, ev)
        for b in writes:
            for ev in b.rs:
                if ev[2] == en:
                    continue
                self._wait(E, ev)
            for ev in b.w:
                if ev[2] == en:
                    continue
                self._wait(E, ev)
        ins = fn(E.h)
        self.ninst += 1
        if inc:
            E.n += 1
            ins.then_inc(E.sem, 1)
            ev = (E.sem, E.n, en)
        else:
            ev = (E.sem, E.n + 1, en)
        for b in reads:
            b.rs = [r for r in b.rs if r[2] != en] + [ev]
        for b in writes:
            b.w = [ev]
            b.rs = []
        return ins

    def dma(self, qn, out, in_, reads=(), writes=(), sembuf=None):
        E = self.eng[qn]
        for ev in self._deps(qn, reads, writes):
            self._wait(E, ev)
        if sembuf.dsem is None:
            sembuf.dsem = self.nc.alloc_semaphore("d_" + sembuf.name)
            sembuf.dq = qn
            self.nsem += 1
        assert sembuf.dq == qn, f"DMA semaphore of {sembuf.name} shared between queues {sembuf.dq} and {qn}"
        sembuf.dcount += 16
        E.h.dma_start(out=out, in_=in_).then_inc(sembuf.dsem, 16)
        self.ninst += 1
        ev = (sembuf.dsem, sembuf.dcount, "dma:" + sembuf.name)
        self.dma_sems[id(sembuf.dsem)] = (sembuf.dsem, sembuf.dcount)
        for b in reads:
            b.rs = b.rs + [ev]
        for b in writes:
            if b.multi:
                b.w = [w for w in b.w if w[0] is not sembuf.dsem] + [ev]
            else:
                b.w = [ev]
                b.rs = []

    def barrier(self):
        engs = list(self.eng.values())
        for E in engs:
            for O in engs:
                if O.n > 0:
                    self._wait(E, (O.sem, O.n, O.name))
            for sem, cnt in self.dma_sems.values():
                self._wait(E, (sem, cnt, "dma"))

    def wait_all(self, qn, bufs):
        E = self.eng[qn]
        for b in bufs:
            for ev in b.w + b.rs:
                self._wait(E, ev)


def mm_group(S, out, pairs, reads, writes):
    n = len(pairs)
    for i, (l, r) in enumerate(pairs):
        S.op("pe", lambda e, l=l, r=r, i=i: e.matmul(out, l, r, start=(i == 0), stop=(i == n - 1)),
             reads=reads, writes=writes, inc=(i == n - 1))


def build(dbg=False, stop=None):
    nc = bass.Bass("TRN2", target_bir_lowering=False)
    S = Sync(nc)

    def din(name, shape, dt=F32):
        return nc.dram_tensor(name, list(shape), dt, kind="ExternalInput")

    x_loc = din("x_loc", [T, D])
    w_in = din("w_in", [D, INW])
    w_out = din("w_out", [D, D])
    w_gate = din("w_gate", [D, FH])
    w_up = din("w_up", [D, FH])
    w_down = din("w_down", [FH, D])
    pp_d = din("pp", [P, PP_N])
    rp_d = din("rp", [P, RP_N])
    cb_d = din("cb", [P, CB_N], BF16)
    out_d = nc.dram_tensor("out", [TO, D], F32, kind="ExternalOutput")

    sk = "ExternalOutput" if dbg else "Internal"
    s_aq = nc.dram_tensor("s_aq", [NH, P, TO], BF16, kind=sk)
    s_ak = nc.dram_tensor("s_ak", [NH, P, T], BF16, kind=sk)
    s_av = nc.dram_tensor("s_av", [T, NH * HD], BF16, kind=sk)
    s_mq = nc.dram_tensor("s_mq", [NH, P, TO], BF16, kind=sk)
    s_mk = nc.dram_tensor("s_mk", [NH, P, T], BF16, kind=sk)
    s_mv = nc.dram_tensor("s_mv", [T, NH * HD], BF16, kind=sk)
    s_mo = nc.dram_tensor("s_mo", [TO, NH * HD], BF16, kind=sk)
    s_x1 = nc.dram_tensor("s_x1", [TO, D], F32, kind=sk)
    s_h2 = nc.dram_tensor("s_h2", [16, P, KC * P], BF16, kind="Internal")
    dbg_outs = ["s_aq", "s_ak", "s_av", "s_mq", "s_mk", "s_mv", "s_mo", "s_x1"]
    if dbg:
        d_gates = nc.dram_tensor("d_gates", [P, 32 * 16], F32, kind="ExternalOutput")
        d_gp = nc.dram_tensor("d_gp", [P, 6 * 256], F32, kind="ExternalOutput")
        d_mixT = nc.dram_tensor("d_mixT", [P, 16 * TO], BF16, kind="ExternalOutput")
        dbg_outs += ["d_gates", "d_gp", "d_mixT"]
    B_aq, B_ak, B_av = Buf("s_aq", multi=True), Buf("s_ak", multi=True), Buf("s_av", multi=True)
    B_mq, B_mk, B_mv = Buf("s_mq", multi=True), Buf("s_mk", multi=True), Buf("s_mv", multi=True)
    B_mo, B_x1, B_out = Buf("s_mo", multi=True), Buf("s_x1", multi=True), Buf("out", multi=True)
    B_dbg = Buf("dbg", multi=True)
    B_h2 = Buf("s_h2", multi=True)

    def sb(name, shape, dt):
        return nc.sbuf_tensor(name, list(shape), dt)

    uid = [0]

    def SB(stack, name, shape, dt):
        uid[0] += 1
        name = f"{name}_{uid[0]}"
        t = stack.enter_context(nc.sbuf_tensor(name, list(shape), dt))
        return Buf(name, t)

    def PS(stack, name, shape, dt=F32):
        uid[0] += 1
        name = f"{name}_{uid[0]}"
        t = stack.enter_context(nc.psum_tensor(name, list(shape), dt))
        return Buf(name, t)

    from contextlib import ExitStack, contextmanager

    @contextmanager
    def stage():
        with ExitStack() as es:
            yield es
            S.barrier()

    with ExitStack() as top:
        pp = SB(top, "pp_sb", [P, PP_N], F32)
        cb = SB(top, "cbc", [P, CB_N], BF16)
        smallp = SB(top, "smallp", [P, 16 + 128 + 128 + 8], F32)
        gates_all = SB(top, "gates_all", [P, 32, 16], F32)
        halo = SB(top, "halo", [P, 16, 3], F32)
        gpU = SB(top, "gpU", [P, 32, 8], F32)
        gpUS = SB(top, "gpUS", [P, 32, 8], F32)
        gpG = SB(top, "gpG", [P, 32, 8], F32)
        gpGS = SB(top, "gpGS", [P, 32, 8], F32)
        gpEL = SB(top, "gpEL", [P, 32, 8], F32)
        S.dma("sp", pp.t[:], pp_d.ap(), writes=[pp], sembuf=pp)
        S.dma("sp", cb.t[:], cb_d.ap(), writes=[cb], sembuf=cb)
        S.dma("sp", smallp.t[:, 0:272], rp_d.ap()[:, RP_GB:RP_GB + 272], writes=[smallp], sembuf=smallp)
        ident = cb.t[:, CB_IDENT:CB_IDENT + 128]
        ones_b = cb.t[:, CB_ONES:CB_ONES + 128]
        maskA = cb.t[:, CB_MASKA:CB_MASKA + 256]
        maskC = cb.t[:, CB_MASKC:CB_MASKC + 256]
        maskLE = cb.t[:, CB_MASKA + 128:CB_MASKA + 256]
        mbA = cb.t[:, CB_MBA:CB_MBA + 256]
        mbC = cb.t[:, CB_MBC:CB_MBC + 256]
        ident_f = pp.t[:, PP_IDENT:PP_IDENT + 128]
        trineg_f = pp.t[:, PP_TRINEG:PP_TRINEG + 128]
        negones_f = pp.t[:, PP_NEGONES:PP_NEGONES + 128]
        ones_f = pp.t[:, PP_ONES:PP_ONES + 128]
        eps_col = pp.t[:, PP_EPS:PP_EPS + 1]
        qgs = smallp.t[:, 272:273]
        negshift = smallp.t[:, 273:274]
        S.op("dve", lambda e: e.tensor_scalar(qgs, pp.t[:, PP_QG:PP_QG + 1], QS, None, ALU.mult),
             reads=[pp], writes=[smallp])
        S.op("dve", lambda e: e.reduce_max(smallp.t[:, 274:275], smallp.t[:, 16:144], axis=AX.X,
                                           apply_absolute_value=True), reads=[smallp], writes=[smallp])
        S.op("dve", lambda e: e.reduce_max(smallp.t[:, 275:276], smallp.t[:, 144:272], axis=AX.X,
                                           apply_absolute_value=True), reads=[smallp], writes=[smallp])
        S.op("dve", lambda e: e.scalar_tensor_tensor(negshift, smallp.t[:, 274:275], -(HD ** 0.5),
                                                     smallp.t[:, 275:276], ALU.mult, ALU.mult),
             reads=[smallp], writes=[smallp])

        def norm_transpose(st, src_row_ap, src_buf, ntiles, grow, dstT, dst_bufs, tag):
            xb = [SB(st, f"xb{tag}{i}", [P, D], F32) for i in range(3)]
            xn = [SB(st, f"xn{tag}{i}", [P, D], BF16) for i in range(3)]
            junk = SB(st, f"junk{tag}", [P, D], BF16)
            stt = [SB(st, f"st{tag}{i}", [P, 8], F32) for i in range(3)]
            tp = [PS(st, f"tp{tag}{i}", [P, KC, P], BF16) for i in range(2)]
            dst_bufs2 = dst_bufs[len(dst_bufs) // 2:]
            dst_bufs = dst_bufs[:len(dst_bufs) // 2]
            def nA(i):
                j = i % 3
                jn = i % 3
                S.dma("sp", xb[j].t[:], src_row_ap(i), reads=[src_buf] if src_buf else [],
                      writes=[xb[j]], sembuf=xb[j])
                S.op("act", lambda e: e.activation(out=junk.t[:], in_=xb[j].t[:], func=AF.Square,
                                                   accum_out=stt[jn].t[:, 0:1]),
                     reads=[xb[j]], writes=[junk, stt[jn]])
                S.op("dve", lambda e: e.tensor_scalar(stt[jn].t[:, 1:2], stt[jn].t[:, 0:1], 1.0 / D, EPS,
                                                      ALU.mult, ALU.add), reads=[stt[jn]], writes=[stt[jn]])
                S.op("act", lambda e: e.activation(out=stt[jn].t[:, 2:3], in_=stt[jn].t[:, 1:2], func=AF.Sqrt),
                     reads=[stt[jn]], writes=[stt[jn]])
                S.op("dve", lambda e: e.reciprocal(stt[jn].t[:, 3:4], stt[jn].t[:, 2:3]),
                     reads=[stt[jn]], writes=[stt[jn]])
                S.op("dve", lambda e: e.scalar_tensor_tensor(xn[jn].t[:], xb[j].t[:], stt[jn].t[:, 3:4],
                                                             grow.t[:], ALU.mult, ALU.mult),
                     reads=[xb[j], stt[jn], grow], writes=[xn[jn]])

            def nB(i):
                jn = i % 3
                jt = i % 2
                for k in range(KC):
                    S.op("pe", lambda e, k=k: e.transpose(tp[jt].t[:, k, :], xn[jn].t[:, k * P:(k + 1) * P], ident),
                         reads=[xn[jn], cb], writes=[tp[jt]], inc=(k == KC - 1))
                S.op("act", lambda e: e.copy(out=dstT.t[:, :, i * P:(i + 1) * P], in_=tp[jt].t[:]),
                     reads=[tp[jt]], writes=[dst_bufs[i], dst_bufs2[i]])

            for i in range(ntiles + 2):
                if i < ntiles:
                    nA(i)
                if i >= 2:
                    nB(i - 2)

        def w_src(wd, ncols_total, col0, ncols, nk=KC, row0=0):
            return bass.AP(wd, row0 * ncols_total + col0, [[ncols_total, P], [P * ncols_total, nk], [1, ncols]])

        with stage() as st:
            hT = SB(st, "hT", [P, KC, 2048], BF16)
            hT_b = [Buf(f"hTb{i}") for i in range(16)]
            hT_b2 = [Buf(f"hTc{i}") for i in range(16)]
            grow1 = SB(st, "grow1", [P, D], F32)
            S.dma("sp", grow1.t[:], rp_d.ap()[:, RP_G1:RP_G1 + D], writes=[grow1], sembuf=grow1)
            wb = [SB(st, f"wb{i}", [P, KC, 512], BF16) for i in range(2)]
            wg16 = SB(st, "wg16", [P, KC, 16], BF16)
            S.dma("pool", wg16.t[:], w_src(w_in, INW, 7168, 16), writes=[wg16], sembuf=wg16)
            stg = [SB(st, f"stg{i}", [P, 2048], BF16) for i in range(2)]
            vst = [SB(st, f"vst{i}", [P, 512], BF16) for i in range(4)]
            cbuf = [SB(st, f"cbuf{i}", [P, 515], F32) for i in range(2)]
            cacc = [SB(st, f"cacc{i}", [P, 512], F32) for i in range(2)]
            sq = [SB(st, f"sq{i}", [P, 512], BF16) for i in range(2)]
            sv = [SB(st, f"sv{i}", [P, 512], F32) for i in range(2)]
            sigf = [SB(st, f"sigf{i}", [P, 512], F32) for i in range(2)]
            mgrow1 = SB(st, "mgrow1", [P, 1024], F32)
            S.dma("sp", mgrow1.t[:], rp_d.ap()[:, RP_MG:RP_MG + 1024], writes=[mgrow1], sembuf=mgrow1)
            S.op("dve", lambda e: e.memset(halo.t[:], 0.0), writes=[halo])
            cnt = {"stg": 0, "vst": 0, "r": 0}

            for is_ctx in (True, False):
                tok0 = 0 if is_ctx else TC
                pre_cols = (1024, 1536) if is_ctx else (0, 512)
                prefetched = stop != "inproj_small"
                if prefetched:
                    for gi_, col_ in enumerate(pre_cols):
                        S.dma("pool", wb[gi_].t[:], w_src(w_in, INW, col_, 512), writes=[wb[gi_]], sembuf=wb[gi_])
                with stage() as st1:
                    norm_transpose(st1, lambda i: x_loc.ap()[tok0 + i * P: tok0 + (i + 1) * P, :], None, 16,
                                   grow1, hT, hT_b + hT_b2, "a")
                with stage() as st2:
                    psA = [PS(st2, f"psA{i}", [P, 512]) for i in range(4)]
                    psB = [PS(st2, f"psB{i}", [P, 512]) for i in range(2)]
                    psG = PS(st2, "psG", [P, 512])
                    groups = []
                    if not is_ctx:
                        groups += [("aq", 0, 0), ("aq", 512, 4)]
                    groups += [("ak", 1024, 0), ("ak", 1536, 4)]
                    groups += [("mq", 3072, 0), ("mq", 3584, 4)]
                    groups += [("mk", 4096, 0), ("mk", 4608, 4)]
                    groups += [("av", 2048, 0), ("av", 2560, 1), ("mv", 5120, 0), ("mv", 5632, 1)]
                    if not is_ctx:
                        groups += [("mo", 6144, 0), ("mo", 6656, 1)]
                    if stop == "inproj_small":
                        groups = groups[:1] + [g for g in groups if g[0] == "av"][:1]

                    def load_w(gi):
                        kind, col0, _ = groups[gi]
                        S.dma("pool", wb[gi % 2].t[:], w_src(w_in, INW, col0, 512), writes=[wb[gi % 2]],
                              sembuf=wb[gi % 2])

                    if not prefetched:
                        load_w(0)
                    psi = 0
                    pending = []
                    for gi, (kind, col0, hb) in enumerate(groups):
                        if gi + 1 < len(groups) and not (prefetched and gi == 0):
                            load_w(gi + 1)
                        w = wb[gi % 2]
                        if kind in ("aq", "ak", "mq", "mk"):
                            for hc in range(4):
                                head = hb + hc
                                halo_only = (kind == "mq" and is_ctx)
                                sg = stg[cnt["stg"] % 2]
                                if not halo_only:
                                    cnt["stg"] += 1
                                hq = head + (8 if kind == "mk" else 0)
                                tgs = [3] if halo_only else [0, 1, 2, 3]
                                for tg in tgs:
                                    ps = psA[psi % 4]
                                    psi += 1
                                    mm_group(S, ps.t[:],
                                             [(w.t[:, k, hc * P:(hc + 1) * P], hT.t[:, k, tg * 512:(tg + 1) * 512])
                                              for k in range(KC)],
                                             reads=[w] + hT_b[tg * 4:(tg + 1) * 4] + hT_b2[tg * 4:(tg + 1) * 4], writes=[ps])
                                    r = cnt["r"] % 2
                                    cnt["r"] += 1
                                    dst = sg.t[:, tg * 512:(tg + 1) * 512]
                                    if kind in ("aq", "ak"):
                                        gcol = qgs if kind == "aq" else pp.t[:, PP_KG:PP_KG + 1]
                                        S.op("act", lambda e: e.activation(out=sq[r].t[:], in_=ps.t[:], func=AF.Square),
                                             reads=[ps], writes=[sq[r]])
                                        for fn_ in pending:
                                            fn_()
                                        pending.clear()

                                        def rest(ps=ps, r=r, dst=dst, gcol=gcol, sg=sg):
                                            S.op("pe", lambda e: e.matmul(psB[r].t[:], ones_b, sq[r].t[:], start=True, stop=True),
                                                 reads=[sq[r], cb], writes=[psB[r]])
                                            S.op("act", lambda e: e.activation(out=sv[r].t[:], in_=psB[r].t[:], func=AF.Ln,
                                                                               bias=eps_col, scale=1.0 / HD),
                                                 reads=[psB[r], pp], writes=[sv[r]])
                                            S.op("act", lambda e: e.activation(out=sv[r].t[:], in_=sv[r].t[:], func=AF.Exp,
                                                                               scale=-0.5),
                                                 reads=[sv[r]], writes=[sv[r]])
                                            S.op("dve", lambda e: e.scalar_tensor_tensor(dst, ps.t[:], gcol, sv[r].t[:],
                                                                                         ALU.mult, ALU.mult),
                                                 reads=[ps, sv[r], smallp, pp], writes=[sg])
                                        pending.append(rest)
                                    else:
                                        cbf_ = cbuf[r]
                                        if tg == tgs[0]:
                                            S.op("dve", lambda e: e.tensor_copy(cbf_.t[:, 0:3], halo.t[:, hq, :]),
                                                 reads=[halo], writes=[cbf_])
                                        S.op("act", lambda e: e.copy(out=cbf_.t[:, 3:515], in_=ps.t[:]),
                                             reads=[ps], writes=[cbf_])
                                        for fn_ in pending:
                                            fn_()
                                        pending.clear()
                                        if halo_only or (is_ctx and tg == 3):
                                            S.op("dve", lambda e: e.tensor_copy(halo.t[:, hq, :], cbf_.t[:, 512:515]),
                                                 reads=[cbf_], writes=[halo])
                                        if halo_only:
                                            continue
                                        if tg < 3:
                                            nxt = cbuf[(r + 1) % 2]
                                            S.op("dve", lambda e: e.tensor_copy(nxt.t[:, 0:3], cbf_.t[:, 512:515]),
                                                 reads=[cbf_], writes=[nxt])
                                        ac = cacc[r]
                                        S.op("dve", lambda e: e.tensor_scalar(ac.t[:], cbf_.t[:, 0:512],
                                                                              pp.t[:, PP_CONVW + hq * 4:PP_CONVW + hq * 4 + 1],
                                                                              None, ALU.mult),
                                             reads=[cbf_, pp], writes=[ac])
                                        for jj in (1, 2, 3):
                                            S.op("dve", lambda e, jj=jj: e.scalar_tensor_tensor(
                                                ac.t[:], cbf_.t[:, jj:jj + 512],
                                                pp.t[:, PP_CONVW + hq * 4 + jj:PP_CONVW + hq * 4 + jj + 1],
                                                ac.t[:], ALU.mult, ALU.add), reads=[cbf_, pp, ac], writes=[ac])
                                        def silu_(dst=dst, ac=ac, hq=hq, sg=sg):
                                            S.op("act", lambda e: e.activation(out=dst, in_=ac.t[:], func=AF.Silu,
                                                                               bias=pp.t[:, PP_CONVB + hq:PP_CONVB + hq + 1]),
                                                 reads=[ac, pp], writes=[sg])
                                        pending.append(silu_)
                                for fn_ in pending:
                                    fn_()
                                pending.clear()
                                if halo_only:
                                    continue
                                sdst, sB = {"aq": (s_aq, B_aq), "ak": (s_ak, B_ak), "mq": (s_mq, B_mq),
                                            "mk": (s_mk, B_mk)}[kind]
                                t0 = tok0 if kind in ("ak", "mk") else 0
                                S.dma("pool", sdst.ap()[head, :, t0:t0 + 2048], sg.t[:], reads=[sg], writes=[sB],
                                      sembuf=sg)
                        else:
                            for tt in range(16):
                                ps = psA[psi % 4]
                                psi += 1
                                mm_group(S, ps.t[:],
                                         [(hT.t[:, k, tt * P:(tt + 1) * P], w.t[:, k, :]) for k in range(KC)],
                                         reads=[w, hT_b[tt], hT_b2[tt]], writes=[ps])
                                vb = vst[cnt["vst"] % 4]
                                cnt["vst"] += 1
                                if kind == "mo":
                                    sf = sigf[cnt["vst"] % 2]
                                    S.op("act", lambda e: e.activation(out=sf.t[:], in_=ps.t[:], func=AF.Sigmoid),
                                         reads=[ps], writes=[sf])
                                    S.op("dve", lambda e: e.tensor_tensor(vb.t[:], sf.t[:],
                                                                          mgrow1.t[:, hb * 512:(hb + 1) * 512], ALU.mult),
                                         reads=[sf, mgrow1], writes=[vb])
                                elif tt % 2 == 0:
                                    S.op("act", lambda e: e.copy(out=vb.t[:], in_=ps.t[:]), reads=[ps], writes=[vb])
                                else:
                                    S.op("dve", lambda e: e.tensor_copy(vb.t[:], ps.t[:]), reads=[ps], writes=[vb])
                                sdst, sB = {"av": (s_av, B_av), "mv": (s_mv, B_mv), "mo": (s_mo, B_mo)}[kind]
                                t0 = 0 if kind == "mo" else tok0
                                S.dma("pool", sdst.ap()[t0 + tt * P:t0 + (tt + 1) * P, hb * 512:(hb + 1) * 512],
                                      vb.t[:], reads=[vb], writes=[sB], sembuf=vb)
                    for tt in range(16):
                        mm_group(S, psG.t[:, 0:16],
                                 [(hT.t[:, k, tt * P:(tt + 1) * P], wg16.t[:, k, :]) for k in range(KC)],
                                 reads=[wg16, hT_b[tt], hT_b2[tt]], writes=[psG])
                        ch = tok0 // P + tt
                        S.op("dve", lambda e: e.tensor_tensor(gates_all.t[:, ch, :], psG.t[:, 0:16],
                                                              smallp.t[:, 0:16], ALU.add),
                             reads=[psG, smallp], writes=[gates_all])
            S.wait_all("sp", [B_aq, B_ak, B_av, B_mq, B_mk, B_mv, B_mo])
            S.wait_all("pool", [B_aq, B_ak, B_av, B_mq, B_mk, B_mv, B_mo])
            if dbg:
                S.dma("sp", d_gates.ap(), bass.AP(gates_all.t, 0, [[512, P], [1, 512]]), reads=[gates_all],
                      writes=[B_dbg], sembuf=gates_all)

        if stop in ("inproj", "inproj_small"):
            S.wait_all("sp", [B_dbg])
            return nc, dbg_outs, S

        def flat(t, n):
            return bass.AP(t, 0, [[n, P], [1, n]])

        with stage() as st:
            ge = SB(st, "ge", [P, 32, 8], F32)
            lsp = SB(st, "lsp", [P, 32, 8], F32)
            gB = SB(st, "gB", [P, 32, 8], F32)
            gBL = SB(st, "gBL", [P, 32, 8], F32)
            gC = SB(st, "gC", [P, 32, 8], F32)
            gMAXC = SB(st, "gMAXC", [P, 32, 8], F32)
            gM = SB(st, "gM", [P, 32, 8], F32)
            gMP = SB(st, "gMP", [P, 33, 8], F32)
            gtmp = SB(st, "gtmp", [P, 32, 8], F32)
            gcol = SB(st, "gcol", [P, 2], F32)
            gdiag = [SB(st, f"gdiag{i}", [P, P], F32) for i in range(2)]
            p1 = PS(st, "gp1", [P, 512])
            p2 = PS(st, "gp2", [P, 512])
            p3 = PS(st, "gp3", [P, 512])
            p4 = PS(st, "gp4", [P, 512])
            S.op("act", lambda e: e.activation(out=ge.t[:], in_=gates_all.t[:, :, 8:16], func=AF.Exp, scale=-1.0),
                 reads=[gates_all], writes=[ge])
            S.op("act", lambda e: e.activation(out=lsp.t[:], in_=ge.t[:], func=AF.Ln, bias=1.0),
                 reads=[ge], writes=[lsp])
            S.op("pe", lambda e: e.matmul(p1.t[:, 0:256], trineg_f, flat(lsp.t, 256), start=True, stop=True),
                 reads=[lsp, pp], writes=[p1])
            S.op("pe", lambda e: e.matmul(p2.t[:, 0:256], negones_f, flat(lsp.t, 256), start=True, stop=True),
                 reads=[lsp, pp], writes=[p2])
            S.op("dve", lambda e: e.tensor_copy(flat(gB.t, 256), p1.t[:, 0:256]), reads=[p1], writes=[gB])
            S.op("dve", lambda e: e.tensor_copy(flat(gBL.t, 256), p2.t[:, 0:256]), reads=[p2], writes=[gBL])
            S.op("dve", lambda e: e.tensor_tensor(gC.t[:], gates_all.t[:, :, 0:8], gB.t[:], ALU.subtract),
                 reads=[gates_all, gB], writes=[gC])
            for hf in range(2):
                S.op("pe", lambda e: e.transpose(p3.t[:, hf * P:(hf + 1) * P], flat(gC.t, 256)[:, hf * P:(hf + 1) * P],
                                                 ident_f), reads=[gC, pp], writes=[p3])
                S.op("dve", lambda e: e.reduce_max(gcol.t[:, hf:hf + 1], p3.t[:, hf * P:(hf + 1) * P], axis=AX.X),
                     reads=[p3], writes=[gcol])
                S.op("dve", lambda e: e.tensor_scalar(gdiag[hf].t[:], ident_f, gcol.t[:, hf:hf + 1], None, ALU.mult),
                     reads=[gcol, pp], writes=[gdiag[hf]])
                S.op("pe", lambda e: e.matmul(p4.t[:, hf * P:(hf + 1) * P], ones_f, gdiag[hf].t[:], start=True, stop=True),
                     reads=[gdiag[hf], pp], writes=[p4])
            S.op("dve", lambda e: e.tensor_copy(flat(gMAXC.t, 256), p4.t[:, 0:256]), reads=[p4], writes=[gMAXC])
            S.op("dve", lambda e: e.memset(gMP.t[:, 0, :], MNEG), writes=[gMP])
            for c in range(32):
                S.op("dve", lambda e: e.tensor_tensor(gM.t[:, c, :], gMP.t[:, c, :], gMAXC.t[:, c, :], ALU.max),
                     reads=[gMP, gMAXC], writes=[gM])
                S.op("dve", lambda e: e.tensor_tensor(gMP.t[:, c + 1, :], gBL.t[:, c, :], gM.t[:, c, :], ALU.add),
                     reads=[gBL, gM], writes=[gMP])
                if c == 15:
                    S.op("dve", lambda e: e.tensor_scalar(gMP.t[:, 16, :], gMP.t[:, 16, :],
                                                          pp.t[:, PP_CBIAS:PP_CBIAS + 1], None, ALU.add),
                         reads=[gMP, pp], writes=[gMP])
            S.op("dve", lambda e: e.tensor_tensor(gtmp.t[:], gMP.t[:, 0:32, :], gM.t[:], ALU.subtract),
                 reads=[gMP, gM], writes=[gtmp])
            S.op("act", lambda e: e.activation(out=gpG.t[:], in_=gtmp.t[:], func=AF.Exp), reads=[gtmp], writes=[gpG])
            S.op("dve", lambda e: e.tensor_scalar(gpGS.t[:], gpG.t[:], QS, None, ALU.mult), reads=[gpG], writes=[gpGS])
            S.op("dve", lambda e: e.tensor_tensor(gtmp.t[:], gC.t[:], gM.t[:], ALU.subtract),
                 reads=[gC, gM, gpG], writes=[gtmp])
            S.op("act", lambda e: e.activation(out=gpU.t[:], in_=gtmp.t[:], func=AF.Exp), reads=[gtmp], writes=[gpU])
            S.op("dve", lambda e: e.tensor_scalar(gpUS.t[:], gpU.t[:], QS, None, ALU.mult), reads=[gpU], writes=[gpUS])
            S.op("dve", lambda e: e.tensor_tensor(gtmp.t[:], gB.t[:], gM.t[:], ALU.add),
                 reads=[gB, gM, gpU], writes=[gtmp])
            S.op("act", lambda e: e.activation(out=gpEL.t[:], in_=gtmp.t[:], func=AF.Exp, scale=-1.0),
                 reads=[gtmp], writes=[gpEL])
            S.op("dve", lambda e: e.tensor_scalar(gpEL.t[:], gpEL.t[:], 1.0 / QS, None, ALU.mult),
                 reads=[gpEL], writes=[gpEL])
            if dbg:
                for i, bsrc in enumerate([gpU, gpG, gpEL, gM, gB, gC]):
                    S.dma("sp", d_gp.ap()[:, i * 256:(i + 1) * 256], flat(bsrc.t, 256), reads=[bsrc],
                          writes=[B_dbg], sembuf=bsrc)
                S.wait_all("sp", [B_dbg])
        if stop == "gates":
            return nc, dbg_outs, S

        with stage() as mixst:
            mixT = SB(mixst, "mixT", [P, 16, TO], BF16)
            mix_b = [Buf(f"mixb{i}") for i in range(16)]
            with stage() as st:
                qT = [SB(st, f"aqT{i}", [P, TO], BF16) for i in range(2)]
                kT = [SB(st, f"akT{i}", [P, T], BF16) for i in range(2)]
                v1 = [SB(st, f"av1{i}", [P, 17, P], BF16) for i in range(2)]
                v2 = [SB(st, f"av2{i}", [P, 4, 5, P], BF16) for i in range(2)]
                v3 = [SB(st, f"av3{i}", [P, 16, 2, P], BF16) for i in range(2)]
                accs = [SB(st, f"aacc{i}", [P, 2, TO], F32) for i in range(2)]
                pT = [SB(st, f"apT{i}", [P, 256], BF16) for i in range(4)]
                sp_ = [PS(st, f"asp{i}", [P, 512]) for i in range(3)]
                po_ = [PS(st, f"apo{i}", [P, 512]) for i in range(3)]

                def load_head(h):
                    j = h % 2
                    S.dma("sp", qT[j].t[:], s_aq.ap()[h], reads=[B_aq], writes=[qT[j]], sembuf=qT[j])
                    S.dma("sp", kT[j].t[:], s_ak.ap()[h], reads=[B_ak], writes=[kT[j]], sembuf=kT[j])
                    S.dma("sp", v1[j].t[:], bass.AP(s_av, (TC - P) * 1024 + h * P, [[1024, P], [P * 1024, 17], [1, P]]),
                          reads=[B_av], writes=[v1[j]], sembuf=v1[j])
                    S.dma("sp", v2[j].t[:], bass.AP(s_av, (4 * P * 3) * 1024 + h * P,
                                                    [[4 * 1024, P], [1024, 4], [512 * 1024, 5], [1, P]]),
                          reads=[B_av], writes=[v2[j]], sembuf=v2[j])
                    for kb in range(2):
                        S.dma("sp", v3[j].t[:, :, kb, :], bass.AP(s_av, kb * 2048 * 1024 + h * P,
                                                                  [[16 * 1024, P], [1024, 16], [1, P]]),
                              reads=[B_av], writes=[v3[j]], sembuf=v3[j])

                units = []
                for h in range(NH):
                    for pi, dil in enumerate((1, 4, 16)):
                        for r in range(dil):
                            for qb in range(16 // dil):
                                units.append((h, pi, dil, r, qb))

                def phaseA(u):
                    h, pi, dil, r, qb = units[u]
                    j = h % 2
                    if (pi, r, qb) == (0, 0, 2) and h + 1 < NH:
                        load_head(h + 1)
                    q0 = r + dil * P * qb
                    qsl = slice(q0, q0 + dil * (P - 1) + 1, dil)
                    kc0 = TC + q0
                    kp0 = kc0 - dil * P
                    ksl_c = slice(kc0, kc0 + dil * (P - 1) + 1, dil)
                    ksl_p = slice(kp0, kp0 + dil * (P - 1) + 1, dil)
                    sp, pt = sp_[u % 3], pT[u % 4]
                    mb = mbC if qb == 0 else mbA
                    S.op("pe", lambda e: e.matmul(sp.t[:, 0:256], ident, mb, start=True, stop=False),
                         reads=[cb], writes=[sp], inc=False)
                    S.op("pe", lambda e: e.matmul(sp.t[:, 0:P], kT[j].t[:, ksl_p], qT[j].t[:, qsl],
                                                  start=False, stop=False),
                         reads=[kT[j], qT[j]], writes=[sp], inc=False)
                    S.op("pe", lambda e: e.matmul(sp.t[:, P:2 * P], kT[j].t[:, ksl_c], qT[j].t[:, qsl],
                                                  start=False, stop=True),
                         reads=[kT[j], qT[j]], writes=[sp])
                    S.op("act", lambda e: e.activation(out=pt.t[:], in_=sp.t[:, 0:256], func=AF.Exp,
                                                       bias=negshift), reads=[sp, smallp], writes=[pt])

                def phaseB(u):
                    h, pi, dil, r, qb = units[u]
                    j = h % 2
                    acc = accs[j]
                    q0 = r + dil * P * qb
                    if pi == 0:
                        vp, vc, vbuf = v1[j].t[:, qb, :], v1[j].t[:, qb + 1, :], v1[j]
                    elif pi == 1:
                        vp, vc, vbuf = v2[j].t[:, r, qb, :], v2[j].t[:, r, qb + 1, :], v2[j]
                    else:
                        vp, vc, vbuf = v3[j].t[:, r, 0, :], v3[j].t[:, r, 1, :], v3[j]
                    po, pt = po_[u % 3], pT[u % 4]
                    S.op("pe", lambda e: e.matmul(po.t[:, 0:P], vp, pt.t[:, 0:P], start=True, stop=False),
                         reads=[pt, vbuf], writes=[po], inc=False)
                    S.op("pe", lambda e: e.matmul(po.t[:, 0:P], vc, pt.t[:, P:2 * P], start=False, stop=True),
                         reads=[pt, vbuf], writes=[po], inc=False)
                    S.op("pe", lambda e: e.matmul(po.t[:, P:2 * P], ones_b, pt.t[:, 0:P], start=True, stop=False),
                         reads=[pt, cb], writes=[po], inc=False)
                    S.op("pe", lambda e: e.matmul(po.t[:, P:2 * P], ones_b, pt.t[:, P:2 * P], start=False, stop=True),
                         reads=[pt, cb], writes=[po])
                    accv = bass.AP(acc.t, q0, [[2 * TO, P], [TO, 2], [dil, P]])
                    pov = bass.AP(po.t, 0, [[512, P], [P, 2], [1, P]])
                    if pi == 0:
                        S.op("act", lambda e: e.copy(out=accv, in_=pov), reads=[po], writes=[acc])
                    else:
                        S.op("dve", lambda e: e.tensor_tensor(accv, accv, pov, ALU.add),
                             reads=[po, acc], writes=[acc])
                    if (pi, r, qb) == (2, 15, 0):
                        S.op("act", lambda e: e.activation(out=acc.t[:, 1, :], in_=acc.t[:, 1, :], func=AF.Ln),
                             reads=[acc], writes=[acc])
                        S.op("act", lambda e: e.activation(out=acc.t[:, 1, :], in_=acc.t[:, 1, :], func=AF.Exp, scale=-1.0),
                             reads=[acc], writes=[acc])
                        S.op("dve", lambda e: e.tensor_tensor(mixT.t[:, h, :], acc.t[:, 0, :], acc.t[:, 1, :], ALU.mult),
                             reads=[acc], writes=mix_b)

                load_head(0)
                SK = 2
                for idx in range(len(units) + SK):
                    if idx < len(units):
                        phaseA(idx)
                    if idx >= SK:
                        phaseB(idx - SK)
            if stop == "attn":
                if dbg:
                    S.dma("sp", d_mixT.ap(), bass.AP(mixT.t, 0, [[16 * TO, P], [1, 16 * TO]]), reads=mix_b,
                          writes=[B_dbg], sembuf=mixT)
                    S.wait_all("sp", [B_dbg])
                return nc, dbg_outs, S

            with stage() as st:
                kk = [SB(st, f"mkk{i}", [P, NH, 512], BF16) for i in range(2)]
                qq = [SB(st, f"mqq{i}", [P, NH, 512], BF16) for i in range(2)]
                vv = [SB(st, f"mvv{i}", [P, 4, NH, 132], BF16) for i in range(2)]
                mo = [SB(st, f"mmo{i}", [P, 4, 1024], BF16) for i in range(2)]
                Cst = [SB(st, f"mCst{g}", [P, 4, 132], F32) for g in range(2)]
                CGf = [SB(st, f"mCG{i}", [P, 4, 132], F32) for i in range(2)]
                cbf = [SB(st, f"mcbf{i}", [P, 4, 132], BF16) for i in range(2)]
                vu = [SB(st, f"mvu{i}", [P, 4, 132], BF16) for i in range(4)]
                wt = [SB(st, f"mwt{i}", [P, 4, P], BF16) for i in range(2)]
                ktok = [SB(st, f"mktok{i}", [P, 4, P], BF16) for i in range(2)]
                hh = [SB(st, f"mhh{i}", [P, 4, P], F32) for i in range(4)]
                sqj = [SB(st, f"msqj{i}", [P, 4, P], F32) for i in range(4)]
                ot = [SB(st, f"mot{i}", [P, 4, P], BF16) for i in range(4)]
                sm = [SB(st, f"msm{i}", [P, 32], F32) for i in range(4)]
                FA = [PS(st, f"mFA{i}", [P, 4, P]) for i in range(2)]
                FC = [PS(st, f"mFC{i}", [P, 4, P]) for i in range(2)]
                FD = [PS(st, f"mFD{i}", [P, 512]) for i in range(2)]
                FB = [PS(st, f"mFB{i}", [P, 8, P], BF16) for i in range(2)]
                for g in range(2):
                    S.op("dve", lambda e, g=g: e.memset(Cst[g].t[:], 0.0), writes=[Cst[g]])
                for i in range(2):
                    S.op("dve", lambda e, i=i: e.memset(vv[i].t[:, :, :, 128:129], 1.0), writes=[vv[i]])

                def load_grp(g):
                    j = g % 2
                    S.dma("sp", kk[j].t[:], bass.AP(s_mk, g * 512, [[T, P], [P * T, NH], [1, 512]]),
                          reads=[B_mk], writes=[kk[j]], sembuf=kk[j])
                    for ci_ in range(4):
                        S.dma("sp", vv[j].t[:, ci_, :, 0:P],
                              bass.AP(s_mv, (g * 512 + ci_ * P) * 1024, [[1024, P], [P, NH], [1, P]]),
                              reads=[B_mv], writes=[vv[j]], sembuf=vv[j])
                    if g >= 4:
                        S.dma("sp", qq[j].t[:], bass.AP(s_mq, (g - 4) * 512, [[TO, P], [P * TO, NH], [1, 512]]),
                              reads=[B_mq], writes=[qq[j]], sembuf=qq[j])

                def load_mo(g):
                    j = g % 2
                    S.dma("sp", mo[j].t[:], bass.AP(s_mo, (g - 4) * 512 * 1024, [[1024, P], [P * 1024, 4], [1, 1024]]),
                          reads=[B_mo], writes=[mo[j]], sembuf=mo[j])

                munits = [(c, hg) for c in range(32) for hg in range(2)]
                mask_bc = bass.AP(cb.t, CB_MASKA + P, [[CB_N, P], [0, 4], [1, P]])

                def bc4(t, off, n):
                    return bass.AP(t, off, [[256, P], [1, 4], [0, n]])

                def phaseV(u):
                    c, hg = munits[u]
                    g, ci = c // 4, c % 4
                    j = g % 2
                    hs = slice(hg * 4, hg * 4 + 4)
                    if ci == 1 and hg == 1 and g + 1 < 8:
                        load_grp(g + 1)
                    if ci == 3 and hg == 0 and 4 <= g + 1 < 8:
                        load_mo(g + 1)
                    S.op("pool", lambda e: e.tensor_tensor(vu[u % 4].t[:, :, 0:129], vv[j].t[:, ci, hs, 0:129],
                                                           bc4(gpU.t, c * 8 + hg * 4, 129), ALU.mult),
                         reads=[vv[j], gpU], writes=[vu[u % 4]])

                def phaseX(u):
                    c, hg = munits[u]
                    g, ci = c // 4, c % 4
                    j = g % 2
                    s = u % 2
                    vu_ = vu[u % 4]
                    own = c >= 16
                    csl = slice(ci * P, (ci + 1) * P)
                    hs = slice(hg * 4, hg * 4 + 4)
                    if own:
                        for h4 in range(4):
                            S.op("pe", lambda e, h4=h4: e.matmul(FA[s].t[:, h4, :], kk[j].t[:, hg * 4 + h4, csl],
                                                                 qq[j].t[:, hg * 4 + h4, csl], start=True, stop=True),
                                 reads=[kk[j], qq[j]], writes=[FA[s]], inc=(h4 == 3))
                        S.op("dve", lambda e: e.tensor_tensor(wt[s].t[:], FA[s].t[:], mask_bc, ALU.mult),
                             reads=[FA[s], cb], writes=[wt[s]])
                    if c < 31:
                        for h4 in range(4):
                            S.op("pe", lambda e, h4=h4: e.transpose(FB[s].t[:, h4, :], kk[j].t[:, hg * 4 + h4, csl], ident),
                                 reads=[kk[j], cb], writes=[FB[s]], inc=(h4 == 3))
                        S.op("act", lambda e: e.copy(out=ktok[s].t[:], in_=FB[s].t[:, 0:4, :]),
                             reads=[FB[s]], writes=[ktok[s]])
                        for h4 in range(4):
                            S.op("pe", lambda e, h4=h4: e.matmul(FC[s].t[:, h4, :], ktok[s].t[:, h4, :],
                                                                 vu_.t[:, h4, 0:P], start=True, stop=True),
                                 reads=[ktok[s], vu_], writes=[FC[s]], inc=False)
                            S.op("pe", lambda e, h4=h4: e.matmul(FD[s].t[:, 8 + h4:9 + h4], ktok[s].t[:, h4, :],
                                                                 vu_.t[:, h4, P:P + 1], start=True, stop=True),
                                 reads=[ktok[s], vu_], writes=[FD[s]], inc=(h4 == 3))
                    S.op("dve", lambda e: e.tensor_tensor(CGf[s].t[:, :, 0:129], Cst[hg].t[:, :, 0:129],
                                                          bc4(gpG.t, c * 8 + hg * 4, 129), ALU.mult),
                         reads=[Cst[hg], gpG], writes=[CGf[s]])
                    if own:
                        S.op("act", lambda e: e.copy(out=cbf[s].t[:, :, 0:129], in_=CGf[s].t[:, :, 0:129]),
                             reads=[CGf[s]], writes=[cbf[s]])
                    if c < 31:
                        S.op("dve", lambda e: e.tensor_tensor(Cst[hg].t[:, :, 0:P], CGf[s].t[:, :, 0:P], FC[s].t[:],
                                                              ALU.add),
                             reads=[CGf[s], FC[s]], writes=[Cst[hg]])
                        S.op("dve", lambda e: e.tensor_tensor(Cst[hg].t[:, :, P:P + 1], CGf[s].t[:, :, P:P + 1],
                                                              bass.AP(FD[s].t, 8, [[512, P], [1, 4], [1, 1]]), ALU.add),
                             reads=[CGf[s], FD[s]], writes=[Cst[hg]])
                    if own:
                        for h4 in range(4):
                            h = hg * 4 + h4
                            S.op("pe", lambda e, h4=h4: e.matmul(FA[s].t[:, h4, :], wt[s].t[:, h4, :], vu_.t[:, h4, 0:P],
                                                                 start=True, stop=False),
                                 reads=[wt[s], vu_], writes=[FA[s]], inc=False)
                            S.op("pe", lambda e, h4=h4, h=h: e.matmul(FA[s].t[:, h4, :], qq[j].t[:, h, csl],
                                                                      cbf[s].t[:, h4, 0:P], start=False, stop=True),
                                 reads=[qq[j], cbf[s]], writes=[FA[s]], inc=False)
                            S.op("pe", lambda e, h4=h4: e.matmul(FD[s].t[:, h4:h4 + 1], wt[s].t[:, h4, :],
                                                                 vu_.t[:, h4, P:P + 1], start=True, stop=False),
                                 reads=[wt[s], vu_], writes=[FD[s]], inc=False)
                            S.op("pe", lambda e, h4=h4, h=h: e.matmul(FD[s].t[:, h4:h4 + 1], qq[j].t[:, h, csl],
                                                                      cbf[s].t[:, h4, P:P + 1], start=False, stop=True),
                                 reads=[qq[j], cbf[s]], writes=[FD[s]], inc=(h4 == 3))

                def phaseYa(u):
                    c, hg = munits[u]
                    if c < 16:
                        return
                    s = u % 2
                    r4 = u % 4
                    m = sm[r4]
                    S.op("dve", lambda e: e.tensor_copy(m.t[:, 0:4], FD[s].t[:, 0:4]), reads=[FD[s]], writes=[m])
                    S.op("dve", lambda e: e.scalar_tensor_tensor(m.t[:, 4:8], m.t[:, 0:4], -1.0, m.t[:, 0:4],
                                                                 ALU.mult, ALU.max), reads=[m], writes=[m])
                    S.op("dve", lambda e: e.tensor_tensor(m.t[:, 4:8], m.t[:, 4:8], gpEL.t[:, c, hg * 4:hg * 4 + 4],
                                                          ALU.max), reads=[m, gpEL], writes=[m])
                    S.op("dve", lambda e: e.reciprocal(m.t[:, 8:12], m.t[:, 4:8]), reads=[m], writes=[m])
                    S.op("dve", lambda e: e.tensor_tensor(hh[r4].t[:], FA[s].t[:],
                                                          bass.AP(m.t, 8, [[32, P], [1, 4], [0, P]]), ALU.mult),
                         reads=[FA[s], m], writes=[hh[r4]])
                    S.op("act", lambda e: e.activation(out=sqj[r4].t[:], in_=hh[r4].t[:], func=AF.Square),
                         reads=[hh[r4]], writes=[sqj[r4]])

                def phaseYb(u):
                    c, hg = munits[u]
                    if c < 16:
                        return
                    r4 = u % 4
                    m = sm[r4]
                    S.op("dve", lambda e: e.reduce_sum(m.t[:, 12:16], sqj[r4].t[:], axis=AX.X),
                         reads=[sqj[r4], m], writes=[m])
                    S.op("dve", lambda e: e.tensor_scalar(m.t[:, 16:20], m.t[:, 12:16], 1.0 / HD, EPS, ALU.mult, ALU.add),
                         reads=[m], writes=[m])
                    S.op("act", lambda e: e.activation(out=m.t[:, 20:24], in_=m.t[:, 16:20], func=AF.Sqrt),
                         reads=[m], writes=[m])

                def phaseYc(u):
                    c, hg = munits[u]
                    if c < 16:
                        return
                    g, ci = c // 4, c % 4
                    j = g % 2
                    s = u % 2
                    r4 = u % 4
                    m = sm[r4]
                    S.op("dve", lambda e: e.reciprocal(m.t[:, 24:28], m.t[:, 20:24]), reads=[m], writes=[m])
                    S.op("dve", lambda e: e.tensor_tensor(hh[r4].t[:], hh[r4].t[:],
                                                          bass.AP(m.t, 24, [[32, P], [1, 4], [0, P]]), ALU.mult),
                         reads=[hh[r4], m], writes=[hh[r4]])
                    S.op("dve", lambda e: e.tensor_tensor(ot[r4].t[:], hh[r4].t[:],
                                                          bass.AP(mo[j].t, ci * 1024 + hg * 512,
                                                                  [[4096, P], [P, 4], [1, P]]), ALU.mult),
                         reads=[hh[r4], mo[j]], writes=[ot[r4]])
                    for h4 in range(4):
                        S.op("pe", lambda e, h4=h4: e.transpose(FB[s].t[:, 4 + h4, :], ot[r4].t[:, h4, :], ident),
                             reads=[ot[r4], cb], writes=[FB[s]], inc=(h4 == 3))
                    tt = c - 16
                    S.op("act", lambda e: e.copy(out=mixT.t[:, 8 + hg * 4:12 + hg * 4, tt * P:(tt + 1) * P],
                                                 in_=FB[s].t[:, 4:8, :]), reads=[FB[s]], writes=[mix_b[tt]])

                load_grp(0)
                NU = len(munits)
                for idx in range(NU + 5):
                    if idx < NU:
                        phaseV(idx)
                    if 2 <= idx < NU + 2:
                        phaseX(idx - 2)
                    if 3 <= idx < NU + 3:
                        phaseYa(idx - 3)
                    if 4 <= idx < NU + 4:
                        phaseYb(idx - 4)
                    if idx >= 5:
                        phaseYc(idx - 5)
            if dbg:
                S.dma("sp", d_mixT.ap(), bass.AP(mixT.t, 0, [[16 * TO, P], [1, 16 * TO]]), reads=mix_b,
                      writes=[B_dbg], sembuf=mixT)
                S.wait_all("sp", [B_dbg])
            if stop == "mix":
                return nc, dbg_outs, S

            with stage() as st:
                wo = SB(st, "wo", [P, KC, D], BF16)
                wo_b = [Buf(f"wob{i}") for i in range(4)]
                xr = [SB(st, f"xr{i}", [P, D], F32) for i in range(3)]
                xr_st = [Buf(f"xrst{i}") for i in range(3)]
                grow2 = SB(st, "grow2", [P, D], F32)
                xn2 = [SB(st, f"xn2{i}", [P, D], BF16) for i in range(2)]
                junk2 = SB(st, "junk2", [P, D], BF16)
                stt2 = [SB(st, f"stt2{i}", [P, 8], F32) for i in range(3)]
                stg2 = [SB(st, f"stg2{i}", [P, KC, P], BF16) for i in range(2)]
                pso = [PS(st, f"pso{i}", [P, 512]) for i in range(4)]
                tp2 = [PS(st, f"tp2{i}", [P, KC, P], BF16) for i in range(2)]
                S.dma("sp", grow2.t[:], rp_d.ap()[:, RP_G2:RP_G2 + D], writes=[grow2], sembuf=grow2)
                for cg in range(4):
                    S.dma("pool", wo.t[:, :, cg * 512:(cg + 1) * 512], w_src(w_out, D, cg * 512, 512),
                          writes=[wo_b[cg]], sembuf=wo_b[cg])

                def o_load(tt):
                    S.dma("sp", xr[tt % 3].t[:], x_loc.ap()[TC + tt * P:TC + (tt + 1) * P, :], writes=[xr[tt % 3]],
                          sembuf=xr[tt % 3])

                def o_mm(tt):
                    x_ = xr[tt % 3]
                    for cg in range(4):
                        ps = pso[(tt * 4 + cg) % 4]
                        mm_group(S, ps.t[:], [(mixT.t[:, k, tt * P:(tt + 1) * P], wo.t[:, k, cg * 512:(cg + 1) * 512])
                                              for k in range(KC)], reads=[wo_b[cg], mix_b[tt]], writes=[ps])
                        S.op("dve", lambda e, cg=cg, ps=ps: e.tensor_tensor(x_.t[:, cg * 512:(cg + 1) * 512], ps.t[:],
                                                                            x_.t[:, cg * 512:(cg + 1) * 512], ALU.add),
                             reads=[ps, x_], writes=[x_])

                def o_n1(tt):
                    x_ = xr[tt % 3]
                    m = stt2[tt % 3]
                    S.dma("pool", s_x1.ap()[tt * P:(tt + 1) * P, :], x_.t[:], reads=[x_], writes=[B_x1],
                          sembuf=xr_st[tt % 3])
                    S.op("act", lambda e: e.activation(out=junk2.t[:], in_=x_.t[:], func=AF.Square,
                                                       accum_out=m.t[:, 0:1]), reads=[x_], writes=[junk2, m])
                    S.op("dve", lambda e: e.tensor_scalar(m.t[:, 1:2], m.t[:, 0:1], 1.0 / D, EPS, ALU.mult, ALU.add),
                         reads=[m], writes=[m])
                    S.op("act", lambda e: e.activation(out=m.t[:, 2:3], in_=m.t[:, 1:2], func=AF.Sqrt),
                         reads=[m], writes=[m])
                    S.op("dve", lambda e: e.reciprocal(m.t[:, 3:4], m.t[:, 2:3]), reads=[m], writes=[m])
                    S.op("dve", lambda e: e.scalar_tensor_tensor(xn2[tt % 2].t[:], x_.t[:], m.t[:, 3:4], grow2.t[:],
                                                                 ALU.mult, ALU.mult),
                         reads=[x_, m, grow2], writes=[xn2[tt % 2]])

                def o_n2(tt):
                    xn_, tp_, sg_ = xn2[tt % 2], tp2[tt % 2], stg2[tt % 2]
                    for k in range(KC):
                        S.op("pe", lambda e, k=k: e.transpose(tp_.t[:, k, :], xn_.t[:, k * P:(k + 1) * P], ident),
                             reads=[xn_, cb], writes=[tp_], inc=(k == KC - 1))
                    S.op("act", lambda e: e.copy(out=sg_.t[:], in_=tp_.t[:]), reads=[tp_], writes=[sg_])
                    S.dma("pool", bass.AP(s_h2, tt * P * KC * P, [[KC * P, P], [1, KC * P]]),
                          bass.AP(sg_.t, 0, [[KC * P, P], [1, KC * P]]), reads=[sg_], writes=[B_h2], sembuf=sg_)

                o_load(0)
                o_load(1)
                for idx in range(17):
                    if idx + 2 < 16:
                        o_load(idx + 2)
                    if idx < 16:
                        o_mm(idx)
                        o_n1(idx)
                    if idx >= 1:
                        o_n2(idx - 1)
                S.wait_all("sp", [B_x1, B_h2])
                S.wait_all("pool", [B_x1, B_h2])
        if stop == "x1":
            return nc, dbg_outs, S

        with stage() as st:
            h2T = SB(st, "h2T", [P, KC, 1024], BF16)
            h2_b = [Buf(f"h2b{i}") for i in range(8)]
            h2_b2 = [Buf(f"h2c{i}") for i in range(8)]
            actT = SB(st, "actT", [P, FKC, 1024], BF16)
            wd0 = SB(st, "wd0", [P, FKC, 256], BF16)
            wgu0 = SB(st, "wgu0", [P, 2, KC, 256], BF16)

            def load_gu0():
                S.dma("pool", wgu0.t[:, 0], w_src(w_gate, FH, 0, 256), writes=[wgu0], sembuf=wgu0)
                S.dma("pool", wgu0.t[:, 1], w_src(w_up, FH, 0, 256), writes=[wgu0], sembuf=wgu0)
            act_b = [Buf(f"actb{i}") for i in range(8)]

            def load_h2(half_):
                for t_ in range(8):
                    S.dma("sp", h2T.t[:, :, t_ * P:(t_ + 1) * P],
                          bass.AP(s_h2, (half_ * 8 + t_) * P * KC * P, [[KC * P, P], [P, KC], [1, P]]),
                          reads=[B_h2], writes=[h2_b[t_], h2_b2[t_]], sembuf=h2_b[t_])
            for half in range(2):
                t0 = half * 1024
                if half == 0:
                    load_h2(0)
                with stage() as st2:
                    wgu = [wgu0, SB(st2, "wgu1", [P, 2, KC, 256], BF16)]
                    sgl = [SB(st2, f"sgl{i}", [P, 512], F32) for i in range(2)]
                    pg = [PS(st2, f"pg{i}", [P, 512]) for i in range(3)]
                    pu = [PS(st2, f"pu{i}", [P, 512]) for i in range(3)]

                    def load_gu(gi):
                        j = gi % 2
                        S.dma("pool", wgu[j].t[:, 0], w_src(w_gate, FH, gi * 256, 256), writes=[wgu[j]], sembuf=wgu[j])
                        S.dma("pool", wgu[j].t[:, 1], w_src(w_up, FH, gi * 256, 256), writes=[wgu[j]], sembuf=wgu[j])

                    if half == 0:
                        load_gu0()
                    ui = 0
                    for gi in range(22):
                        if gi + 1 < 22:
                            load_gu(gi + 1)
                        if gi == 17:
                            S.dma("pool", wd0.t[:], w_src(w_down, D, 0, 256, nk=FKC), writes=[wd0], sembuf=wd0)
                        w = wgu[gi % 2]
                        for hc in range(2):
                            fc = gi * 2 + hc
                            for tg in range(2):
                                r = ui % 3
                                ui += 1
                                rds = [w] + h2_b[tg * 4:(tg + 1) * 4] + h2_b2[tg * 4:(tg + 1) * 4]
                                mm_group(S, pg[r].t[:], [(w.t[:, 0, k, hc * P:(hc + 1) * P], h2T.t[:, k, tg * 512:(tg + 1) * 512])
                                                         for k in range(KC)], reads=rds, writes=[pg[r]])
                                mm_group(S, pu[r].t[:], [(w.t[:, 1, k, hc * P:(hc + 1) * P], h2T.t[:, k, tg * 512:(tg + 1) * 512])
                                                         for k in range(KC)], reads=rds, writes=[pu[r]])
                                sg_ = sgl[ui % 2]
                                S.op("act", lambda e: e.activation(out=sg_.t[:], in_=pg[r].t[:], func=AF.Silu),
                                     reads=[pg[r]], writes=[sg_])
                                S.op("dve", lambda e: e.tensor_tensor(actT.t[:, fc, tg * 512:(tg + 1) * 512], sg_.t[:],
                                                                      pu[r].t[:], ALU.mult),
                                     reads=[sg_, pu[r]], writes=act_b[tg * 4:(tg + 1) * 4])
                with stage() as st3:
                    wd = [wd0, SB(st3, "wd1", [P, FKC, 256], BF16)]
                    x1q = [SB(st3, f"x1q{i}", [P, 256], F32) for i in range(3)]
                    oq = [SB(st3, f"oq{i}", [P, 256], F32) for i in range(3)]
                    pd = [PS(st3, f"pd{i}", [P, 512]) for i in range(4)]

                    def load_wd(cg):
                        S.dma("pool", wd[cg % 2].t[:], w_src(w_down, D, cg * 256, 256, nk=FKC), writes=[wd[cg % 2]],
                              sembuf=wd[cg % 2])

                    units = [(cg, tt) for cg in range(8) for tt in range(8)]

                    def load_x1(u):
                        cg, tt = units[u]
                        S.dma("sp", x1q[u % 3].t[:], s_x1.ap()[t0 + tt * P:t0 + (tt + 1) * P, cg * 256:(cg + 1) * 256],
                              reads=[B_x1], writes=[x1q[u % 3]], sembuf=x1q[u % 3])

                    load_x1(0)
                    load_x1(1)
                    for u, (cg, tt) in enumerate(units):
                        if half == 0 and u == 4:
                            load_h2(1)
                        if half == 0 and cg == 6 and tt == 0:
                            load_gu0()
                        if tt == 0 and cg + 1 < 8:
                            load_wd(cg + 1)
                        if u + 2 < len(units):
                            load_x1(u + 2)
                        ps = pd[u % 4]
                        w = wd[cg % 2]
                        mm_group(S, ps.t[:, 0:256], [(actT.t[:, k, tt * P:(tt + 1) * P], w.t[:, k, :]) for k in range(FKC)],
                                 reads=[w, act_b[tt]], writes=[ps])
                        o_ = oq[u % 3]
                        S.op("dve", lambda e: e.tensor_tensor(o_.t[:], ps.t[:, 0:256], x1q[u % 3].t[:], ALU.add),
                             reads=[ps, x1q[u % 3]], writes=[o_])
                        S.dma("pool", out_d.ap()[t0 + tt * P:t0 + (tt + 1) * P, cg * 256:(cg + 1) * 256], o_.t[:],
                              reads=[o_], writes=[B_out], sembuf=o_)
            S.wait_all("sp", [B_out])
            S.wait_all("pool", [B_out])
    return nc, dbg_outs, S


def host_consts(s):
    pj = np.arange(P)
    cbm = np.zeros((P, CB_N), np.float32)
    cbm[:, CB_IDENT:CB_IDENT + P] = np.eye(P)
    cbm[:, CB_ONES:CB_ONES + P] = 1.0
    prev = (pj[:, None] >= pj[None, :]).astype(np.float32)
    cur = (pj[:, None] <= pj[None, :]).astype(np.float32)
    cbm[:, CB_MASKA:CB_MASKA + P] = prev
    cbm[:, CB_MASKA + P:CB_MASKA + 2 * P] = cur
    cbm[:, CB_MASKC:CB_MASKC + P] = prev * (1.0 if s == 1 else 0.0)
    cbm[:, CB_MASKC + P:CB_MASKC + 2 * P] = cur
    cbm[:, CB_MBA:CB_MBA + 2 * P] = (1.0 - cbm[:, CB_MASKA:CB_MASKA + 2 * P]) * MNEG
    cbm[:, CB_MBC:CB_MBC + 2 * P] = (1.0 - cbm[:, CB_MASKC:CB_MASKC + 2 * P]) * MNEG
    return cbm.astype(ml_dtypes.bfloat16)


def make_in_maps(x, norm_mix_g, w_in, conv_w, conv_b, gate_b, q_norm_g, k_norm_g, mlstm_norm_g,
                 w_out, norm_ffn_g, w_gate, w_up, w_down):
    f = lambda a: np.ascontiguousarray(np.asarray(a, dtype=np.float32))
    x = f(x)
    w_in_, w_out_, w_gate_, w_up_, w_down_ = f(w_in[0]), f(w_out[0]), f(w_gate[0]), f(w_up[0]), f(w_down[0])
    pp = np.zeros((P, PP_N), np.float32)
    cw = f(conv_w[0])
    pp[:, PP_CONVW:PP_CONVW + 64] = cw.reshape(4, 16, P).transpose(2, 1, 0).reshape(P, 64)
    pp[:, PP_CONVB:PP_CONVB + 16] = f(conv_b[0]).reshape(16, P).T
    pp[:, PP_QG] = f(q_norm_g[0])
    pp[:, PP_KG] = f(k_norm_g[0])
    pp[:, PP_EPS] = EPS
    pj = np.arange(P)
    pp[:, PP_IDENT:PP_IDENT + P] = np.eye(P)
    pp[:, PP_TRINEG:PP_TRINEG + P] = -(pj[:, None] <= pj[None, :]).astype(np.float32)
    pp[:, PP_NEGONES:PP_NEGONES + P] = -1.0
    pp[:, PP_ONES:PP_ONES + P] = 1.0
    rp = np.zeros((P, RP_N), np.float32)
    rp[:, RP_G1:RP_G1 + D] = f(norm_mix_g[0])[None]
    rp[:, RP_G2:RP_G2 + D] = f(norm_ffn_g[0])[None]
    rp[:, RP_MG:RP_MG + 1024] = f(mlstm_norm_g[0]).reshape(1, 1024)
    rp[:, RP_GB:RP_GB + 16] = f(gate_b[0])[None]
    rp[:, RP_QG:RP_QG + P] = f(q_norm_g[0])[None]
    rp[:, RP_KG:RP_KG + P] = f(k_norm_g[0])[None]
    cbs = [host_consts(0), host_consts(1)]
    in_maps = []
    for core in range(8):
        b, s = core // 2, core % 2
        if s == 1:
            xl = x[b]
        else:
            xl = np.concatenate([np.zeros((TC, D), np.float32), x[b, :TO]], axis=0)
        ppc = pp.copy()
        ppc[:, PP_CBIAS] = 0.0 if s == 1 else MNEG
        in_maps.append({"x_loc": np.ascontiguousarray(xl), "w_in": w_in_, "w_out": w_out_, "w_gate": w_gate_,
                        "w_up": w_up_, "w_down": w_down_, "pp": ppc, "rp": rp, "cb": cbs[s]})
    return in_maps


_NC_CACHE = {}


def kernel(**inputs):
    in_maps = make_in_maps(**inputs)
    if "nc" not in _NC_CACHE:
        _NC_CACHE["nc"] = build()[0]
    nc = _NC_CACHE["nc"]
    res = run_bass_kernel_spmd(nc, in_maps, core_ids=list(range(8)))
    out = np.empty((4, 4096, D), np.float32)
    for core in range(8):
        b, s = core // 2, core % 2
        out[b, s * TO:(s + 1) * TO] = np.asarray(res.results[core]["out"], dtype=np.float32)
    return out
```

```python
import numpy as np
import ml_dtypes
import concourse.bass as bass
import concourse.mybir as mybir
from concourse.bass_utils import run_bass_kernel_spmd

F32 = mybir.dt.float32
BF16 = mybir.dt.bfloat16
AF = mybir.ActivationFunctionType
ALU = mybir.AluOpType
AX = mybir.AxisListType

P = 128
D = 2048
KC = 16
TO = 2048
TC = 2048
T = TO + TC
NH = 8
HD = 128
INW = 7184
FH = 5632
FKC = 44
EPS = 1e-6
MNEG = -30000.0
QS = HD ** -0.5

PP_CONVW = 0
PP_CONVB = 64
PP_QG = 80
PP_KG = 81
PP_CBIAS = 82
PP_EPS = 83
PP_IDENT = 84
PP_TRINEG = 212
PP_NEGONES = 340
PP_ONES = 468
PP_N = 596
RP_G1 = 0
RP_G2 = 2048
RP_MG = 4096
RP_GB = 5120
RP_QG = 5136
RP_KG = 5264
RP_N = 5392
CB_IDENT = 0
CB_ONES = 128
CB_MASKA = 256
CB_MASKC = 512
CB_MBA = 768
CB_MBC = 1024
CB_N = 1280


class Buf:
    def __init__(self, name, t=None, multi=False):
        self.name = name
        self.t = t
        self.w = []
        self.rs = []
        self.multi = multi
        self.dsem = None
        self.dcount = 0


class Eng:
    def __init__(self, name, h, sem):
        self.name = name
        self.h = h
        self.sem = sem
        self.n = 0
        self.seen = {}


class Sync:
    def __init__(self, nc):
        self.nc = nc
        self.eng = {}
        for name, h in (("pe", nc.tensor), ("act", nc.scalar), ("dve", nc.vector),
                        ("pool", nc.gpsimd), ("sp", nc.sync)):
            self.eng[name] = Eng(name, h, nc.alloc_semaphore("prog_" + name))
        self.nsem = 5
        self.ninst = 0
        self.dma_sems = {}

    def _wait(self, E, ev):
        sem, val, who = ev
        k = id(sem)
        if E.seen.get(k, 0) >= val:
            return
        E.h.wait_ge(sem, val)
        E.seen[k] = val
        self.ninst += 1

    def _deps(self, en, reads, writes):
        deps = []
        for b in reads:
            deps += b.w
        for b in writes:
            deps += b.rs
            if not b.multi:
                deps += b.w
        return deps

    def op(self, en, fn, reads=(), writes=(), inc=True):
        E = self.eng[en]
        for b in reads:
            for ev in b.w:
                if ev[2] == en and en == "pe":
                    continue
                self._wait(E, ev)
        for b in writes:
            for ev in b.rs:
                if ev[2] == en:
                    continue
                self._wait(E, ev)
            for ev in b.w:
                if ev[2] == en:
                    continue
                self._wait(E, ev)
        ins = fn(E.h)
        self.ninst += 1
        if inc:
            E.n += 1
            ins.then_inc(E.sem, 1)
            ev = (E.sem, E.n, en)
        else:
            ev = (E.sem, E.n + 1, en)
        for b in reads:
            b.rs = [r for r in b.rs if r[2] != en] + [ev]
        for b in writes:
            b.w = [ev]
            b.rs = []
        return ins

    def dma(self, qn, out, in_, reads=(), writes=(), sembuf=None):
        E = self.eng[qn]
        for ev in self._deps(qn, reads, writes):
            self._wait(E, ev)
        if sembuf.dsem is None:
            sembuf.dsem = self.nc.alloc_semaphore("d_" + sembuf.name)
            sembuf.dq = qn
            self.nsem += 1
        assert sembuf.dq == qn, f"DMA semaphore of {sembuf.name} shared between queues {sembuf.dq} and {qn}"
        sembuf.dcount += 16
        E.h.dma_start(out=out, in_=in_).then_inc(sembuf.dsem, 16)
        self.ninst += 1
        ev = (sembuf.dsem, sembuf.dcount, "dma:" + sembuf.name)
        self.dma_sems[id(sembuf.dsem)] = (sembuf.dsem, sembuf.dcount)
        for b in reads:
            b.rs = b.rs + [ev]
        for b in writes:
            if b.multi:
                b.w = [w for w in b.w if w[0] is not sembuf.dsem] + [ev]
            else:
                b.w = [ev]
                b.rs = []

    def barrier(self):
        engs = list(self.eng.values())
        for E in engs:
            for O in engs:
                if O.n > 0:
                    self._wait(E, (O.sem, O.n, O.name))
            for sem, cnt in self.dma_sems.values():
                self._wait(E, (sem, cnt, "dma"))

    def wait_all(self, qn, bufs):
        E = self.eng[qn]
        for b in bufs:
            for ev in b.w + b.rs:
                self._wait(E, ev)


def mm_group(S, out, pairs, reads, writes):
    n = len(pairs)
    for i, (l, r) in enumerate(pairs):
        S.op("pe", lambda e, l=l, r=r, i=i: e.matmul(out, l, r, start=(i == 0), stop=(i == n - 1)),
             reads=reads, writes=writes, inc=(i == n - 1))


def build(dbg=False, stop=None):
    nc = bass.Bass("TRN2", target_bir_lowering=False)
    S = Sync(nc)

    def din(name, shape, dt=F32):
        return nc.dram_tensor(name, list(shape), dt, kind="ExternalInput")

    x_loc = din("x_loc", [T, D])
    w_in = din("w_in", [D, INW])
    w_out = din("w_out", [D, D])
    w_gate = din("w_gate", [D, FH])
    w_up = din("w_up", [D, FH])
    w_down = din("w_down", [FH, D])
    pp_d = din("pp", [P, PP_N])
    rp_d = din("rp", [P, RP_N])
    cb_d = din("cb", [P, CB_N], BF16)
    out_d = nc.dram_tensor("out", [TO, D], F32, kind="ExternalOutput")

    sk = "ExternalOutput" if dbg else "Internal"
    s_aq = nc.dram_tensor("s_aq", [NH, P, TO], BF16, kind=sk)
    s_ak = nc.dram_tensor("s_ak", [NH, P, T], BF16, kind=sk)
    s_av = nc.dram_tensor("s_av", [T, NH * HD], BF16, kind=sk)
    s_mq = nc.dram_tensor("s_mq", [NH, P, TO], BF16, kind=sk)
    s_mk = nc.dram_tensor("s_mk", [NH, P, T], BF16, kind=sk)
    s_mv = nc.dram_tensor("s_mv", [T, NH * HD], BF16, kind=sk)
    s_mo = nc.dram_tensor("s_mo", [TO, NH * HD], BF16, kind=sk)
    s_x1 = nc.dram_tensor("s_x1", [TO, D], F32, kind=sk)
    s_h2 = nc.dram_tensor("s_h2", [16, P, KC * P], BF16, kind="Internal")
    dbg_outs = ["s_aq", "s_ak", "s_av", "s_mq", "s_mk", "s_mv", "s_mo", "s_x1"]
    if dbg:
        d_gates = nc.dram_tensor("d_gates", [P, 32 * 16], F32, kind="ExternalOutput")
        d_gp = nc.dram_tensor("d_gp", [P, 6 * 256], F32, kind="ExternalOutput")
        d_mixT = nc.dram_tensor("d_mixT", [P, 16 * TO], BF16, kind="ExternalOutput")
        dbg_outs += ["d_gates", "d_gp", "d_mixT"]
    B_aq, B_ak, B_av = Buf("s_aq", multi=True), Buf("s_ak", multi=True), Buf("s_av", multi=True)
    B_mq, B_mk, B_mv = Buf("s_mq", multi=True), Buf("s_mk", multi=True), Buf("s_mv", multi=True)
    B_mo, B_x1, B_out = Buf("s_mo", multi=True), Buf("s_x1", multi=True), Buf("out", multi=True)
    B_dbg = Buf("dbg", multi=True)
    B_h2 = Buf("s_h2", multi=True)

    def sb(name, shape, dt):
        return nc.sbuf_tensor(name, list(shape), dt)

    uid = [0]

    def SB(stack, name, shape, dt):
        uid[0] += 1
        name = f"{name}_{uid[0]}"
        t = stack.enter_context(nc.sbuf_tensor(name, list(shape), dt))
        return Buf(name, t)

    def PS(stack, name, shape, dt=F32):
        uid[0] += 1
        name = f"{name}_{uid[0]}"
        t = stack.enter_context(nc.psum_tensor(name, list(shape), dt))
        return Buf(name, t)

    from contextlib import ExitStack, contextmanager

    @contextmanager
    def stage():
        with ExitStack() as es:
            yield es
            S.barrier()

    with ExitStack() as top:
        pp = SB(top, "pp_sb", [P, PP_N], F32)
        cb = SB(top, "cbc", [P, CB_N], BF16)
        smallp = SB(top, "smallp", [P, 16 + 128 + 128 + 8], F32)
        gates_all = SB(top, "gates_all", [P, 32, 16], F32)
        halo = SB(top, "halo", [P, 16, 3], F32)
        gpU = SB(top, "gpU", [P, 32, 8], F32)
        gpUS = SB(top, "gpUS", [P, 32, 8], F32)
        gpG = SB(top, "gpG", [P, 32, 8], F32)
        gpGS = SB(top, "gpGS", [P, 32, 8], F32)
        gpEL = SB(top, "gpEL", [P, 32, 8], F32)
        S.dma("sp", pp.t[:], pp_d.ap(), writes=[pp], sembuf=pp)
        S.dma("sp", cb.t[:], cb_d.ap(), writes=[cb], sembuf=cb)
        S.dma("sp", smallp.t[:, 0:272], rp_d.ap()[:, RP_GB:RP_GB + 272], writes=[smallp], sembuf=smallp)
        ident = cb.t[:, CB_IDENT:CB_IDENT + 128]
        ones_b = cb.t[:, CB_ONES:CB_ONES + 128]
        maskA = cb.t[:, CB_MASKA:CB_MASKA + 256]
        maskC = cb.t[:, CB_MASKC:CB_MASKC + 256]
        maskLE = cb.t[:, CB_MASKA + 128:CB_MASKA + 256]
        mbA = cb.t[:, CB_MBA:CB_MBA + 256]
        mbC = cb.t[:, CB_MBC:CB_MBC + 256]
        ident_f = pp.t[:, PP_IDENT:PP_IDENT + 128]
        trineg_f = pp.t[:, PP_TRINEG:PP_TRINEG + 128]
        negones_f = pp.t[:, PP_NEGONES:PP_NEGONES + 128]
        ones_f = pp.t[:, PP_ONES:PP_ONES + 128]
        eps_col = pp.t[:, PP_EPS:PP_EPS + 1]
        qgs = smallp.t[:, 272:273]
        negshift = smallp.t[:, 273:274]
        S.op("dve", lambda e: e.tensor_scalar(qgs, pp.t[:, PP_QG:PP_QG + 1], QS, None, ALU.mult),
             reads=[pp], writes=[smallp])
        S.op("dve", lambda e: e.reduce_max(smallp.t[:, 274:275], smallp.t[:, 16:144], axis=AX.X,
                                           apply_absolute_value=True), reads=[smallp], writes=[smallp])
        S.op("dve", lambda e: e.reduce_max(smallp.t[:, 275:276], smallp.t[:, 144:272], axis=AX.X,
                                           apply_absolute_value=True), reads=[smallp], writes=[smallp])
        S.op("dve", lambda e: e.scalar_tensor_tensor(negshift, smallp.t[:, 274:275], -(HD ** 0.5),
                                                     smallp.t[:, 275:276], ALU.mult, ALU.mult),
             reads=[smallp], writes=[smallp])

        def norm_transpose(st, src_row_ap, src_buf, ntiles, grow, dstT, dst_bufs, tag):
            xb = [SB(st, f"xb{tag}{i}", [P, D], F32) for i in range(3)]
            xn = [SB(st, f"xn{tag}{i}", [P, D], BF16) for i in range(3)]
            junk = SB(st, f"junk{tag}", [P, D], BF16)
            stt = [SB(st, f"st{tag}{i}", [P, 8], F32) for i in range(3)]
            tp = [PS(st, f"tp{tag}{i}", [P, KC, P], BF16) for i in range(2)]
            dst_bufs2 = dst_bufs[len(dst_bufs) // 2:]
            dst_bufs = dst_bufs[:len(dst_bufs) // 2]
            def nA(i):
                j = i % 3
                jn = i % 3
                S.dma("sp", xb[j].t[:], src_row_ap(i), reads=[src_buf] if src_buf else [],
                      writes=[xb[j]], sembuf=xb[j])
                S.op("act", lambda e: e.activation(out=junk.t[:], in_=xb[j].t[:], func=AF.Square,
                                                   accum_out=stt[jn].t[:, 0:1]),
                     reads=[xb[j]], writes=[junk, stt[jn]])
                S.op("dve", lambda e: e.tensor_scalar(stt[jn].t[:, 1:2], stt[jn].t[:, 0:1], 1.0 / D, EPS,
                                                      ALU.mult, ALU.add), reads=[stt[jn]], writes=[stt[jn]])
                S.op("act", lambda e: e.activation(out=stt[jn].t[:, 2:3], in_=stt[jn].t[:, 1:2], func=AF.Sqrt),
                     reads=[stt[jn]], writes=[stt[jn]])
                S.op("dve", lambda e: e.reciprocal(stt[jn].t[:, 3:4], stt[jn].t[:, 2:3]),
                     reads=[stt[jn]], writes=[stt[jn]])
                S.op("dve", lambda e: e.scalar_tensor_tensor(xn[jn].t[:], xb[j].t[:], stt[jn].t[:, 3:4],
                                                             grow.t[:], ALU.mult, ALU.mult),
                     reads=[xb[j], stt[jn], grow], writes=[xn[jn]])

            def nB(i):
                jn = i % 3
                jt = i % 2
                for k in range(KC):
                    S.op("pe", lambda e, k=k: e.transpose(tp[jt].t[:, k, :], xn[jn].t[:, k * P:(k + 1) * P], ident),
                         reads=[xn[jn], cb], writes=[tp[jt]], inc=(k == KC - 1))
                S.op("act", lambda e: e.copy(out=dstT.t[:, :, i * P:(i + 1) * P], in_=tp[jt].t[:]),
                     reads=[tp[jt]], writes=[dst_bufs[i], dst_bufs2[i]])

            for i in range(ntiles + 2):
                if i < ntiles:
                    nA(i)
                if i >= 2:
                    nB(i - 2)

        def w_src(wd, ncols_total, col0, ncols, nk=KC, row0=0):
            return bass.AP(wd, row0 * ncols_total + col0, [[ncols_total, P], [P * ncols_total, nk], [1, ncols]])

        with stage() as st:
            hT = SB(st, "hT", [P, KC, 2048], BF16)
            hT_b = [Buf(f"hTb{i}") for i in range(16)]
            hT_b2 = [Buf(f"hTc{i}") for i in range(16)]
            grow1 = SB(st, "grow1", [P, D], F32)
            S.dma("sp", grow1.t[:], rp_d.ap()[:, RP_G1:RP_G1 + D], writes=[grow1], sembuf=grow1)
            wb = [SB(st, f"wb{i}", [P, KC, 512], BF16) for i in range(2)]
            wg16 = SB(st, "wg16", [P, KC, 16], BF16)
            S.dma("pool", wg16.t[:], w_src(w_in, INW, 7168, 16), writes=[wg16], sembuf=wg16)
            stg = [SB(st, f"stg{i}", [P, 2048], BF16) for i in range(2)]
            vst = [SB(st, f"vst{i}", [P, 512], BF16) for i in range(4)]
            cbuf = [SB(st, f"cbuf{i}", [P, 515], F32) for i in range(2)]
            cacc = [SB(st, f"cacc{i}", [P, 512], F32) for i in range(2)]
            sq = [SB(st, f"sq{i}", [P, 512], BF16) for i in range(2)]
            sv = [SB(st, f"sv{i}", [P, 512], F32) for i in range(2)]
            sigf = [SB(st, f"sigf{i}", [P, 512], F32) for i in range(2)]
            mgrow1 = SB(st, "mgrow1", [P, 1024], F32)
            S.dma("sp", mgrow1.t[:], rp_d.ap()[:, RP_MG:RP_MG + 1024], writes=[mgrow1], sembuf=mgrow1)
            S.op("dve", lambda e: e.memset(halo.t[:], 0.0), writes=[halo])
            cnt = {"stg": 0, "vst": 0, "r": 0}

            for is_ctx in (True, False):
                tok0 = 0 if is_ctx else TC
                pre_cols = (1024, 1536) if is_ctx else (0, 512)
                prefetched = stop != "inproj_small"
                if prefetched:
                    for gi_, col_ in enumerate(pre_cols):
                        S.dma("pool", wb[gi_].t[:], w_src(w_in, INW, col_, 512), writes=[wb[gi_]], sembuf=wb[gi_])
                with stage() as st1:
                    norm_transpose(st1, lambda i: x_loc.ap()[tok0 + i * P: tok0 + (i + 1) * P, :], None, 16,
                                   grow1, hT, hT_b + hT_b2, "a")
                with stage() as st2:
                    psA = [PS(st2, f"psA{i}", [P, 512]) for i in range(4)]
                    psB = [PS(st2, f"psB{i}", [P, 512]) for i in range(2)]
                    psG = PS(st2, "psG", [P, 512])
                    groups = []
                    if not is_ctx:
                        groups += [("aq", 0, 0), ("aq", 512, 4)]
                    groups += [("ak", 1024, 0), ("ak", 1536, 4)]
                    groups += [("mq", 3072, 0), ("mq", 3584, 4)]
                    groups += [("mk", 4096, 0), ("mk", 4608, 4)]
                    groups += [("av", 2048, 0), ("av", 2560, 1), ("mv", 5120, 0), ("mv", 5632, 1)]
                    if not is_ctx:
                        groups += [("mo", 6144, 0), ("mo", 6656, 1)]
                    if stop == "inproj_small":
                        groups = groups[:1] + [g for g in groups if g[0] == "av"][:1]

                    def load_w(gi):
                        kind, col0, _ = groups[gi]
                        S.dma("pool", wb[gi % 2].t[:], w_src(w_in, INW, col0, 512), writes=[wb[gi % 2]],
                              sembuf=wb[gi % 2])

                    if not prefetched:
                        load_w(0)
                    psi = 0
                    pending = []
                    for gi, (kind, col0, hb) in enumerate(groups):
                        if gi + 1 < len(groups) and not (prefetched and gi == 0):
                            load_w(gi + 1)
                        w = wb[gi % 2]
                        if kind in ("aq", "ak", "mq", "mk"):
                            for hc in range(4):
                                head = hb + hc
                                halo_only = (kind == "mq" and is_ctx)
                                sg = stg[cnt["stg"] % 2]
                                if not halo_only:
                                    cnt["stg"] += 1
                                hq = head + (8 if kind == "mk" else 0)
                                tgs = [3] if halo_only else [0, 1, 2, 3]
                                for tg in tgs:
                                    ps = psA[psi % 4]
                                    psi += 1
                                    mm_group(S, ps.t[:],
                                             [(w.t[:, k, hc * P:(hc + 1) * P], hT.t[:, k, tg * 512:(tg + 1) * 512])
                                              for k in range(KC)],
                                             reads=[w] + hT_b[tg * 4:(tg + 1) * 4] + hT_b2[tg * 4:(tg + 1) * 4], writes=[ps])
                                    r = cnt["r"] % 2
                                    cnt["r"] += 1
                                    dst = sg.t[:, tg * 512:(tg + 1) * 512]
                                    if kind in ("aq", "ak"):
                                        gcol = qgs if kind == "aq" else pp.t[:, PP_KG:PP_KG + 1]
                                        S.op("act", lambda e: e.activation(out=sq[r].t[:], in_=ps.t[:], func=AF.Square),
                                             reads=[ps], writes=[sq[r]])
                                        for fn_ in pending:
                                            fn_()
                                        pending.clear()

                                        def rest(ps=ps, r=r, dst=dst, gcol=gcol, sg=sg):
                                            S.op("pe", lambda e: e.matmul(psB[r].t[:], ones_b, sq[r].t[:], start=True, stop=True),
                                                 reads=[sq[r], cb], writes=[psB[r]])
                                            S.op("act", lambda e: e.activation(out=sv[r].t[:], in_=psB[r].t[:], func=AF.Ln,
                                                                               bias=eps_col, scale=1.0 / HD),
                                                 reads=[psB[r], pp], writes=[sv[r]])
                                            S.op("act", lambda e: e.activation(out=sv[r].t[:], in_=sv[r].t[:], func=AF.Exp,
                                                                               scale=-0.5),
                                                 reads=[sv[r]], writes=[sv[r]])
                                            S.op("dve", lambda e: e.scalar_tensor_tensor(dst, ps.t[:], gcol, sv[r].t[:],
                                                                                         ALU.mult, ALU.mult),
                                                 reads=[ps, sv[r], smallp, pp], writes=[sg])
                                        pending.append(rest)
                                    else:
                                        cbf_ = cbuf[r]
                                        if tg == tgs[0]:
                                            S.op("dve", lambda e: e.tensor_copy(cbf_.t[:, 0:3], halo.t[:, hq, :]),
                                                 reads=[halo], writes=[cbf_])
                                        S.op("act", lambda e: e.copy(out=cbf_.t[:, 3:515], in_=ps.t[:]),
                                             reads=[ps], writes=[cbf_])
                                        for fn_ in pending:
                                            fn_()
                                        pending.clear()
                                        if halo_only or (is_ctx and tg == 3):
                                            S.op("dve", lambda e: e.tensor_copy(halo.t[:, hq, :], cbf_.t[:, 512:515]),
                                                 reads=[cbf_], writes=[halo])
                                        if halo_only:
                                            continue
                                        if tg < 3:
                                            nxt = cbuf[(r + 1) % 2]
                                            S.op("dve", lambda e: e.tensor_copy(nxt.t[:, 0:3], cbf_.t[:, 512:515]),
                                                 reads=[cbf_], writes=[nxt])
                                        ac = cacc[r]
                                        S.op("dve", lambda e: e.tensor_scalar(ac.t[:], cbf_.t[:, 0:512],
                                                                              pp.t[:, PP_CONVW + hq * 4:PP_CONVW + hq * 4 + 1],
                                                                              None, ALU.mult),
                                             reads=[cbf_, pp], writes=[ac])
                                        for jj in (1, 2, 3):
                                            S.op("dve", lambda e, jj=jj: e.scalar_tensor_tensor(
                                                ac.t[:], cbf_.t[:, jj:jj + 512],
                                                pp.t[:, PP_CONVW + hq * 4 + jj:PP_CONVW + hq * 4 + jj + 1],
                                                ac.t[:], ALU.mult, ALU.add), reads=[cbf_, pp, ac], writes=[ac])
                                        def silu_(dst=dst, ac=ac, hq=hq, sg=sg):
                                            S.op("act", lambda e: e.activation(out=dst, in_=ac.t[:], func=AF.Silu,
                                                                               bias=pp.t[:, PP_CONVB + hq:PP_CONVB + hq + 1]),
                                                 reads=[ac, pp], writes=[sg])
                                        pending.append(silu_)
                                for fn_ in pending:
                                    fn_()
                                pending.clear()
                                if halo_only:
                                    continue
                                sdst, sB = {"aq": (s_aq, B_aq), "ak": (s_ak, B_ak), "mq": (s_mq, B_mq),
                                            "mk": (s_mk, B_mk)}[kind]
                                t0 = tok0 if kind in ("ak", "mk") else 0
                                S.dma("pool", sdst.ap()[head, :, t0:t0 + 2048], sg.t[:], reads=[sg], writes=[sB],
                                      sembuf=sg)
                        else:
                            for tt in range(16):
                                ps = psA[psi % 4]
                                psi += 1
                                mm_group(S, ps.t[:],
                                         [(hT.t[:, k, tt * P:(tt + 1) * P], w.t[:, k, :]) for k in range(KC)],
                                         reads=[w, hT_b[tt], hT_b2[tt]], writes=[ps])
                                vb = vst[cnt["vst"] % 4]
                                cnt["vst"] += 1
                                if kind == "mo":
                                    sf = sigf[cnt["vst"] % 2]
                                    S.op("act", lambda e: e.activation(out=sf.t[:], in_=ps.t[:], func=AF.Sigmoid),
                                         reads=[ps], writes=[sf])
                                    S.op("dve", lambda e: e.tensor_tensor(vb.t[:], sf.t[:],
                                                                          mgrow1.t[:, hb * 512:(hb + 1) * 512], ALU.mult),
                                         reads=[sf, mgrow1], writes=[vb])
                                elif tt % 2 == 0:
                                    S.op("act", lambda e: e.copy(out=vb.t[:], in_=ps.t[:]), reads=[ps], writes=[vb])
                                else:
                                    S.op("dve", lambda e: e.tensor_copy(vb.t[:], ps.t[:]), reads=[ps], writes=[vb])
                                sdst, sB = {"av": (s_av, B_av), "mv": (s_mv, B_mv), "mo": (s_mo, B_mo)}[kind]
                                t0 = 0 if kind == "mo" else tok0
                                S.dma("pool", sdst.ap()[t0 + tt * P:t0 + (tt + 1) * P, hb * 512:(hb + 1) * 512],
                                      vb.t[:], reads=[vb], writes=[sB], sembuf=vb)
                    for tt in range(16):
                        mm_group(S, psG.t[:, 0:16],
                                 [(hT.t[:, k, tt * P:(tt + 1) * P], wg16.t[:, k, :]) for k in range(KC)],
                                 reads=[wg16, hT_b[tt], hT_b2[tt]], writes=[psG])
                        ch = tok0 // P + tt
                        S.op("dve", lambda e: e.tensor_tensor(gates_all.t[:, ch, :], psG.t[:, 0:16],
                                                              smallp.t[:, 0:16], ALU.add),
                             reads=[psG, smallp], writes=[gates_all])
            S.wait_all("sp", [B_aq, B_ak, B_av, B_mq, B_mk, B_mv, B_mo])
            S.wait_all("pool", [B_aq, B_ak, B_av, B_mq, B_mk, B_mv, B_mo])
            if dbg:
                S.dma("sp", d_gates.ap(), bass.AP(gates_all.t, 0, [[512, P], [1, 512]]), reads=[gates_all],
                      writes=[B_dbg], sembuf=gates_all)

        if stop in ("inproj", "inproj_small"):
            S.wait_all("sp", [B_dbg])
            return nc, dbg_outs, S

        def flat(t, n):
            return bass.AP(t, 0, [[n, P], [1, n]])

        with stage() as st:
            ge = SB(st, "ge", [P, 32, 8], F32)
            lsp = SB(st, "lsp", [P, 32, 8], F32)
            gB = SB(st, "gB", [P, 32, 8], F32)
            gBL = SB(st, "gBL", [P, 32, 8], F32)
            gC = SB(st, "gC", [P, 32, 8], F32)
            gMAXC = SB(st, "gMAXC", [P, 32, 8], F32)
            gM = SB(st, "gM", [P, 32, 8], F32)
            gMP = SB(st, "gMP", [P, 33, 8], F32)
            gtmp = SB(st, "gtmp", [P, 32, 8], F32)
            gcol = SB(st, "gcol", [P, 2], F32)
            gdiag = [SB(st, f"gdiag{i}", [P, P], F32) for i in range(2)]
            p1 = PS(st, "gp1", [P, 512])
            p2 = PS(st, "gp2", [P, 512])
            p3 = PS(st, "gp3", [P, 512])
            p4 = PS(st, "gp4", [P, 512])
            S.op("act", lambda e: e.activation(out=ge.t[:], in_=gates_all.t[:, :, 8:16], func=AF.Exp, scale=-1.0),
                 reads=[gates_all], writes=[ge])
            S.op("act", lambda e: e.activation(out=lsp.t[:], in_=ge.t[:], func=AF.Ln, bias=1.0),
                 reads=[ge], writes=[lsp])
            S.op("pe", lambda e: e.matmul(p1.t[:, 0:256], trineg_f, flat(lsp.t, 256), start=True, stop=True),
                 reads=[lsp, pp], writes=[p1])
            S.op("pe", lambda e: e.matmul(p2.t[:, 0:256], negones_f, flat(lsp.t, 256), start=True, stop=True),
                 reads=[lsp, pp], writes=[p2])
            S.op("dve", lambda e: e.tensor_copy(flat(gB.t, 256), p1.t[:, 0:256]), reads=[p1], writes=[gB])
            S.op("dve", lambda e: e.tensor_copy(flat(gBL.t, 256), p2.t[:, 0:256]), reads=[p2], writes=[gBL])
            S.op("dve", lambda e: e.tensor_tensor(gC.t[:], gates_all.t[:, :, 0:8], gB.t[:], ALU.subtract),
                 reads=[gates_all, gB], writes=[gC])
            for hf in range(2):
                S.op("pe", lambda e: e.transpose(p3.t[:, hf * P:(hf + 1) * P], flat(gC.t, 256)[:, hf * P:(hf + 1) * P],
                                                 ident_f), reads=[gC, pp], writes=[p3])
                S.op("dve", lambda e: e.reduce_max(gcol.t[:, hf:hf + 1], p3.t[:, hf * P:(hf + 1) * P], axis=AX.X),
                     reads=[p3], writes=[gcol])
                S.op("dve", lambda e: e.tensor_scalar(gdiag[hf].t[:], ident_f, gcol.t[:, hf:hf + 1], None, ALU.mult),
                     reads=[gcol, pp], writes=[gdiag[hf]])
                S.op("pe", lambda e: e.matmul(p4.t[:, hf * P:(hf + 1) * P], ones_f, gdiag[hf].t[:], start=True, stop=True),
                     reads=[gdiag[hf], pp], writes=[p4])
            S.op("dve", lambda e: e.tensor_copy(flat(gMAXC.t, 256), p4.t[:, 0:256]), reads=[p4], writes=[gMAXC])
            S.op("dve", lambda e: e.memset(gMP.t[:, 0, :], MNEG), writes=[gMP])
            for c in range(32):
                S.op("dve", lambda e: e.tensor_tensor(gM.t[:, c, :], gMP.t[:, c, :], gMAXC.t[:, c, :], ALU.max),
                     reads=[gMP, gMAXC], writes=[gM])
                S.op("dve", lambda e: e.tensor_tensor(gMP.t[:, c + 1, :], gBL.t[:, c, :], gM.t[:, c, :], ALU.add),
                     reads=[gBL, gM], writes=[gMP])
                if c == 15:
                    S.op("dve", lambda e: e.tensor_scalar(gMP.t[:, 16, :], gMP.t[:, 16, :],
                                                          pp.t[:, PP_CBIAS:PP_CBIAS + 1], None, ALU.add),
                         reads=[gMP, pp], writes=[gMP])
            S.op("dve", lambda e: e.tensor_tensor(gtmp.t[:], gMP.t[:, 0:32, :], gM.t[:], ALU.subtract),
                 reads=[gMP, gM], writes=[gtmp])
            S.op("act", lambda e: e.activation(out=gpG.t[:], in_=gtmp.t[:], func=AF.Exp), reads=[gtmp], writes=[gpG])
            S.op("dve", lambda e: e.tensor_scalar(gpGS.t[:], gpG.t[:], QS, None, ALU.mult), reads=[gpG], writes=[gpGS])
            S.op("dve", lambda e: e.tensor_tensor(gtmp.t[:], gC.t[:], gM.t[:], ALU.subtract),
                 reads=[gC, gM, gpG], writes=[gtmp])
            S.op("act", lambda e: e.activation(out=gpU.t[:], in_=gtmp.t[:], func=AF.Exp), reads=[gtmp], writes=[gpU])
            S.op("dve", lambda e: e.tensor_scalar(gpUS.t[:], gpU.t[:], QS, None, ALU.mult), reads=[gpU], writes=[gpUS])
            S.op("dve", lambda e: e.tensor_tensor(gtmp.t[:], gB.t[:], gM.t[:], ALU.add),
                 reads=[gB, gM, gpU], writes=[gtmp])
            S.op("act", lambda e: e.activation(out=gpEL.t[:], in_=gtmp.t[:], func=AF.Exp, scale=-1.0),
                 reads=[gtmp], writes=[gpEL])
            S.op("dve", lambda e: e.tensor_scalar(gpEL.t[:], gpEL.t[:], 1.0 / QS, None, ALU.mult),
                 reads=[gpEL], writes=[gpEL])
            if dbg:
                for i, bsrc in enumerate([gpU, gpG, gpEL, gM, gB, gC]):
                    S.dma("sp", d_gp.ap()[:, i * 256:(i + 1) * 256], flat(bsrc.t, 256), reads=[bsrc],
                          writes=[B_dbg], sembuf=bsrc)
                S.wait_all("sp", [B_dbg])
        if stop == "gates":
            return nc, dbg_outs, S

        with stage() as mixst:
            mixT = SB(mixst, "mixT", [P, 16, TO], BF16)
            mix_b = [Buf(f"mixb{i}") for i in range(16)]
            with stage() as st:
                qT = [SB(st, f"aqT{i}", [P, TO], BF16) for i in range(2)]
                kT = [SB(st, f"akT{i}", [P, T], BF16) for i in range(2)]
                v1 = [SB(st, f"av1{i}", [P, 17, P], BF16) for i in range(2)]
                v2 = [SB(st, f"av2{i}", [P, 4, 5, P], BF16) for i in range(2)]
                v3 = [SB(st, f"av3{i}", [P, 16, 2, P], BF16) for i in range(2)]
                accs = [SB(st, f"aacc{i}", [P, 2, TO], F32) for i in range(2)]
                pT = [SB(st, f"apT{i}", [P, 256], BF16) for i in range(4)]
                sp_ = [PS(st, f"asp{i}", [P, 512]) for i in range(3)]
                po_ = [PS(st, f"apo{i}", [P, 512]) for i in range(3)]

                def load_head(h):
                    j = h % 2
                    S.dma("sp", qT[j].t[:], s_aq.ap()[h], reads=[B_aq], writes=[qT[j]], sembuf=qT[j])
                    S.dma("sp", kT[j].t[:], s_ak.ap()[h], reads=[B_ak], writes=[kT[j]], sembuf=kT[j])
                    S.dma("sp", v1[j].t[:], bass.AP(s_av, (TC - P) * 1024 + h * P, [[1024, P], [P * 1024, 17], [1, P]]),
                          reads=[B_av], writes=[v1[j]], sembuf=v1[j])
                    S.dma("sp", v2[j].t[:], bass.AP(s_av, (4 * P * 3) * 1024 + h * P,
                                                    [[4 * 1024, P], [1024, 4], [512 * 1024, 5], [1, P]]),
                          reads=[B_av], writes=[v2[j]], sembuf=v2[j])
                    for kb in range(2):
                        S.dma("sp", v3[j].t[:, :, kb, :], bass.AP(s_av, kb * 2048 * 1024 + h * P,
                                                                  [[16 * 1024, P], [1024, 16], [1, P]]),
                              reads=[B_av], writes=[v3[j]], sembuf=v3[j])

                units = []
                for h in range(NH):
                    for pi, dil in enumerate((1, 4, 16)):
                        for r in range(dil):
                            for qb in range(16 // dil):
                                units.append((h, pi, dil, r, qb))

                def phaseA(u):
                    h, pi, dil, r, qb = units[u]
                    j = h % 2
                    if (pi, r, qb) == (0, 0, 2) and h + 1 < NH:
                        load_head(h + 1)
                    q0 = r + dil * P * qb
                    qsl = slice(q0, q0 + dil * (P - 1) + 1, dil)
                    kc0 = TC + q0
                    kp0 = kc0 - dil * P
                    ksl_c = slice(kc0, kc0 + dil * (P - 1) + 1, dil)
                    ksl_p = slice(kp0, kp0 + dil * (P - 1) + 1, dil)
                    sp, pt = sp_[u % 3], pT[u % 4]
                    mb = mbC if qb == 0 else mbA
                    S.op("pe", lambda e: e.matmul(sp.t[:, 0:256], ident, mb, start=True, stop=False),
                         reads=[cb], writes=[sp], inc=False)
                    S.op("pe", lambda e: e.matmul(sp.t[:, 0:P], kT[j].t[:, ksl_p], qT[j].t[:, qsl],
                                                  start=False, stop=False),
                         reads=[kT[j], qT[j]], writes=[sp], inc=False)
                    S.op("pe", lambda e: e.matmul(sp.t[:, P:2 * P], kT[j].t[:, ksl_c], qT[j].t[:, qsl],
                                                  start=False, stop=True),
                         reads=[kT[j], qT[j]], writes=[sp])
                    S.op("act", lambda e: e.activation(out=pt.t[:], in_=sp.t[:, 0:256], func=AF.Exp,
                                                       bias=negshift), reads=[sp, smallp], writes=[pt])

                def phaseB(u):
                    h, pi, dil, r, qb = units[u]
                    j = h % 2
                    acc = accs[j]
                    q0 = r + dil * P * qb
                    if pi == 0:
                        vp, vc, vbuf = v1[j].t[:, qb, :], v1[j].t[:, qb + 1, :], v1[j]
                    elif pi == 1:
                        vp, vc, vbuf = v2[j].t[:, r, qb, :], v2[j].t[:, r, qb + 1, :], v2[j]
                    else:
                        vp, vc, vbuf = v3[j].t[:, r, 0, :], v3[j].t[:, r, 1, :], v3[j]
                    po, pt = po_[u % 3], pT[u % 4]
                    S.op("pe", lambda e: e.matmul(po.t[:, 0:P], vp, pt.t[:, 0:P], start=True, stop=False),
                         reads=[pt, vbuf], writes=[po], inc=False)
                    S.op("pe", lambda e: e.matmul(po.t[:, 0:P], vc, pt.t[:, P:2 * P], start=False, stop=True),
                         reads=[pt, vbuf], writes=[po], inc=False)
                    S.op("pe", lambda e: e.matmul(po.t[:, P:2 * P], ones_b, pt.t[:, 0:P], start=True, stop=False),
                         reads=[pt, cb], writes=[po], inc=False)
                    S.op("pe", lambda e: e.matmul(po.t[:, P:2 * P], ones_b, pt.t[:, P:2 * P], start=False, stop=True),
                         reads=[pt, cb], writes=[po])
                    accv = bass.AP(acc.t, q0, [[2 * TO, P], [TO, 2], [dil, P]])
                    pov = bass.AP(po.t, 0, [[512, P], [P, 2], [1, P]])
                    if pi == 0:
                        S.op("act", lambda e: e.copy(out=accv, in_=pov), reads=[po], writes=[acc])
                    else:
                        S.op("dve", lambda e: e.tensor_tensor(accv, accv, pov, ALU.add),
                             reads=[po, acc], writes=[acc])
                    if (pi, r, qb) == (2, 15, 0):
                        S.op("act", lambda e: e.activation(out=acc.t[:, 1, :], in_=acc.t[:, 1, :], func=AF.Ln),
                             reads=[acc], writes=[acc])
                        S.op("act", lambda e: e.activation(out=acc.t[:, 1, :], in_=acc.t[:, 1, :], func=AF.Exp, scale=-1.0),
                             reads=[acc], writes=[acc])
                        S.op("dve", lambda e: e.tensor_tensor(mixT.t[:, h, :], acc.t[:, 0, :], acc.t[:, 1, :], ALU.mult),
                             reads=[acc], writes=mix_b)

                load_head(0)
                SK = 2
                for idx in range(len(units) + SK):
                    if idx < len(units):
                        phaseA(idx)
                    if idx >= SK:
                        phaseB(idx - SK)
            if stop == "attn":
                if dbg:
                    S.dma("sp", d_mixT.ap(), bass.AP(mixT.t, 0, [[16 * TO, P], [1, 16 * TO]]), reads=mix_b,
                          writes=[B_dbg], sembuf=mixT)
                    S.wait_all("sp", [B_dbg])
                return nc, dbg_outs, S

            with stage() as st:
                kk = [SB(st, f"mkk{i}", [P, NH, 512], BF16) for i in range(2)]
                qq = [SB(st, f"mqq{i}", [P, NH, 512], BF16) for i in range(2)]
                vv = [SB(st, f"mvv{i}", [P, 4, NH, 132], BF16) for i in range(2)]
                mo = [SB(st, f"mmo{i}", [P, 4, 1024], BF16) for i in range(2)]
                Cst = [SB(st, f"mCst{g}", [P, 4, 132], F32) for g in range(2)]
                CGf = [SB(st, f"mCG{i}", [P, 4, 132], F32) for i in range(2)]
                cbf = [SB(st, f"mcbf{i}", [P, 4, 132], BF16) for i in range(2)]
                vu = [SB(st, f"mvu{i}", [P, 4, 132], BF16) for i in range(4)]
                wt = [SB(st, f"mwt{i}", [P, 4, P], BF16) for i in range(2)]
                ktok = [SB(st, f"mktok{i}", [P, 4, P], BF16) for i in range(2)]
                hh = [SB(st, f"mhh{i}", [P, 4, P], F32) for i in range(4)]
                sqj = [SB(st, f"msqj{i}", [P, 4, P], F32) for i in range(4)]
                ot = [SB(st, f"mot{i}", [P, 4, P], BF16) for i in range(4)]
                sm = [SB(st, f"msm{i}", [P, 32], F32) for i in range(4)]
                FA = [PS(st, f"mFA{i}", [P, 4, P]) for i in range(2)]
                FC = [PS(st, f"mFC{i}", [P, 4, P]) for i in range(2)]
                FD = [PS(st, f"mFD{i}", [P, 512]) for i in range(2)]
                FB = [PS(st, f"mFB{i}", [P, 8, P], BF16) for i in range(2)]
                for g in range(2):
                    S.op("dve", lambda e, g=g: e.memset(Cst[g].t[:], 0.0), writes=[Cst[g]])
                for i in range(2):
                    S.op("dve", lambda e, i=i: e.memset(vv[i].t[:, :, :, 128:129], 1.0), writes=[vv[i]])

                def load_grp(g):
                    j = g % 2
                    S.dma("sp", kk[j].t[:], bass.AP(s_mk, g * 512, [[T, P], [P * T, NH], [1, 512]]),
                          reads=[B_mk], writes=[kk[j]], sembuf=kk[j])
                    for ci_ in range(4):
                        S.dma("sp", vv[j].t[:, ci_, :, 0:P],
                              bass.AP(s_mv, (g * 512 + ci_ * P) * 1024, [[1024, P], [P, NH], [1, P]]),
                              reads=[B_mv], writes=[vv[j]], sembuf=vv[j])
                    if g >= 4:
                        S.dma("sp", qq[j].t[:], bass.AP(s_mq, (g - 4) * 512, [[TO, P], [P * TO, NH], [1, 512]]),
                              reads=[B_mq], writes=[qq[j]], sembuf=qq[j])

                def load_mo(g):
                    j = g % 2
                    S.dma("sp", mo[j].t[:], bass.AP(s_mo, (g - 4) * 512 * 1024, [[1024, P], [P * 1024, 4], [1, 1024]]),
                          reads=[B_mo], writes=[mo[j]], sembuf=mo[j])

                munits = [(c, hg) for c in range(32) for hg in range(2)]
                mask_bc = bass.AP(cb.t, CB_MASKA + P, [[CB_N, P], [0, 4], [1, P]])

                def bc4(t, off, n):
                    return bass.AP(t, off, [[256, P], [1, 4], [0, n]])

                def phaseV(u):
                    c, hg = munits[u]
                    g, ci = c // 4, c % 4
                    j = g % 2
                    hs = slice(hg * 4, hg * 4 + 4)
                    if ci == 1 and hg == 1 and g + 1 < 8:
                        load_grp(g + 1)
                    if ci == 3 and hg == 0 and 4 <= g + 1 < 8:
                        load_mo(g + 1)
                    S.op("pool", lambda e: e.tensor_tensor(vu[u % 4].t[:, :, 0:129], vv[j].t[:, ci, hs, 0:129],
                                                           bc4(gpU.t, c * 8 + hg * 4, 129), ALU.mult),
                         reads=[vv[j], gpU], writes=[vu[u % 4]])

                def phaseX(u):
                    c, hg = munits[u]
                    g, ci = c // 4, c % 4
                    j = g % 2
                    s = u % 2
                    vu_ = vu[u % 4]
                    own = c >= 16
                    csl = slice(ci * P, (ci + 1) * P)
                    hs = slice(hg * 4, hg * 4 + 4)
                    if own:
                        for h4 in range(4):
                            S.op("pe", lambda e, h4=h4: e.matmul(FA[s].t[:, h4, :], kk[j].t[:, hg * 4 + h4, csl],
                                                                 qq[j].t[:, hg * 4 + h4, csl], start=True, stop=True),
                                 reads=[kk[j], qq[j]], writes=[FA[s]], inc=(h4 == 3))
                        S.op("dve", lambda e: e.tensor_tensor(wt[s].t[:], FA[s].t[:], mask_bc, ALU.mult),
                             reads=[FA[s], cb], writes=[wt[s]])
                    if c < 31:
                        for h4 in range(4):
                            S.op("pe", lambda e, h4=h4: e.transpose(FB[s].t[:, h4, :], kk[j].t[:, hg * 4 + h4, csl], ident),
                                 reads=[kk[j], cb], writes=[FB[s]], inc=(h4 == 3))
                        S.op("act", lambda e: e.copy(out=ktok[s].t[:], in_=FB[s].t[:, 0:4, :]),
                             reads=[FB[s]], writes=[ktok[s]])
                        for h4 in range(4):
                            S.op("pe", lambda e, h4=h4: e.matmul(FC[s].t[:, h4, :], ktok[s].t[:, h4, :],
                                                                 vu_.t[:, h4, 0:P], start=True, stop=True),
                                 reads=[ktok[s], vu_], writes=[FC[s]], inc=False)
                            S.op("pe", lambda e, h4=h4: e.matmul(FD[s].t[:, 8 + h4:9 + h4], ktok[s].t[:, h4, :],
                                                                 vu_.t[:, h4, P:P + 1], start=True, stop=True),
                                 reads=[ktok[s], vu_], writes=[FD[s]], inc=(h4 == 3))
                    S.op("dve", lambda e: e.tensor_tensor(CGf[s].t[:, :, 0:129], Cst[hg].t[:, :, 0:129],
                                                          bc4(gpG.t, c * 8 + hg * 4, 129), ALU.mult),
                         reads=[Cst[hg], gpG], writes=[CGf[s]])
                    if own:
                        S.op("act", lambda e: e.copy(out=cbf[s].t[:, :, 0:129], in_=CGf[s].t[:, :, 0:129]),
                             reads=[CGf[s]], writes=[cbf[s]])
                    if c < 31:
                        S.op("dve", lambda e: e.tensor_tensor(Cst[hg].t[:, :, 0:P], CGf[s].t[:, :, 0:P], FC[s].t[:],
                                                              ALU.add),
                             reads=[CGf[s], FC[s]], writes=[Cst[hg]])
                        S.op("dve", lambda e: e.tensor_tensor(Cst[hg].t[:, :, P:P + 1], CGf[s].t[:, :, P:P + 1],
                                                              bass.AP(FD[s].t, 8, [[512, P], [1, 4], [1, 1]]), ALU.add),
                             reads=[CGf[s], FD[s]], writes=[Cst[hg]])
                    if own:
                        for h4 in range(4):
                            h = hg * 4 + h4
                            S.op("pe", lambda e, h4=h4: e.matmul(FA[s].t[:, h4, :], wt[s].t[:, h4, :], vu_.t[:, h4, 0:P],
                                                                 start=True, stop=False),
                                 reads=[wt[s], vu_], writes=[FA[s]], inc=False)
                            S.op("pe", lambda e, h4=h4, h=h: e.matmul(FA[s].t[:, h4, :], qq[j].t[:, h, csl],
                                                                      cbf[s].t[:, h4, 0:P], start=False, stop=True),
                                 reads=[qq[j], cbf[s]], writes=[FA[s]], inc=False)
                            S.op("pe", lambda e, h4=h4: e.matmul(FD[s].t[:, h4:h4 + 1], wt[s].t[:, h4, :],
                                                                 vu_.t[:, h4, P:P + 1], start=True, stop=False),
                                 reads=[wt[s], vu_], writes=[FD[s]], inc=False)
                            S.op("pe", lambda e, h4=h4, h=h: e.matmul(FD[s].t[:, h4:h4 + 1], qq[j].t[:, h, csl],
                                                                      cbf[s].t[:, h4, P:P + 1], start=False, stop=True),
                                 reads=[qq[j], cbf[s]], writes=[FD[s]], inc=(h4 == 3))

                def phaseYa(u):
                    c, hg = munits[u]
                    if c < 16:
                        return
                    s = u % 2
                    r4 = u % 4
                    m = sm[r4]
                    S.op("dve", lambda e: e.tensor_copy(m.t[:, 0:4], FD[s].t[:, 0:4]), reads=[FD[s]], writes=[m])
                    S.op("dve", lambda e: e.scalar_tensor_tensor(m.t[:, 4:8], m.t[:, 0:4], -1.0, m.t[:, 0:4],
                                                                 ALU.mult, ALU.max), reads=[m], writes=[m])
                    S.op("dve", lambda e: e.tensor_tensor(m.t[:, 4:8], m.t[:, 4:8], gpEL.t[:, c, hg * 4:hg * 4 + 4],
                                                          ALU.max), reads=[m, gpEL], writes=[m])
                    S.op("dve", lambda e: e.reciprocal(m.t[:, 8:12], m.t[:, 4:8]), reads=[m], writes=[m])
                    S.op("dve", lambda e: e.tensor_tensor(hh[r4].t[:], FA[s].t[:],
                                                          bass.AP(m.t, 8, [[32, P], [1, 4], [0, P]]), ALU.mult),
                         reads=[FA[s], m], writes=[hh[r4]])
                    S.op("act", lambda e: e.activation(out=sqj[r4].t[:], in_=hh[r4].t[:], func=AF.Square),
                         reads=[hh[r4]], writes=[sqj[r4]])

                def phaseYb(u):
                    c, hg = munits[u]
                    if c < 16:
                        return
                    r4 = u % 4
                    m = sm[r4]
                    S.op("dve", lambda e: e.reduce_sum(m.t[:, 12:16], sqj[r4].t[:], axis=AX.X),
                         reads=[sqj[r4], m], writes=[m])
                    S.op("dve", lambda e: e.tensor_scalar(m.t[:, 16:20], m.t[:, 12:16], 1.0 / HD, EPS, ALU.mult, ALU.add),
                         reads=[m], writes=[m])
                    S.op("act", lambda e: e.activation(out=m.t[:, 20:24], in_=m.t[:, 16:20], func=AF.Sqrt),
                         reads=[m], writes=[m])

                def phaseYc(u):
                    c, hg = munits[u]
                    if c < 16:
                        return
                    g, ci = c // 4, c % 4
                    j = g % 2
                    s = u % 2
                    r4 = u % 4
                    m = sm[r4]
                    S.op("dve", lambda e: e.reciprocal(m.t[:, 24:28], m.t[:, 20:24]), reads=[m], writes=[m])
                    S.op("dve", lambda e: e.tensor_tensor(hh[r4].t[:], hh[r4].t[:],
                                                          bass.AP(m.t, 24, [[32, P], [1, 4], [0, P]]), ALU.mult),
                         reads=[hh[r4], m], writes=[hh[r4]])
                    S.op("dve", lambda e: e.tensor_tensor(ot[r4].t[:], hh[r4].t[:],
                                                          bass.AP(mo[j].t, ci * 1024 + hg * 512,
                                                                  [[4096, P], [P, 4], [1, P]]), ALU.mult),
                         reads=[hh[r4], mo[j]], writes=[ot[r4]])
                    for h4 in range(4):
                        S.op("pe", lambda e, h4=h4: e.transpose(FB[s].t[:, 4 + h4, :], ot[r4].t[:, h4, :], ident),
                             reads=[ot[r4], cb], writes=[FB[s]], inc=(h4 == 3))
                    tt = c - 16
                    S.op("act", lambda e: e.copy(out=mixT.t[:, 8 + hg * 4:12 + hg * 4, tt * P:(tt + 1) * P],
                                                 in_=FB[s].t[:, 4:8, :]), reads=[FB[s]], writes=[mix_b[tt]])

                load_grp(0)
                NU = len(munits)
                for idx in range(NU + 5):
                    if idx < NU:
                        phaseV(idx)
                    if 2 <= idx < NU + 2:
                        phaseX(idx - 2)
                    if 3 <= idx < NU + 3:
                        phaseYa(idx - 3)
                    if 4 <= idx < NU + 4:
                        phaseYb(idx - 4)
                    if idx >= 5:
                        phaseYc(idx - 5)
            if dbg:
                S.dma("sp", d_mixT.ap(), bass.AP(mixT.t, 0, [[16 * TO, P], [1, 16 * TO]]), reads=mix_b,
                      writes=[B_dbg], sembuf=mixT)
                S.wait_all("sp", [B_dbg])
            if stop == "mix":
                return nc, dbg_outs, S

            with stage() as st:
                wo = SB(st, "wo", [P, KC, D], BF16)
                wo_b = [Buf(f"wob{i}") for i in range(4)]
                xr = [SB(st, f"xr{i}", [P, D], F32) for i in range(3)]
                xr_st = [Buf(f"xrst{i}") for i in range(3)]
                grow2 = SB(st, "grow2", [P, D], F32)
                xn2 = [SB(st, f"xn2{i}", [P, D], BF16) for i in range(2)]
                junk2 = SB(st, "junk2", [P, D], BF16)
                stt2 = [SB(st, f"stt2{i}", [P, 8], F32) for i in range(3)]
                stg2 = [SB(st, f"stg2{i}", [P, KC, P], BF16) for i in range(2)]
                pso = [PS(st, f"pso{i}", [P, 512]) for i in range(4)]
                tp2 = [PS(st, f"tp2{i}", [P, KC, P], BF16) for i in range(2)]
                S.dma("sp", grow2.t[:], rp_d.ap()[:, RP_G2:RP_G2 + D], writes=[grow2], sembuf=grow2)
                for cg in range(4):
                    S.dma("pool", wo.t[:, :, cg * 512:(cg + 1) * 512], w_src(w_out, D, cg * 512, 512),
                          writes=[wo_b[cg]], sembuf=wo_b[cg])

                def o_load(tt):
                    S.dma("sp", xr[tt % 3].t[:], x_loc.ap()[TC + tt * P:TC + (tt + 1) * P, :], writes=[xr[tt % 3]],
                          sembuf=xr[tt % 3])

                def o_mm(tt, cgs=(0, 1, 2, 3), bank=None):
                    x_ = xr[tt % 3]
                    for cg in cgs:
                        ps = pso[(tt * 4 + cg) % 4 if bank is None else bank(tt, cg)]
                        mm_group(S, ps.t[:], [(mixT.t[:, k, tt * P:(tt + 1) * P], wo.t[:, k, cg * 512:(cg + 1) * 512])
                                              for k in range(KC)], reads=[wo_b[cg], mix_b[tt]], writes=[ps])
                        S.op("dve", lambda e, cg=cg, ps=ps: e.tensor_tensor(x_.t[:, cg * 512:(cg + 1) * 512], ps.t[:],
                                                                            x_.t[:, cg * 512:(cg + 1) * 512], ALU.add),
                             reads=[ps, x_], writes=[x_])

                def o_n1(tt):
                    x_ = xr[tt % 3]
                    m = stt2[tt % 3]
                    S.dma("pool", s_x1.ap()[tt * P:(tt + 1) * P, :], x_.t[:], reads=[x_], writes=[B_x1],
                          sembuf=xr_st[tt % 3])
                    S.op("act", lambda e: e.activation(out=junk2.t[:], in_=x_.t[:], func=AF.Square,
                                                       accum_out=m.t[:, 0:1]), reads=[x_], writes=[junk2, m])
                    S.op("dve", lambda e: e.tensor_scalar(m.t[:, 1:2], m.t[:, 0:1], 1.0 / D, EPS, ALU.mult, ALU.add),
                         reads=[m], writes=[m])
                    S.op("act", lambda e: e.activation(out=m.t[:, 2:3], in_=m.t[:, 1:2], func=AF.Sqrt),
                         reads=[m], writes=[m])
                    S.op("dve", lambda e: e.reciprocal(m.t[:, 3:4], m.t[:, 2:3]), reads=[m], writes=[m])
                    S.op("dve", lambda e: e.scalar_tensor_tensor(xn2[tt % 2].t[:], x_.t[:], m.t[:, 3:4], grow2.t[:],
                                                                 ALU.mult, ALU.mult),
                         reads=[x_, m, grow2], writes=[xn2[tt % 2]])

                def o_n2(tt):
                    xn_, tp_, sg_ = xn2[tt % 2], tp2[tt % 2], stg2[tt % 2]
                    for k in range(KC):
                        S.op("pe", lambda e, k=k: e.transpose(tp_.t[:, k, :], xn_.t[:, k * P:(k + 1) * P], ident),
                             reads=[xn_, cb], writes=[tp_], inc=(k == KC - 1))
                    S.op("act", lambda e: e.copy(out=sg_.t[:], in_=tp_.t[:]), reads=[tp_], writes=[sg_])
                    S.dma("pool", bass.AP(s_h2, tt * P * KC * P, [[KC * P, P], [1, KC * P]]),
                          bass.AP(sg_.t, 0, [[KC * P, P], [1, KC * P]]), reads=[sg_], writes=[B_h2], sembuf=sg_)

                o_load(0)
                o_load(1)
                o_load(2)
                for cg in range(4):
                    o_mm(0, (cg,), bank=lambda t, c: (2 * c + t) % 4)
                    o_mm(1, (cg,), bank=lambda t, c: (2 * c + t) % 4)
                o_n1(0)
                o_n1(1)
                o_n2(0)
                for idx in range(2, 17):
                    if idx + 1 < 16:
                        o_load(idx + 1)
                    if idx < 16:
                        o_mm(idx)
                        o_n1(idx)
                    o_n2(idx - 1)
                S.wait_all("sp", [B_x1, B_h2])
                S.wait_all("pool", [B_x1, B_h2])
        if stop == "x1":
            return nc, dbg_outs, S

        with stage() as st:
            h2T = SB(st, "h2T", [P, KC, 1024], BF16)
            h2_b = [Buf(f"h2b{i}") for i in range(8)]
            h2_b2 = [Buf(f"h2c{i}") for i in range(8)]
            actT = SB(st, "actT", [P, FKC, 1024], BF16)
            wd0 = SB(st, "wd0", [P, FKC, 256], BF16)
            wgu0 = SB(st, "wgu0", [P, 2, KC, 256], BF16)

            def load_gu0():
                S.dma("pool", wgu0.t[:, 0], w_src(w_gate, FH, 0, 256), writes=[wgu0], sembuf=wgu0)
                S.dma("pool", wgu0.t[:, 1], w_src(w_up, FH, 0, 256), writes=[wgu0], sembuf=wgu0)
            act_b = [Buf(f"actb{i}") for i in range(8)]

            def load_h2(half_):
                for t_ in range(8):
                    S.dma("sp", h2T.t[:, :, t_ * P:(t_ + 1) * P],
                          bass.AP(s_h2, (half_ * 8 + t_) * P * KC * P, [[KC * P, P], [P, KC], [1, P]]),
                          reads=[B_h2], writes=[h2_b[t_], h2_b2[t_]], sembuf=h2_b[t_])
            for half in range(2):
                t0 = half * 1024
                if half == 0:
                    load_h2(0)
                with stage() as st2:
                    wgu = [wgu0, SB(st2, "wgu1", [P, 2, KC, 256], BF16)]
                    sgl = [SB(st2, f"sgl{i}", [P, 512], F32) for i in range(2)]
                    pg = [PS(st2, f"pg{i}", [P, 512]) for i in range(3)]
                    pu = [PS(st2, f"pu{i}", [P, 512]) for i in range(3)]

                    def load_gu(gi):
                        j = gi % 2
                        S.dma("pool", wgu[j].t[:, 0], w_src(w_gate, FH, gi * 256, 256), writes=[wgu[j]], sembuf=wgu[j])
                        S.dma("pool", wgu[j].t[:, 1], w_src(w_up, FH, gi * 256, 256), writes=[wgu[j]], sembuf=wgu[j])

                    if half == 0:
                        load_gu0()
                    ui = 0
                    for gi in range(22):
                        if gi + 1 < 22:
                            load_gu(gi + 1)
                        if gi == 17:
                            S.dma("pool", wd0.t[:], w_src(w_down, D, 0, 256, nk=FKC), writes=[wd0], sembuf=wd0)
                        w = wgu[gi % 2]
                        for hc in range(2):
                            fc = gi * 2 + hc
                            for tg in range(2):
                                r = ui % 3
                                ui += 1
                                rds = [w] + h2_b[tg * 4:(tg + 1) * 4] + h2_b2[tg * 4:(tg + 1) * 4]
                                mm_group(S, pg[r].t[:], [(w.t[:, 0, k, hc * P:(hc + 1) * P], h2T.t[:, k, tg * 512:(tg + 1) * 512])
                                                         for k in range(KC)], reads=rds, writes=[pg[r]])
                                mm_group(S, pu[r].t[:], [(w.t[:, 1, k, hc * P:(hc + 1) * P], h2T.t[:, k, tg * 512:(tg + 1) * 512])
                                                         for k in range(KC)], reads=rds, writes=[pu[r]])
                                sg_ = sgl[ui % 2]
                                S.op("act", lambda e: e.activation(out=sg_.t[:], in_=pg[r].t[:], func=AF.Silu),
                                     reads=[pg[r]], writes=[sg_])
                                S.op("dve", lambda e: e.tensor_tensor(actT.t[:, fc, tg * 512:(tg + 1) * 512], sg_.t[:],
                                                                      pu[r].t[:], ALU.mult),
                                     reads=[sg_, pu[r]], writes=act_b[tg * 4:(tg + 1) * 4])
                with stage() as st3:
                    wd = [wd0, SB(st3, "wd1", [P, FKC, 256], BF16)]
                    x1q = [SB(st3, f"x1q{i}", [P, 256], F32) for i in range(3)]
                    oq = [SB(st3, f"oq{i}", [P, 256], F32) for i in range(3)]
                    pd = [PS(st3, f"pd{i}", [P, 512]) for i in range(4)]

                    def load_wd(cg):
                        S.dma("pool", wd[cg % 2].t[:], w_src(w_down, D, cg * 256, 256, nk=FKC), writes=[wd[cg % 2]],
                              sembuf=wd[cg % 2])

                    units = [(cg, tt) for cg in range(8) for tt in range(8)]

                    def load_x1(u):
                        cg, tt = units[u]
                        S.dma("sp", x1q[u % 3].t[:], s_x1.ap()[t0 + tt * P:t0 + (tt + 1) * P, cg * 256:(cg + 1) * 256],
                              reads=[B_x1], writes=[x1q[u % 3]], sembuf=x1q[u % 3])

                    load_x1(0)
                    load_x1(1)
                    for u, (cg, tt) in enumerate(units):
                        if half == 0 and u == 4:
                            load_h2(1)
                        if half == 0 and cg == 6 and tt == 0:
                            load_gu0()
                        if tt == 0 and cg + 1 < 8:
                            load_wd(cg + 1)
                        if u + 2 < len(units):
                            load_x1(u + 2)
                        ps = pd[u % 4]
                        w = wd[cg % 2]
                        mm_group(S, ps.t[:, 0:256], [(actT.t[:, k, tt * P:(tt + 1) * P], w.t[:, k, :]) for k in range(FKC)],
                                 reads=[w, act_b[tt]], writes=[ps])
                        o_ = oq[u % 3]
                        S.op("dve", lambda e: e.tensor_tensor(o_.t[:], ps.t[:, 0:256], x1q[u % 3].t[:], ALU.add),
                             reads=[ps, x1q[u % 3]], writes=[o_])
                        S.dma("pool", out_d.ap()[t0 + tt * P:t0 + (tt + 1) * P, cg * 256:(cg + 1) * 256], o_.t[:],
                              reads=[o_], writes=[B_out], sembuf=o_)
            S.wait_all("sp", [B_out])
            S.wait_all("pool", [B_out])
    return nc, dbg_outs, S


def host_consts(s):
    pj = np.arange(P)
    cbm = np.zeros((P, CB_N), np.float32)
    cbm[:, CB_IDENT:CB_IDENT + P] = np.eye(P)
    cbm[:, CB_ONES:CB_ONES + P] = 1.0
    prev = (pj[:, None] >= pj[None, :]).astype(np.float32)
    cur = (pj[:, None] <= pj[None, :]).astype(np.float32)
    cbm[:, CB_MASKA:CB_MASKA + P] = prev
    cbm[:, CB_MASKA + P:CB_MASKA + 2 * P] = cur
    cbm[:, CB_MASKC:CB_MASKC + P] = prev * (1.0 if s == 1 else 0.0)
    cbm[:, CB_MASKC + P:CB_MASKC + 2 * P] = cur
    cbm[:, CB_MBA:CB_MBA + 2 * P] = (1.0 - cbm[:, CB_MASKA:CB_MASKA + 2 * P]) * MNEG
    cbm[:, CB_MBC:CB_MBC + 2 * P] = (1.0 - cbm[:, CB_MASKC:CB_MASKC + 2 * P]) * MNEG
    return cbm.astype(ml_dtypes.bfloat16)


def make_in_maps(x, norm_mix_g, w_in, conv_w, conv_b, gate_b, q_norm_g, k_norm_g, mlstm_norm_g,
                 w_out, norm_ffn_g, w_gate, w_up, w_down):
    f = lambda a: np.ascontiguousarray(np.asarray(a, dtype=np.float32))
    x = f(x)
    w_in_, w_out_, w_gate_, w_up_, w_down_ = f(w_in[0]), f(w_out[0]), f(w_gate[0]), f(w_up[0]), f(w_down[0])
    pp = np.zeros((P, PP_N), np.float32)
    cw = f(conv_w[0])
    pp[:, PP_CONVW:PP_CONVW + 64] = cw.reshape(4, 16, P).transpose(2, 1, 0).reshape(P, 64)
    pp[:, PP_CONVB:PP_CONVB + 16] = f(conv_b[0]).reshape(16, P).T
    pp[:, PP_QG] = f(q_norm_g[0])
    pp[:, PP_KG] = f(k_norm_g[0])
    pp[:, PP_EPS] = EPS
    pj = np.arange(P)
    pp[:, PP_IDENT:PP_IDENT + P] = np.eye(P)
    pp[:, PP_TRINEG:PP_TRINEG + P] = -(pj[:, None] <= pj[None, :]).astype(np.float32)
    pp[:, PP_NEGONES:PP_NEGONES + P] = -1.0
    pp[:, PP_ONES:PP_ONES + P] = 1.0
    rp = np.zeros((P, RP_N), np.float32)
    rp[:, RP_G1:RP_G1 + D] = f(norm_mix_g[0])[None]
    rp[:, RP_G2:RP_G2 + D] = f(norm_ffn_g[0])[None]
    rp[:, RP_MG:RP_MG + 1024] = f(mlstm_norm_g[0]).reshape(1, 1024)
    rp[:, RP_GB:RP_GB + 16] = f(gate_b[0])[None]
    rp[:, RP_QG:RP_QG + P] = f(q_norm_g[0])[None]
    rp[:, RP_KG:RP_KG + P] = f(k_norm_g[0])[None]
    cbs = [host_consts(0), host_consts(1)]
    in_maps = []
    for core in range(8):
        b, s = core // 2, core % 2
        if s == 1:
            xl = x[b]
        else:
            xl = np.concatenate([np.zeros((TC, D), np.float32), x[b, :TO]], axis=0)
        ppc = pp.copy()
        ppc[:, PP_CBIAS] = 0.0 if s == 1 else MNEG
        in_maps.append({"x_loc": np.ascontiguousarray(xl), "w_in": w_in_, "w_out": w_out_, "w_gate": w_gate_,
                        "w_up": w_up_, "w_down": w_down_, "pp": ppc, "rp": rp, "cb": cbs[s]})
    return in_maps


_NC_CACHE = {}


def kernel(**inputs):
    in_maps = make_in_maps(**inputs)
    if "nc" not in _NC_CACHE:
        _NC_CACHE["nc"] = build()[0]
    nc = _NC_CACHE["nc"]
    res = run_bass_kernel_spmd(nc, in_maps, core_ids=list(range(8)))
    out = np.empty((4, 4096, D), np.float32)
    for core in range(8):
        b, s = core // 2, core % 2
        out[b, s * TO:(s + 1) * TO] = np.asarray(res.results[core]["out"], dtype=np.float32)
    return out
```

```python
import numpy as np
import ml_dtypes
import concourse.bass as bass
import concourse.mybir as mybir
from concourse.bass_utils import run_bass_kernel_spmd

F32 = mybir.dt.float32
BF16 = mybir.dt.bfloat16
AF = mybir.ActivationFunctionType
ALU = mybir.AluOpType
AX = mybir.AxisListType

P = 128
D = 2048
KC = 16
TO = 2048
TC = 2048
T = TO + TC
NH = 8
HD = 128
INW = 7184
FH = 5632
FKC = 44
EPS = 1e-6
MNEG = -30000.0
QS = HD ** -0.5

PP_CONVW = 0
PP_CONVB = 64
PP_QG = 80
PP_KG = 81
PP_CBIAS = 82
PP_EPS = 83
PP_IDENT = 84
PP_TRINEG = 212
PP_NEGONES = 340
PP_ONES = 468
PP_N = 596
RP_G1 = 0
RP_G2 = 2048
RP_MG = 4096
RP_GB = 5120
RP_QG = 5136
RP_KG = 5264
RP_N = 5392
CB_IDENT = 0
CB_ONES = 128
CB_MASKA = 256
CB_MASKC = 512
CB_MBA = 768
CB_MBC = 1024
CB_N = 1280


class Buf:
    def __init__(self, name, t=None, multi=False):
        self.name = name
        self.t = t
        self.w = []
        self.rs = []
        self.multi = multi
        self.dsem = None
        self.dcount = 0


class Eng:
    def __init__(self, name, h, sem):
        self.name = name
        self.h = h
        self.sem = sem
        self.n = 0
        self.seen = {}


class Sync:
    def __init__(self, nc):
        self.nc = nc
        self.eng = {}
        for name, h in (("pe", nc.tensor), ("act", nc.scalar), ("dve", nc.vector),
                        ("pool", nc.gpsimd), ("sp", nc.sync)):
            self.eng[name] = Eng(name, h, nc.alloc_semaphore("prog_" + name))
        self.nsem = 5
        self.ninst = 0
        self.dma_sems = {}

    def _wait(self, E, ev):
        sem, val, who = ev
        k = id(sem)
        if E.seen.get(k, 0) >= val:
            return
        E.h.wait_ge(sem, val)
        E.seen[k] = val
        self.ninst += 1

    def _deps(self, en, reads, writes):
        deps = []
        for b in reads:
            deps += b.w
        for b in writes:
            deps += b.rs
            if not b.multi:
                deps += b.w
        return deps

    def op(self, en, fn, reads=(), writes=(), inc=True):
        E = self.eng[en]
        for b in reads:
            for ev in b.w:
                if ev[2] == en and en == "pe":
                    continue
                self._wait(E, ev)
        for b in writes:
            for ev in b.rs:
                if ev[2] == en:
                    continue
                self._wait(E, ev)
            for ev in b.w:
                if ev[2] == en:
                    continue
                self._wait(E, ev)
        ins = fn(E.h)
        self.ninst += 1
        if inc:
            E.n += 1
            ins.then_inc(E.sem, 1)
            ev = (E.sem, E.n, en)
        else:
            ev = (E.sem, E.n + 1, en)
        for b in reads:
            b.rs = [r for r in b.rs if r[2] != en] + [ev]
        for b in writes:
            b.w = [ev]
            b.rs = []
        return ins

    def dma(self, qn, out, in_, reads=(), writes=(), sembuf=None):
        E = self.eng[qn]
        for ev in self._deps(qn, reads, writes):
            self._wait(E, ev)
        if sembuf.dsem is None:
            sembuf.dsem = self.nc.alloc_semaphore("d_" + sembuf.name)
            sembuf.dq = qn
            self.nsem += 1
        assert sembuf.dq == qn, f"DMA semaphore of {sembuf.name} shared between queues {sembuf.dq} and {qn}"
        sembuf.dcount += 16
        E.h.dma_start(out=out, in_=in_).then_inc(sembuf.dsem, 16)
        self.ninst += 1
        ev = (sembuf.dsem, sembuf.dcount, "dma:" + sembuf.name)
        self.dma_sems[id(sembuf.dsem)] = (sembuf.dsem, sembuf.dcount)
        for b in reads:
            b.rs = b.rs + [ev]
        for b in writes:
            if b.multi:
                b.w = [w for w in b.w if w[0] is not sembuf.dsem] + [ev]
            else:
                b.w = [ev]
                b.rs = []

    def barrier(self):
        engs = list(self.eng.values())
        for E in engs:
            for O in engs:
                if O.n > 0:
                    self._wait(E, (O.sem, O.n, O.name))
            for sem, cnt in self.dma_sems.values():
                self._wait(E, (sem, cnt, "dma"))

    def wait_all(self, qn, bufs):
        E = self.eng[qn]
        for b in bufs:
            for ev in b.w + b.rs:
                self._wait(E, ev)


def mm_group(S, out, pairs, reads, writes):
    n = len(pairs)
    for i, (l, r) in enumerate(pairs):
        S.op("pe", lambda e, l=l, r=r, i=i: e.matmul(out, l, r, start=(i == 0), stop=(i == n - 1)),
             reads=reads, writes=writes, inc=(i == n - 1))


def build(dbg=False, stop=None):
    nc = bass.Bass("TRN2", target_bir_lowering=False)
    S = Sync(nc)

    def din(name, shape, dt=F32):
        return nc.dram_tensor(name, list(shape), dt, kind="ExternalInput")

    x_loc = din("x_loc", [T, D])
    w_in = din("w_in", [D, INW])
    w_out = din("w_out", [D, D])
    w_gate = din("w_gate", [D, FH])
    w_up = din("w_up", [D, FH])
    w_down = din("w_down", [FH, D])
    pp_d = din("pp", [P, PP_N])
    rp_d = din("rp", [P, RP_N])
    cb_d = din("cb", [P, CB_N], BF16)
    out_d = nc.dram_tensor("out", [TO, D], F32, kind="ExternalOutput")

    sk = "ExternalOutput" if dbg else "Internal"
    s_aq = nc.dram_tensor("s_aq", [NH, P, TO], BF16, kind=sk)
    s_ak = nc.dram_tensor("s_ak", [NH, P, T], BF16, kind=sk)
    s_av = nc.dram_tensor("s_av", [T, NH * HD], BF16, kind=sk)
    s_mq = nc.dram_tensor("s_mq", [NH, P, TO], BF16, kind=sk)
    s_mk = nc.dram_tensor("s_mk", [NH, P, T], BF16, kind=sk)
    s_mv = nc.dram_tensor("s_mv", [T, NH * HD], BF16, kind=sk)
    s_mo = nc.dram_tensor("s_mo", [TO, NH * HD], BF16, kind=sk)
    s_x1 = nc.dram_tensor("s_x1", [TO, D], F32, kind=sk)
    s_h2 = nc.dram_tensor("s_h2", [16, P, KC * P], BF16, kind="Internal")
    dbg_outs = ["s_aq", "s_ak", "s_av", "s_mq", "s_mk", "s_mv", "s_mo", "s_x1"]
    if dbg:
        d_gates = nc.dram_tensor("d_gates", [P, 32 * 16], F32, kind="ExternalOutput")
        d_gp = nc.dram_tensor("d_gp", [P, 6 * 256], F32, kind="ExternalOutput")
        d_mixT = nc.dram_tensor("d_mixT", [P, 16 * TO], BF16, kind="ExternalOutput")
        dbg_outs += ["d_gates", "d_gp", "d_mixT"]
    B_aq, B_ak, B_av = Buf("s_aq", multi=True), Buf("s_ak", multi=True), Buf("s_av", multi=True)
    B_mq, B_mk, B_mv = Buf("s_mq", multi=True), Buf("s_mk", multi=True), Buf("s_mv", multi=True)
    B_mo, B_x1, B_out = Buf("s_mo", multi=True), Buf("s_x1", multi=True), Buf("out", multi=True)
    B_dbg = Buf("dbg", multi=True)
    B_h2 = Buf("s_h2", multi=True)

    def sb(name, shape, dt):
        return nc.sbuf_tensor(name, list(shape), dt)

    uid = [0]

    def SB(stack, name, shape, dt):
        uid[0] += 1
        name = f"{name}_{uid[0]}"
        t = stack.enter_context(nc.sbuf_tensor(name, list(shape), dt))
        return Buf(name, t)

    def PS(stack, name, shape, dt=F32):
        uid[0] += 1
        name = f"{name}_{uid[0]}"
        t = stack.enter_context(nc.psum_tensor(name, list(shape), dt))
        return Buf(name, t)

    from contextlib import ExitStack, contextmanager

    @contextmanager
    def stage():
        with ExitStack() as es:
            yield es
            S.barrier()

    with ExitStack() as top:
        pp = SB(top, "pp_sb", [P, PP_N], F32)
        cb = SB(top, "cbc", [P, CB_N], BF16)
        smallp = SB(top, "smallp", [P, 16 + 128 + 128 + 8], F32)
        gates_all = SB(top, "gates_all", [P, 32, 16], F32)
        halo = SB(top, "halo", [P, 16, 3], F32)
        gpU = SB(top, "gpU", [P, 32, 8], F32)
        gpUS = SB(top, "gpUS", [P, 32, 8], F32)
        gpG = SB(top, "gpG", [P, 32, 8], F32)
        gpGS = SB(top, "gpGS", [P, 32, 8], F32)
        gpEL = SB(top, "gpEL", [P, 32, 8], F32)
        S.dma("sp", pp.t[:], pp_d.ap(), writes=[pp], sembuf=pp)
        S.dma("sp", cb.t[:], cb_d.ap(), writes=[cb], sembuf=cb)
        S.dma("sp", smallp.t[:, 0:272], rp_d.ap()[:, RP_GB:RP_GB + 272], writes=[smallp], sembuf=smallp)
        ident = cb.t[:, CB_IDENT:CB_IDENT + 128]
        ones_b = cb.t[:, CB_ONES:CB_ONES + 128]
        maskA = cb.t[:, CB_MASKA:CB_MASKA + 256]
        maskC = cb.t[:, CB_MASKC:CB_MASKC + 256]
        maskLE = cb.t[:, CB_MASKA + 128:CB_MASKA + 256]
        mbA = cb.t[:, CB_MBA:CB_MBA + 256]
        mbC = cb.t[:, CB_MBC:CB_MBC + 256]
        ident_f = pp.t[:, PP_IDENT:PP_IDENT + 128]
        trineg_f = pp.t[:, PP_TRINEG:PP_TRINEG + 128]
        negones_f = pp.t[:, PP_NEGONES:PP_NEGONES + 128]
        ones_f = pp.t[:, PP_ONES:PP_ONES + 128]
        eps_col = pp.t[:, PP_EPS:PP_EPS + 1]
        qgs = smallp.t[:, 272:273]
        negshift = smallp.t[:, 273:274]
        S.op("dve", lambda e: e.tensor_scalar(qgs, pp.t[:, PP_QG:PP_QG + 1], QS, None, ALU.mult),
             reads=[pp], writes=[smallp])
        S.op("dve", lambda e: e.reduce_max(smallp.t[:, 274:275], smallp.t[:, 16:144], axis=AX.X,
                                           apply_absolute_value=True), reads=[smallp], writes=[smallp])
        S.op("dve", lambda e: e.reduce_max(smallp.t[:, 275:276], smallp.t[:, 144:272], axis=AX.X,
                                           apply_absolute_value=True), reads=[smallp], writes=[smallp])
        S.op("dve", lambda e: e.scalar_tensor_tensor(negshift, smallp.t[:, 274:275], -(HD ** 0.5),
                                                     smallp.t[:, 275:276], ALU.mult, ALU.mult),
             reads=[smallp], writes=[smallp])

        def norm_transpose(st, src_row_ap, src_buf, ntiles, grow, dstT, dst_bufs, tag):
            xb = [SB(st, f"xb{tag}{i}", [P, D], F32) for i in range(3)]
            xn = [SB(st, f"xn{tag}{i}", [P, D], BF16) for i in range(3)]
            junk = SB(st, f"junk{tag}", [P, D], BF16)
            stt = [SB(st, f"st{tag}{i}", [P, 8], F32) for i in range(3)]
            tp = [PS(st, f"tp{tag}{i}", [P, KC, P], BF16) for i in range(2)]
            dst_bufs2 = dst_bufs[len(dst_bufs) // 2:]
            dst_bufs = dst_bufs[:len(dst_bufs) // 2]
            def nA(i):
                j = i % 3
                jn = i % 3
                S.dma("sp", xb[j].t[:], src_row_ap(i), reads=[src_buf] if src_buf else [],
                      writes=[xb[j]], sembuf=xb[j])
                S.op("act", lambda e: e.activation(out=junk.t[:], in_=xb[j].t[:], func=AF.Square,
                                                   accum_out=stt[jn].t[:, 0:1]),
                     reads=[xb[j]], writes=[junk, stt[jn]])
                S.op("dve", lambda e: e.tensor_scalar(stt[jn].t[:, 1:2], stt[jn].t[:, 0:1], 1.0 / D, EPS,
                                                      ALU.mult, ALU.add), reads=[stt[jn]], writes=[stt[jn]])
                S.op("act", lambda e: e.activation(out=stt[jn].t[:, 2:3], in_=stt[jn].t[:, 1:2], func=AF.Sqrt),
                     reads=[stt[jn]], writes=[stt[jn]])
                S.op("dve", lambda e: e.reciprocal(stt[jn].t[:, 3:4], stt[jn].t[:, 2:3]),
                     reads=[stt[jn]], writes=[stt[jn]])
                S.op("dve", lambda e: e.scalar_tensor_tensor(xn[jn].t[:], xb[j].t[:], stt[jn].t[:, 3:4],
                                                             grow.t[:], ALU.mult, ALU.mult),
                     reads=[xb[j], stt[jn], grow], writes=[xn[jn]])

            def nB(i):
                jn = i % 3
                jt = i % 2
                for k in range(KC):
                    S.op("pe", lambda e, k=k: e.transpose(tp[jt].t[:, k, :], xn[jn].t[:, k * P:(k + 1) * P], ident),
                         reads=[xn[jn], cb], writes=[tp[jt]], inc=(k == KC - 1))
                S.op("act", lambda e: e.copy(out=dstT.t[:, :, i * P:(i + 1) * P], in_=tp[jt].t[:]),
                     reads=[tp[jt]], writes=[dst_bufs[i], dst_bufs2[i]])

            for i in range(ntiles + 2):
                if i < ntiles:
                    nA(i)
                if i >= 2:
                    nB(i - 2)

        def w_src(wd, ncols_total, col0, ncols, nk=KC, row0=0):
            return bass.AP(wd, row0 * ncols_total + col0, [[ncols_total, P], [P * ncols_total, nk], [1, ncols]])

        with stage() as st:
            hT = SB(st, "hT", [P, KC, 2048], BF16)
            hT_b = [Buf(f"hTb{i}") for i in range(16)]
            hT_b2 = [Buf(f"hTc{i}") for i in range(16)]
            grow1 = SB(st, "grow1", [P, D], F32)
            S.dma("sp", grow1.t[:], rp_d.ap()[:, RP_G1:RP_G1 + D], writes=[grow1], sembuf=grow1)
            wb = [SB(st, f"wb{i}", [P, KC, 512], BF16) for i in range(2)]
            wg16 = SB(st, "wg16", [P, KC, 16], BF16)
            S.dma("pool", wg16.t[:], w_src(w_in, INW, 7168, 16), writes=[wg16], sembuf=wg16)
            stg = [SB(st, f"stg{i}", [P, 2048], BF16) for i in range(2)]
            vst = [SB(st, f"vst{i}", [P, 512], BF16) for i in range(4)]
            cbuf = [SB(st, f"cbuf{i}", [P, 515], F32) for i in range(2)]
            cacc = [SB(st, f"cacc{i}", [P, 512], F32) for i in range(2)]
            sq = [SB(st, f"sq{i}", [P, 512], BF16) for i in range(2)]
            sv = [SB(st, f"sv{i}", [P, 512], F32) for i in range(2)]
            sigf = [SB(st, f"sigf{i}", [P, 512], F32) for i in range(2)]
            mgrow1 = SB(st, "mgrow1", [P, 1024], F32)
            S.dma("sp", mgrow1.t[:], rp_d.ap()[:, RP_MG:RP_MG + 1024], writes=[mgrow1], sembuf=mgrow1)
            S.op("dve", lambda e: e.memset(halo.t[:], 0.0), writes=[halo])
            cnt = {"stg": 0, "vst": 0, "r": 0}

            for is_ctx in (True, False):
                tok0 = 0 if is_ctx else TC
                pre_cols = (1024, 1536) if is_ctx else (0, 512)
                prefetched = stop != "inproj_small"
                if prefetched:
                    for gi_, col_ in enumerate(pre_cols):
                        S.dma("pool", wb[gi_].t[:], w_src(w_in, INW, col_, 512), writes=[wb[gi_]], sembuf=wb[gi_])
                with stage() as st1:
                    norm_transpose(st1, lambda i: x_loc.ap()[tok0 + i * P: tok0 + (i + 1) * P, :], None, 16,
                                   grow1, hT, hT_b + hT_b2, "a")
                with stage() as st2:
                    psA = [PS(st2, f"psA{i}", [P, 512]) for i in range(4)]
                    psB = [PS(st2, f"psB{i}", [P, 512]) for i in range(2)]
                    psG = PS(st2, "psG", [P, 512])
                    groups = []
                    if not is_ctx:
                        groups += [("aq", 0, 0), ("aq", 512, 4)]
                    groups += [("ak", 1024, 0), ("ak", 1536, 4)]
                    groups += [("mq", 3072, 0), ("mq", 3584, 4)]
                    groups += [("mk", 4096, 0), ("mk", 4608, 4)]
                    groups += [("av", 2048, 0), ("av", 2560, 1), ("mv", 5120, 0), ("mv", 5632, 1)]
                    if not is_ctx:
                        groups += [("mo", 6144, 0), ("mo", 6656, 1)]
                    if stop == "inproj_small":
                        groups = groups[:1] + [g for g in groups if g[0] == "av"][:1]

                    def load_w(gi):
                        kind, col0, _ = groups[gi]
                        S.dma("pool", wb[gi % 2].t[:], w_src(w_in, INW, col0, 512), writes=[wb[gi % 2]],
                              sembuf=wb[gi % 2])

                    if not prefetched:
                        load_w(0)
                    psi = 0
                    pending = []
                    for gi, (kind, col0, hb) in enumerate(groups):
                        if gi + 1 < len(groups) and not (prefetched and gi == 0):
                            load_w(gi + 1)
                        w = wb[gi % 2]
                        if kind in ("aq", "ak", "mq", "mk"):
                            for hc in range(4):
                                head = hb + hc
                                halo_only = (kind == "mq" and is_ctx)
                                sg = stg[cnt["stg"] % 2]
                                if not halo_only:
                                    cnt["stg"] += 1
                                hq = head + (8 if kind == "mk" else 0)
                                tgs = [3] if halo_only else [0, 1, 2, 3]
                                for tg in tgs:
                                    ps = psA[psi % 4]
                                    psi += 1
                                    mm_group(S, ps.t[:],
                                             [(w.t[:, k, hc * P:(hc + 1) * P], hT.t[:, k, tg * 512:(tg + 1) * 512])
                                              for k in range(KC)],
                                             reads=[w] + hT_b[tg * 4:(tg + 1) * 4] + hT_b2[tg * 4:(tg + 1) * 4], writes=[ps])
                                    r = cnt["r"] % 2
                                    cnt["r"] += 1
                                    dst = sg.t[:, tg * 512:(tg + 1) * 512]
                                    if kind in ("aq", "ak"):
                                        gcol = qgs if kind == "aq" else pp.t[:, PP_KG:PP_KG + 1]
                                        S.op("act", lambda e: e.activation(out=sq[r].t[:], in_=ps.t[:], func=AF.Square),
                                             reads=[ps], writes=[sq[r]])
                                        for fn_ in pending:
                                            fn_()
                                        pending.clear()

                                        def rest(ps=ps, r=r, dst=dst, gcol=gcol, sg=sg):
                                            S.op("pe", lambda e: e.matmul(psB[r].t[:], ones_b, sq[r].t[:], start=True, stop=True),
                                                 reads=[sq[r], cb], writes=[psB[r]])
                                            S.op("act", lambda e: e.activation(out=sv[r].t[:], in_=psB[r].t[:], func=AF.Ln,
                                                                               bias=eps_col, scale=1.0 / HD),
                                                 reads=[psB[r], pp], writes=[sv[r]])
                                            S.op("act", lambda e: e.activation(out=sv[r].t[:], in_=sv[r].t[:], func=AF.Exp,
                                                                               scale=-0.5),
                                                 reads=[sv[r]], writes=[sv[r]])
                                            S.op("dve", lambda e: e.scalar_tensor_tensor(dst, ps.t[:], gcol, sv[r].t[:],
                                                                                         ALU.mult, ALU.mult),
                                                 reads=[ps, sv[r], smallp, pp], writes=[sg])
                                        pending.append(rest)
                                    else:
                                        cbf_ = cbuf[r]
                                        if tg == tgs[0]:
                                            S.op("dve", lambda e: e.tensor_copy(cbf_.t[:, 0:3], halo.t[:, hq, :]),
                                                 reads=[halo], writes=[cbf_])
                                        S.op("act", lambda e: e.copy(out=cbf_.t[:, 3:515], in_=ps.t[:]),
                                             reads=[ps], writes=[cbf_])
                                        for fn_ in pending:
                                            fn_()
                                        pending.clear()
                                        if halo_only or (is_ctx and tg == 3):
                                            S.op("dve", lambda e: e.tensor_copy(halo.t[:, hq, :], cbf_.t[:, 512:515]),
                                                 reads=[cbf_], writes=[halo])
                                        if halo_only:
                                            continue
                                        if tg < 3:
                                            nxt = cbuf[(r + 1) % 2]
                                            S.op("dve", lambda e: e.tensor_copy(nxt.t[:, 0:3], cbf_.t[:, 512:515]),
                                                 reads=[cbf_], writes=[nxt])
                                        ac = cacc[r]
                                        S.op("dve", lambda e: e.tensor_scalar(ac.t[:], cbf_.t[:, 0:512],
                                                                              pp.t[:, PP_CONVW + hq * 4:PP_CONVW + hq * 4 + 1],
                                                                              None, ALU.mult),
                                             reads=[cbf_, pp], writes=[ac])
                                        for jj in (1, 2, 3):
                                            S.op("dve", lambda e, jj=jj: e.scalar_tensor_tensor(
                                                ac.t[:], cbf_.t[:, jj:jj + 512],
                                                pp.t[:, PP_CONVW + hq * 4 + jj:PP_CONVW + hq * 4 + jj + 1],
                                                ac.t[:], ALU.mult, ALU.add), reads=[cbf_, pp, ac], writes=[ac])
                                        def silu_(dst=dst, ac=ac, hq=hq, sg=sg):
                                            S.op("act", lambda e: e.activation(out=dst, in_=ac.t[:], func=AF.Silu,
                                                                               bias=pp.t[:, PP_CONVB + hq:PP_CONVB + hq + 1]),
                                                 reads=[ac, pp], writes=[sg])
                                        pending.append(silu_)
                                for fn_ in pending:
                                    fn_()
                                pending.clear()
                                if halo_only:
                                    continue
                                sdst, sB = {"aq": (s_aq, B_aq), "ak": (s_ak, B_ak), "mq": (s_mq, B_mq),
                                            "mk": (s_mk, B_mk)}[kind]
                                t0 = tok0 if kind in ("ak", "mk") else 0
                                S.dma("pool", sdst.ap()[head, :, t0:t0 + 2048], sg.t[:], reads=[sg], writes=[sB],
                                      sembuf=sg)
                        else:
                            for tt in range(16):
                                ps = psA[psi % 4]
                                psi += 1
                                mm_group(S, ps.t[:],
                                         [(hT.t[:, k, tt * P:(tt + 1) * P], w.t[:, k, :]) for k in range(KC)],
                                         reads=[w, hT_b[tt], hT_b2[tt]], writes=[ps])
                                vb = vst[cnt["vst"] % 4]
                                cnt["vst"] += 1
                                if kind == "mo":
                                    sf = sigf[cnt["vst"] % 2]
                                    S.op("act", lambda e: e.activation(out=sf.t[:], in_=ps.t[:], func=AF.Sigmoid),
                                         reads=[ps], writes=[sf])
                                    S.op("dve", lambda e: e.tensor_tensor(vb.t[:], sf.t[:],
                                                                          mgrow1.t[:, hb * 512:(hb + 1) * 512], ALU.mult),
                                         reads=[sf, mgrow1], writes=[vb])
                                elif tt % 2 == 0:
                                    S.op("act", lambda e: e.copy(out=vb.t[:], in_=ps.t[:]), reads=[ps], writes=[vb])
                                else:
                                    S.op("dve", lambda e: e.tensor_copy(vb.t[:], ps.t[:]), reads=[ps], writes=[vb])
                                sdst, sB = {"av": (s_av, B_av), "mv": (s_mv, B_mv), "mo": (s_mo, B_mo)}[kind]
                                t0 = 0 if kind == "mo" else tok0
                                S.dma("pool", sdst.ap()[t0 + tt * P:t0 + (tt + 1) * P, hb * 512:(hb + 1) * 512],
                                      vb.t[:], reads=[vb], writes=[sB], sembuf=vb)
                    for tt in range(16):
                        mm_group(S, psG.t[:, 0:16],
                                 [(hT.t[:, k, tt * P:(tt + 1) * P], wg16.t[:, k, :]) for k in range(KC)],
                                 reads=[wg16, hT_b[tt], hT_b2[tt]], writes=[psG])
                        ch = tok0 // P + tt
                        S.op("dve", lambda e: e.tensor_tensor(gates_all.t[:, ch, :], psG.t[:, 0:16],
                                                              smallp.t[:, 0:16], ALU.add),
                             reads=[psG, smallp], writes=[gates_all])
            S.wait_all("sp", [B_aq, B_ak, B_av, B_mq, B_mk, B_mv, B_mo])
            S.wait_all("pool", [B_aq, B_ak, B_av, B_mq, B_mk, B_mv, B_mo])
            if dbg:
                S.dma("sp", d_gates.ap(), bass.AP(gates_all.t, 0, [[512, P], [1, 512]]), reads=[gates_all],
                      writes=[B_dbg], sembuf=gates_all)

        if stop in ("inproj", "inproj_small"):
            S.wait_all("sp", [B_dbg])
            return nc, dbg_outs, S

        def flat(t, n):
            return bass.AP(t, 0, [[n, P], [1, n]])

        with stage() as st:
            ge = SB(st, "ge", [P, 32, 8], F32)
            lsp = SB(st, "lsp", [P, 32, 8], F32)
            gB = SB(st, "gB", [P, 32, 8], F32)
            gBL = SB(st, "gBL", [P, 32, 8], F32)
            gC = SB(st, "gC", [P, 32, 8], F32)
            gMAXC = SB(st, "gMAXC", [P, 32, 8], F32)
            gM = SB(st, "gM", [P, 32, 8], F32)
            gMP = SB(st, "gMP", [P, 33, 8], F32)
            gtmp = SB(st, "gtmp", [P, 32, 8], F32)
            gcol = SB(st, "gcol", [P, 2], F32)
            gdiag = [SB(st, f"gdiag{i}", [P, P], F32) for i in range(2)]
            p1 = PS(st, "gp1", [P, 512])
            p2 = PS(st, "gp2", [P, 512])
            p3 = PS(st, "gp3", [P, 512])
            p4 = PS(st, "gp4", [P, 512])
            S.op("act", lambda e: e.activation(out=ge.t[:], in_=gates_all.t[:, :, 8:16], func=AF.Exp, scale=-1.0),
                 reads=[gates_all], writes=[ge])
            S.op("act", lambda e: e.activation(out=lsp.t[:], in_=ge.t[:], func=AF.Ln, bias=1.0),
                 reads=[ge], writes=[lsp])
            S.op("pe", lambda e: e.matmul(p1.t[:, 0:256], trineg_f, flat(lsp.t, 256), start=True, stop=True),
                 reads=[lsp, pp], writes=[p1])
            S.op("pe", lambda e: e.matmul(p2.t[:, 0:256], negones_f, flat(lsp.t, 256), start=True, stop=True),
                 reads=[lsp, pp], writes=[p2])
            S.op("dve", lambda e: e.tensor_copy(flat(gB.t, 256), p1.t[:, 0:256]), reads=[p1], writes=[gB])
            S.op("dve", lambda e: e.tensor_copy(flat(gBL.t, 256), p2.t[:, 0:256]), reads=[p2], writes=[gBL])
            S.op("dve", lambda e: e.tensor_tensor(gC.t[:], gates_all.t[:, :, 0:8], gB.t[:], ALU.subtract),
                 reads=[gates_all, gB], writes=[gC])
            for hf in range(2):
                S.op("pe", lambda e: e.transpose(p3.t[:, hf * P:(hf + 1) * P], flat(gC.t, 256)[:, hf * P:(hf + 1) * P],
                                                 ident_f), reads=[gC, pp], writes=[p3])
                S.op("dve", lambda e: e.reduce_max(gcol.t[:, hf:hf + 1], p3.t[:, hf * P:(hf + 1) * P], axis=AX.X),
                     reads=[p3], writes=[gcol])
                S.op("dve", lambda e: e.tensor_scalar(gdiag[hf].t[:], ident_f, gcol.t[:, hf:hf + 1], None, ALU.mult),
                     reads=[gcol, pp], writes=[gdiag[hf]])
                S.op("pe", lambda e: e.matmul(p4.t[:, hf * P:(hf + 1) * P], ones_f, gdiag[hf].t[:], start=True, stop=True),
                     reads=[gdiag[hf], pp], writes=[p4])
            S.op("dve", lambda e: e.tensor_copy(flat(gMAXC.t, 256), p4.t[:, 0:256]), reads=[p4], writes=[gMAXC])
            S.op("dve", lambda e: e.memset(gMP.t[:, 0, :], MNEG), writes=[gMP])
            for c in range(32):
                S.op("dve", lambda e: e.tensor_tensor(gM.t[:, c, :], gMP.t[:, c, :], gMAXC.t[:, c, :], ALU.max),
                     reads=[gMP, gMAXC], writes=[gM])
                S.op("dve", lambda e: e.tensor_tensor(gMP.t[:, c + 1, :], gBL.t[:, c, :], gM.t[:, c, :], ALU.add),
                     reads=[gBL, gM], writes=[gMP])
                if c == 15:
                    S.op("dve", lambda e: e.tensor_scalar(gMP.t[:, 16, :], gMP.t[:, 16, :],
                                                          pp.t[:, PP_CBIAS:PP_CBIAS + 1], None, ALU.add),
                         reads=[gMP, pp], writes=[gMP])
            S.op("dve", lambda e: e.tensor_tensor(gtmp.t[:], gMP.t[:, 0:32, :], gM.t[:], ALU.subtract),
                 reads=[gMP, gM], writes=[gtmp])
            S.op("act", lambda e: e.activation(out=gpG.t[:], in_=gtmp.t[:], func=AF.Exp), reads=[gtmp], writes=[gpG])
            S.op("dve", lambda e: e.tensor_scalar(gpGS.t[:], gpG.t[:], QS, None, ALU.mult), reads=[gpG], writes=[gpGS])
            S.op("dve", lambda e: e.tensor_tensor(gtmp.t[:], gC.t[:], gM.t[:], ALU.subtract),
                 reads=[gC, gM, gpG], writes=[gtmp])
            S.op("act", lambda e: e.activation(out=gpU.t[:], in_=gtmp.t[:], func=AF.Exp), reads=[gtmp], writes=[gpU])
            S.op("dve", lambda e: e.tensor_scalar(gpUS.t[:], gpU.t[:], QS, None, ALU.mult), reads=[gpU], writes=[gpUS])
            S.op("dve", lambda e: e.tensor_tensor(gtmp.t[:], gB.t[:], gM.t[:], ALU.add),
                 reads=[gB, gM, gpU], writes=[gtmp])
            S.op("act", lambda e: e.activation(out=gpEL.t[:], in_=gtmp.t[:], func=AF.Exp, scale=-1.0),
                 reads=[gtmp], writes=[gpEL])
            S.op("dve", lambda e: e.tensor_scalar(gpEL.t[:], gpEL.t[:], 1.0 / QS, None, ALU.mult),
                 reads=[gpEL], writes=[gpEL])
            if dbg:
                for i, bsrc in enumerate([gpU, gpG, gpEL, gM, gB, gC]):
                    S.dma("sp", d_gp.ap()[:, i * 256:(i + 1) * 256], flat(bsrc.t, 256), reads=[bsrc],
                          writes=[B_dbg], sembuf=bsrc)
                S.wait_all("sp", [B_dbg])
        if stop == "gates":
            return nc, dbg_outs, S

        with stage() as mixst:
            mixT = SB(mixst, "mixT", [P, 16, TO], BF16)
            mix_b = [Buf(f"mixb{i}") for i in range(16)]
            with stage() as st:
                qT = [SB(st, f"aqT{i}", [P, TO], BF16) for i in range(2)]
                kT = [SB(st, f"akT{i}", [P, T], BF16) for i in range(2)]
                v1 = [SB(st, f"av1{i}", [P, 17, P], BF16) for i in range(2)]
                v2 = [SB(st, f"av2{i}", [P, 4, 5, P], BF16) for i in range(2)]
                v3 = [SB(st, f"av3{i}", [P, 16, 2, P], BF16) for i in range(2)]
                accs = [SB(st, f"aacc{i}", [P, 2, TO], F32) for i in range(2)]
                pT = [SB(st, f"apT{i}", [P, 256], BF16) for i in range(4)]
                sp_ = [PS(st, f"asp{i}", [P, 512]) for i in range(3)]
                po_ = [PS(st, f"apo{i}", [P, 512]) for i in range(3)]

                def load_head(h):
                    j = h % 2
                    S.dma("sp", qT[j].t[:], s_aq.ap()[h], reads=[B_aq], writes=[qT[j]], sembuf=qT[j])
                    S.dma("sp", kT[j].t[:], s_ak.ap()[h], reads=[B_ak], writes=[kT[j]], sembuf=kT[j])
                    S.dma("sp", v1[j].t[:], bass.AP(s_av, (TC - P) * 1024 + h * P, [[1024, P], [P * 1024, 17], [1, P]]),
                          reads=[B_av], writes=[v1[j]], sembuf=v1[j])
                    S.dma("sp", v2[j].t[:], bass.AP(s_av, (4 * P * 3) * 1024 + h * P,
                                                    [[4 * 1024, P], [1024, 4], [512 * 1024, 5], [1, P]]),
                          reads=[B_av], writes=[v2[j]], sembuf=v2[j])
                    for kb in range(2):
                        S.dma("sp", v3[j].t[:, :, kb, :], bass.AP(s_av, kb * 2048 * 1024 + h * P,
                                                                  [[16 * 1024, P], [1024, 16], [1, P]]),
                              reads=[B_av], writes=[v3[j]], sembuf=v3[j])

                units = []
                for h in range(NH):
                    for pi, dil in enumerate((1, 4, 16)):
                        for r in range(dil):
                            for qb in range(16 // dil):
                                units.append((h, pi, dil, r, qb))

                def phaseA(u):
                    h, pi, dil, r, qb = units[u]
                    j = h % 2
                    if (pi, r, qb) == (0, 0, 3) and h + 1 < NH:
                        load_head(h + 1)
                    q0 = r + dil * P * qb
                    qsl = slice(q0, q0 + dil * (P - 1) + 1, dil)
                    kc0 = TC + q0
                    kp0 = kc0 - dil * P
                    ksl_c = slice(kc0, kc0 + dil * (P - 1) + 1, dil)
                    ksl_p = slice(kp0, kp0 + dil * (P - 1) + 1, dil)
                    sp, pt = sp_[u % 3], pT[u % 4]
                    mb = mbC if qb == 0 else mbA
                    S.op("pe", lambda e: e.matmul(sp.t[:, 0:256], ident, mb, start=True, stop=False),
                         reads=[cb], writes=[sp], inc=False)
                    S.op("pe", lambda e: e.matmul(sp.t[:, 0:P], kT[j].t[:, ksl_p], qT[j].t[:, qsl],
                                                  start=False, stop=False),
                         reads=[kT[j], qT[j]], writes=[sp], inc=False)
                    S.op("pe", lambda e: e.matmul(sp.t[:, P:2 * P], kT[j].t[:, ksl_c], qT[j].t[:, qsl],
                                                  start=False, stop=True),
                         reads=[kT[j], qT[j]], writes=[sp])
                    S.op("act", lambda e: e.activation(out=pt.t[:], in_=sp.t[:, 0:256], func=AF.Exp,
                                                       bias=negshift), reads=[sp, smallp], writes=[pt])

                def phaseB(u):
                    h, pi, dil, r, qb = units[u]
                    j = h % 2
                    acc = accs[j]
                    q0 = r + dil * P * qb
                    if pi == 0:
                        vp, vc, vbuf = v1[j].t[:, qb, :], v1[j].t[:, qb + 1, :], v1[j]
                    elif pi == 1:
                        vp, vc, vbuf = v2[j].t[:, r, qb, :], v2[j].t[:, r, qb + 1, :], v2[j]
                    else:
                        vp, vc, vbuf = v3[j].t[:, r, 0, :], v3[j].t[:, r, 1, :], v3[j]
                    po, pt = po_[u % 3], pT[u % 4]
                    S.op("pe", lambda e: e.matmul(po.t[:, 0:P], vp, pt.t[:, 0:P], start=True, stop=False),
                         reads=[pt, vbuf], writes=[po], inc=False)
                    S.op("pe", lambda e: e.matmul(po.t[:, 0:P], vc, pt.t[:, P:2 * P], start=False, stop=True),
                         reads=[pt, vbuf], writes=[po], inc=False)
                    S.op("pe", lambda e: e.matmul(po.t[:, P:2 * P], ones_b, pt.t[:, 0:P], start=True, stop=False),
                         reads=[pt, cb], writes=[po], inc=False)
                    S.op("pe", lambda e: e.matmul(po.t[:, P:2 * P], ones_b, pt.t[:, P:2 * P], start=False, stop=True),
                         reads=[pt, cb], writes=[po])
                    accv = bass.AP(acc.t, q0, [[2 * TO, P], [TO, 2], [dil, P]])
                    pov = bass.AP(po.t, 0, [[512, P], [P, 2], [1, P]])
                    if pi == 0:
                        S.op("act", lambda e: e.copy(out=accv, in_=pov), reads=[po], writes=[acc])
                    else:
                        S.op("dve", lambda e: e.tensor_tensor(accv, accv, pov, ALU.add),
                             reads=[po, acc], writes=[acc])
                    if (pi, r, qb) == (2, 15, 0):
                        S.op("act", lambda e: e.activation(out=acc.t[:, 1, :], in_=acc.t[:, 1, :], func=AF.Ln),
                             reads=[acc], writes=[acc])
                        S.op("act", lambda e: e.activation(out=acc.t[:, 1, :], in_=acc.t[:, 1, :], func=AF.Exp, scale=-1.0),
                             reads=[acc], writes=[acc])
                        S.op("dve", lambda e: e.tensor_tensor(mixT.t[:, h, :], acc.t[:, 0, :], acc.t[:, 1, :], ALU.mult),
                             reads=[acc], writes=mix_b)

                load_head(0)
                SK = 3
                for idx in range(len(units) + SK):
                    if idx < len(units):
                        phaseA(idx)
                    if idx >= SK:
                        phaseB(idx - SK)
            if stop == "attn":
                if dbg:
                    S.dma("sp", d_mixT.ap(), bass.AP(mixT.t, 0, [[16 * TO, P], [1, 16 * TO]]), reads=mix_b,
                          writes=[B_dbg], sembuf=mixT)
                    S.wait_all("sp", [B_dbg])
                return nc, dbg_outs, S

            with stage() as st:
                kk = [SB(st, f"mkk{i}", [P, NH, 512], BF16) for i in range(2)]
                qq = [SB(st, f"mqq{i}", [P, NH, 512], BF16) for i in range(2)]
                vv = [SB(st, f"mvv{i}", [P, 4, NH, 132], BF16) for i in range(2)]
                mo = [SB(st, f"mmo{i}", [P, 4, 1024], BF16) for i in range(2)]
                Cst = [SB(st, f"mCst{g}", [P, 4, 132], F32) for g in range(2)]
                CGf = [SB(st, f"mCG{i}", [P, 4, 132], F32) for i in range(2)]
                cbf = [SB(st, f"mcbf{i}", [P, 4, 132], BF16) for i in range(2)]
                vu = [SB(st, f"mvu{i}", [P, 4, 132], BF16) for i in range(4)]
                wt = [SB(st, f"mwt{i}", [P, 4, P], BF16) for i in range(2)]
                ktok = [SB(st, f"mktok{i}", [P, 4, P], BF16) for i in range(2)]
                hh = [SB(st, f"mhh{i}", [P, 4, P], F32) for i in range(4)]
                sqj = [SB(st, f"msqj{i}", [P, 4, P], F32) for i in range(4)]
                ot = [SB(st, f"mot{i}", [P, 4, P], BF16) for i in range(4)]
                sm = [SB(st, f"msm{i}", [P, 32], F32) for i in range(4)]
                FA = [PS(st, f"mFA{i}", [P, 4, P]) for i in range(2)]
                FC = [PS(st, f"mFC{i}", [P, 4, P]) for i in range(2)]
                FD = [PS(st, f"mFD{i}", [P, 512]) for i in range(2)]
                FB = [PS(st, f"mFB{i}", [P, 8, P], BF16) for i in range(2)]
                for g in range(2):
                    S.op("dve", lambda e, g=g: e.memset(Cst[g].t[:], 0.0), writes=[Cst[g]])
                for i in range(2):
                    S.op("dve", lambda e, i=i: e.memset(vv[i].t[:, :, :, 128:129], 1.0), writes=[vv[i]])

                def load_grp(g):
                    j = g % 2
                    S.dma("sp", kk[j].t[:], bass.AP(s_mk, g * 512, [[T, P], [P * T, NH], [1, 512]]),
                          reads=[B_mk], writes=[kk[j]], sembuf=kk[j])
                    for ci_ in range(4):
                        S.dma("sp", vv[j].t[:, ci_, :, 0:P],
                              bass.AP(s_mv, (g * 512 + ci_ * P) * 1024, [[1024, P], [P, NH], [1, P]]),
                              reads=[B_mv], writes=[vv[j]], sembuf=vv[j])
                    if g >= 4:
                        S.dma("sp", qq[j].t[:], bass.AP(s_mq, (g - 4) * 512, [[TO, P], [P * TO, NH], [1, 512]]),
                              reads=[B_mq], writes=[qq[j]], sembuf=qq[j])

                def load_mo(g):
                    j = g % 2
                    S.dma("sp", mo[j].t[:], bass.AP(s_mo, (g - 4) * 512 * 1024, [[1024, P], [P * 1024, 4], [1, 1024]]),
                          reads=[B_mo], writes=[mo[j]], sembuf=mo[j])

                munits = [(c, hg) for c in range(32) for hg in range(2)]
                mask_bc = bass.AP(cb.t, CB_MASKA + P, [[CB_N, P], [0, 4], [1, P]])

                def bc4(t, off, n):
                    return bass.AP(t, off, [[256, P], [1, 4], [0, n]])

                def phaseV(u):
                    c, hg = munits[u]
                    g, ci = c // 4, c % 4
                    j = g % 2
                    hs = slice(hg * 4, hg * 4 + 4)
                    if ci == 1 and hg == 1 and g + 1 < 8:
                        load_grp(g + 1)
                    if ci == 3 and hg == 0 and 4 <= g + 1 < 8:
                        load_mo(g + 1)
                    S.op("pool", lambda e: e.tensor_tensor(vu[u % 4].t[:, :, 0:129], vv[j].t[:, ci, hs, 0:129],
                                                           bc4(gpU.t, c * 8 + hg * 4, 129), ALU.mult),
                         reads=[vv[j], gpU], writes=[vu[u % 4]])

                def phaseX(u):
                    c, hg = munits[u]
                    g, ci = c // 4, c % 4
                    j = g % 2
                    s = u % 2
                    vu_ = vu[u % 4]
                    own = c >= 16
                    csl = slice(ci * P, (ci + 1) * P)
                    hs = slice(hg * 4, hg * 4 + 4)
                    if own:
                        for h4 in range(4):
                            S.op("pe", lambda e, h4=h4: e.matmul(FA[s].t[:, h4, :], kk[j].t[:, hg * 4 + h4, csl],
                                                                 qq[j].t[:, hg * 4 + h4, csl], start=True, stop=True),
                                 reads=[kk[j], qq[j]], writes=[FA[s]], inc=(h4 == 3))
                        S.op("dve", lambda e: e.tensor_tensor(wt[s].t[:], FA[s].t[:], mask_bc, ALU.mult),
                             reads=[FA[s], cb], writes=[wt[s]])
                    if c < 31:
                        for h4 in range(4):
                            S.op("pe", lambda e, h4=h4: e.transpose(FB[s].t[:, h4, :], kk[j].t[:, hg * 4 + h4, csl], ident),
                                 reads=[kk[j], cb], writes=[FB[s]], inc=(h4 == 3))
                        S.op("act", lambda e: e.copy(out=ktok[s].t[:], in_=FB[s].t[:, 0:4, :]),
                             reads=[FB[s]], writes=[ktok[s]])
                        for h4 in range(4):
                            S.op("pe", lambda e, h4=h4: e.matmul(FC[s].t[:, h4, :], ktok[s].t[:, h4, :],
                                                                 vu_.t[:, h4, 0:P], start=True, stop=True),
                                 reads=[ktok[s], vu_], writes=[FC[s]], inc=False)
                            S.op("pe", lambda e, h4=h4: e.matmul(FD[s].t[:, 8 + h4:9 + h4], ktok[s].t[:, h4, :],
                                                                 vu_.t[:, h4, P:P + 1], start=True, stop=True),
                                 reads=[ktok[s], vu_], writes=[FD[s]], inc=(h4 == 3))
                    S.op("dve", lambda e: e.tensor_tensor(CGf[s].t[:, :, 0:129], Cst[hg].t[:, :, 0:129],
                                                          bc4(gpG.t, c * 8 + hg * 4, 129), ALU.mult),
                         reads=[Cst[hg], gpG], writes=[CGf[s]])
                    if own:
                        S.op("act", lambda e: e.copy(out=cbf[s].t[:, :, 0:129], in_=CGf[s].t[:, :, 0:129]),
                             reads=[CGf[s]], writes=[cbf[s]])
                    if c < 31:
                        S.op("dve", lambda e: e.tensor_tensor(Cst[hg].t[:, :, 0:P], CGf[s].t[:, :, 0:P], FC[s].t[:],
                                                              ALU.add),
                             reads=[CGf[s], FC[s]], writes=[Cst[hg]])
                        S.op("dve", lambda e: e.tensor_tensor(Cst[hg].t[:, :, P:P + 1], CGf[s].t[:, :, P:P + 1],
                                                              bass.AP(FD[s].t, 8, [[512, P], [1, 4], [1, 1]]), ALU.add),
                             reads=[CGf[s], FD[s]], writes=[Cst[hg]])
                    if own:
                        for h4 in range(4):
                            h = hg * 4 + h4
                            S.op("pe", lambda e, h4=h4: e.matmul(FA[s].t[:, h4, :], wt[s].t[:, h4, :], vu_.t[:, h4, 0:P],
                                                                 start=True, stop=False),
                                 reads=[wt[s], vu_], writes=[FA[s]], inc=False)
                            S.op("pe", lambda e, h4=h4, h=h: e.matmul(FA[s].t[:, h4, :], qq[j].t[:, h, csl],
                                                                      cbf[s].t[:, h4, 0:P], start=False, stop=True),
                                 reads=[qq[j], cbf[s]], writes=[FA[s]], inc=False)
                            S.op("pe", lambda e, h4=h4: e.matmul(FD[s].t[:, h4:h4 + 1], wt[s].t[:, h4, :],
                                                                 vu_.t[:, h4, P:P + 1], start=True, stop=False),
                                 reads=[wt[s], vu_], writes=[FD[s]], inc=False)
                            S.op("pe", lambda e, h4=h4, h=h: e.matmul(FD[s].t[:, h4:h4 + 1], qq[j].t[:, h, csl],
                                                                      cbf[s].t[:, h4, P:P + 1], start=False, stop=True),
                                 reads=[qq[j], cbf[s]], writes=[FD[s]], inc=(h4 == 3))

                def phaseYa(u):
                    c, hg = munits[u]
                    if c < 16:
                        return
                    s = u % 2
                    r4 = u % 4
                    m = sm[r4]
                    S.op("dve", lambda e: e.tensor_copy(m.t[:, 0:4], FD[s].t[:, 0:4]), reads=[FD[s]], writes=[m])
                    S.op("dve", lambda e: e.scalar_tensor_tensor(m.t[:, 4:8], m.t[:, 0:4], -1.0, m.t[:, 0:4],
                                                                 ALU.mult, ALU.max), reads=[m], writes=[m])
                    S.op("dve", lambda e: e.tensor_tensor(m.t[:, 4:8], m.t[:, 4:8], gpEL.t[:, c, hg * 4:hg * 4 + 4],
                                                          ALU.max), reads=[m, gpEL], writes=[m])
                    S.op("dve", lambda e: e.reciprocal(m.t[:, 8:12], m.t[:, 4:8]), reads=[m], writes=[m])
                    S.op("dve", lambda e: e.tensor_tensor(hh[r4].t[:], FA[s].t[:],
                                                          bass.AP(m.t, 8, [[32, P], [1, 4], [0, P]]), ALU.mult),
                         reads=[FA[s], m], writes=[hh[r4]])
                    S.op("act", lambda e: e.activation(out=sqj[r4].t[:], in_=hh[r4].t[:], func=AF.Square),
                         reads=[hh[r4]], writes=[sqj[r4]])

                def phaseYb(u):
                    c, hg = munits[u]
                    if c < 16:
                        return
                    r4 = u % 4
                    m = sm[r4]
                    S.op("dve", lambda e: e.reduce_sum(m.t[:, 12:16], sqj[r4].t[:], axis=AX.X),
                         reads=[sqj[r4], m], writes=[m])
                    S.op("dve", lambda e: e.tensor_scalar(m.t[:, 16:20], m.t[:, 12:16], 1.0 / HD, EPS, ALU.mult, ALU.add),
                         reads=[m], writes=[m])
                    S.op("act", lambda e: e.activation(out=m.t[:, 20:24], in_=m.t[:, 16:20], func=AF.Sqrt),
                         reads=[m], writes=[m])

                def phaseYc(u):
                    c, hg = munits[u]
                    if c < 16:
                        return
                    g, ci = c // 4, c % 4
                    j = g % 2
                    s = u % 2
                    r4 = u % 4
                    m = sm[r4]
                    S.op("dve", lambda e: e.reciprocal(m.t[:, 24:28], m.t[:, 20:24]), reads=[m], writes=[m])
                    S.op("dve", lambda e: e.tensor_tensor(hh[r4].t[:], hh[r4].t[:],
                                                          bass.AP(m.t, 24, [[32, P], [1, 4], [0, P]]), ALU.mult),
                         reads=[hh[r4], m], writes=[hh[r4]])
                    S.op("dve", lambda e: e.tensor_tensor(ot[r4].t[:], hh[r4].t[:],
                                                          bass.AP(mo[j].t, ci * 1024 + hg * 512,
                                                                  [[4096, P], [P, 4], [1, P]]), ALU.mult),
                         reads=[hh[r4], mo[j]], writes=[ot[r4]])
                    for h4 in range(4):
                        S.op("pe", lambda e, h4=h4: e.transpose(FB[s].t[:, 4 + h4, :], ot[r4].t[:, h4, :], ident),
                             reads=[ot[r4], cb], writes=[FB[s]], inc=(h4 == 3))
                    tt = c - 16
                    S.op("act", lambda e: e.copy(out=mixT.t[:, 8 + hg * 4:12 + hg * 4, tt * P:(tt + 1) * P],
                                                 in_=FB[s].t[:, 4:8, :]), reads=[FB[s]], writes=[mix_b[tt]])

                load_grp(0)
                NU = len(munits)
                for idx in range(NU + 5):
                    if idx < NU:
                        phaseV(idx)
                    if 2 <= idx < NU + 2:
                        phaseX(idx - 2)
                    if 3 <= idx < NU + 3:
                        phaseYa(idx - 3)
                    if 4 <= idx < NU + 4:
                        phaseYb(idx - 4)
                    if idx >= 5:
                        phaseYc(idx - 5)
            if dbg:
                S.dma("sp", d_mixT.ap(), bass.AP(mixT.t, 0, [[16 * TO, P], [1, 16 * TO]]), reads=mix_b,
                      writes=[B_dbg], sembuf=mixT)
                S.wait_all("sp", [B_dbg])
            if stop == "mix":
                return nc, dbg_outs, S

            with stage() as st:
                wo = SB(st, "wo", [P, KC, D], BF16)
                wo_b = [Buf(f"wob{i}") for i in range(4)]
                xr = [SB(st, f"xr{i}", [P, D], F32) for i in range(3)]
                xr_st = [Buf(f"xrst{i}") for i in range(3)]
                grow2 = SB(st, "grow2", [P, D], F32)
                xn2 = [SB(st, f"xn2{i}", [P, D], BF16) for i in range(2)]
                junk2 = SB(st, "junk2", [P, D], BF16)
                stt2 = [SB(st, f"stt2{i}", [P, 8], F32) for i in range(3)]
                stg2 = [SB(st, f"stg2{i}", [P, KC, P], BF16) for i in range(2)]
                pso = [PS(st, f"pso{i}", [P, 512]) for i in range(4)]
                tp2 = [PS(st, f"tp2{i}", [P, KC, P], BF16) for i in range(2)]
                S.dma("sp", grow2.t[:], rp_d.ap()[:, RP_G2:RP_G2 + D], writes=[grow2], sembuf=grow2)
                for cg in range(4):
                    S.dma("pool", wo.t[:, :, cg * 512:(cg + 1) * 512], w_src(w_out, D, cg * 512, 512),
                          writes=[wo_b[cg]], sembuf=wo_b[cg])

                def o_load(tt):
                    S.dma("sp", xr[tt % 3].t[:], x_loc.ap()[TC + tt * P:TC + (tt + 1) * P, :], writes=[xr[tt % 3]],
                          sembuf=xr[tt % 3])

                def o_mm(tt, cgs=(0, 1, 2, 3), bank=None):
                    x_ = xr[tt % 3]
                    for cg in cgs:
                        ps = pso[(tt * 4 + cg) % 4 if bank is None else bank(tt, cg)]
                        mm_group(S, ps.t[:], [(mixT.t[:, k, tt * P:(tt + 1) * P], wo.t[:, k, cg * 512:(cg + 1) * 512])
                                              for k in range(KC)], reads=[wo_b[cg], mix_b[tt]], writes=[ps])
                        S.op("dve", lambda e, cg=cg, ps=ps: e.tensor_tensor(x_.t[:, cg * 512:(cg + 1) * 512], ps.t[:],
                                                                            x_.t[:, cg * 512:(cg + 1) * 512], ALU.add),
                             reads=[ps, x_], writes=[x_])

                def o_n1(tt):
                    x_ = xr[tt % 3]
                    m = stt2[tt % 3]
                    S.dma("pool", s_x1.ap()[tt * P:(tt + 1) * P, :], x_.t[:], reads=[x_], writes=[B_x1],
                          sembuf=xr_st[tt % 3])
                    S.op("act", lambda e: e.activation(out=junk2.t[:], in_=x_.t[:], func=AF.Square,
                                                       accum_out=m.t[:, 0:1]), reads=[x_], writes=[junk2, m])
                    S.op("dve", lambda e: e.tensor_scalar(m.t[:, 1:2], m.t[:, 0:1], 1.0 / D, EPS, ALU.mult, ALU.add),
                         reads=[m], writes=[m])
                    S.op("act", lambda e: e.activation(out=m.t[:, 2:3], in_=m.t[:, 1:2], func=AF.Sqrt),
                         reads=[m], writes=[m])
                    S.op("dve", lambda e: e.reciprocal(m.t[:, 3:4], m.t[:, 2:3]), reads=[m], writes=[m])
                    S.op("dve", lambda e: e.scalar_tensor_tensor(xn2[tt % 2].t[:], x_.t[:], m.t[:, 3:4], grow2.t[:],
                                                                 ALU.mult, ALU.mult),
                         reads=[x_, m, grow2], writes=[xn2[tt % 2]])

                def o_n2(tt):
                    xn_, tp_, sg_ = xn2[tt % 2], tp2[tt % 2], stg2[tt % 2]
                    for k in range(KC):
                        S.op("pe", lambda e, k=k: e.transpose(tp_.t[:, k, :], xn_.t[:, k * P:(k + 1) * P], ident),
                             reads=[xn_, cb], writes=[tp_], inc=(k == KC - 1))
                    S.op("act", lambda e: e.copy(out=sg_.t[:], in_=tp_.t[:]), reads=[tp_], writes=[sg_])
                    S.dma("pool", bass.AP(s_h2, tt * P * KC * P, [[KC * P, P], [1, KC * P]]),
                          bass.AP(sg_.t, 0, [[KC * P, P], [1, KC * P]]), reads=[sg_], writes=[B_h2], sembuf=sg_)

                o_load(0)
                o_load(1)
                o_load(2)
                for cg in range(4):
                    o_mm(0, (cg,), bank=lambda t, c: (2 * c + t) % 4)
                    o_mm(1, (cg,), bank=lambda t, c: (2 * c + t) % 4)
                o_n1(0)
                o_n1(1)
                o_n2(0)
                for idx in range(2, 17):
                    if idx + 1 < 16:
                        o_load(idx + 1)
                    if idx < 16:
                        o_mm(idx)
                        o_n1(idx)
                    o_n2(idx - 1)
                S.wait_all("sp", [B_x1, B_h2])
                S.wait_all("pool", [B_x1, B_h2])
        if stop == "x1":
            return nc, dbg_outs, S

        with stage() as st:
            h2T = SB(st, "h2T", [P, KC, 1024], BF16)
            h2_b = [Buf(f"h2b{i}") for i in range(8)]
            h2_b2 = [Buf(f"h2c{i}") for i in range(8)]
            actT = SB(st, "actT", [P, FKC, 1024], BF16)
            wd0 = SB(st, "wd0", [P, FKC, 256], BF16)
            wgu0 = SB(st, "wgu0", [P, 2, KC, 256], BF16)

            def load_gu0():
                S.dma("pool", wgu0.t[:, 0], w_src(w_gate, FH, 0, 256), writes=[wgu0], sembuf=wgu0)
                S.dma("pool", wgu0.t[:, 1], w_src(w_up, FH, 0, 256), writes=[wgu0], sembuf=wgu0)
            act_b = [Buf(f"actb{i}") for i in range(8)]

            def load_h2(half_):
                for t_ in range(8):
                    S.dma("sp", h2T.t[:, :, t_ * P:(t_ + 1) * P],
                          bass.AP(s_h2, (half_ * 8 + t_) * P * KC * P, [[KC * P, P], [P, KC], [1, P]]),
                          reads=[B_h2], writes=[h2_b[t_], h2_b2[t_]], sembuf=h2_b[t_])
            for half in range(2):
                t0 = half * 1024
                if half == 0:
                    load_h2(0)
                with stage() as st2:
                    wgu = [wgu0, SB(st2, "wgu1", [P, 2, KC, 256], BF16)]
                    sgl = [SB(st2, f"sgl{i}", [P, 512], F32) for i in range(2)]
                    pg = [PS(st2, f"pg{i}", [P, 512]) for i in range(3)]
                    pu = [PS(st2, f"pu{i}", [P, 512]) for i in range(3)]

                    def load_gu(gi):
                        j = gi % 2
                        S.dma("pool", wgu[j].t[:, 0], w_src(w_gate, FH, gi * 256, 256), writes=[wgu[j]], sembuf=wgu[j])
                        S.dma("pool", wgu[j].t[:, 1], w_src(w_up, FH, gi * 256, 256), writes=[wgu[j]], sembuf=wgu[j])

                    if half == 0:
                        load_gu0()
                    ui = 0
                    for gi in range(22):
                        if gi + 1 < 22:
                            load_gu(gi + 1)
                        if gi == 17:
                            S.dma("pool", wd0.t[:], w_src(w_down, D, 0, 256, nk=FKC), writes=[wd0], sembuf=wd0)
                        w = wgu[gi % 2]
                        for hc in range(2):
                            fc = gi * 2 + hc
                            for tg in range(2):
                                r = ui % 3
                                ui += 1
                                rds = [w] + h2_b[tg * 4:(tg + 1) * 4] + h2_b2[tg * 4:(tg + 1) * 4]
                                mm_group(S, pg[r].t[:], [(w.t[:, 0, k, hc * P:(hc + 1) * P], h2T.t[:, k, tg * 512:(tg + 1) * 512])
                                                         for k in range(KC)], reads=rds, writes=[pg[r]])
                                mm_group(S, pu[r].t[:], [(w.t[:, 1, k, hc * P:(hc + 1) * P], h2T.t[:, k, tg * 512:(tg + 1) * 512])
                                                         for k in range(KC)], reads=rds, writes=[pu[r]])
                                sg_ = sgl[ui % 2]
                                S.op("act", lambda e: e.activation(out=sg_.t[:], in_=pg[r].t[:], func=AF.Silu),
                                     reads=[pg[r]], writes=[sg_])
                                S.op("dve", lambda e: e.tensor_tensor(actT.t[:, fc, tg * 512:(tg + 1) * 512], sg_.t[:],
                                                                      pu[r].t[:], ALU.mult),
                                     reads=[sg_, pu[r]], writes=act_b[tg * 4:(tg + 1) * 4])
                with stage() as st3:
                    wd = [wd0, SB(st3, "wd1", [P, FKC, 256], BF16)]
                    x1q = [SB(st3, f"x1q{i}", [P, 256], F32) for i in range(3)]
                    oq = [SB(st3, f"oq{i}", [P, 256], F32) for i in range(3)]
                    pd = [PS(st3, f"pd{i}", [P, 512]) for i in range(4)]

                    def load_wd(cg):
                        S.dma("pool", wd[cg % 2].t[:], w_src(w_down, D, cg * 256, 256, nk=FKC), writes=[wd[cg % 2]],
                              sembuf=wd[cg % 2])

                    units = [(cg, tt) for cg in range(8) for tt in range(8)]

                    def load_x1(u):
                        cg, tt = units[u]
                        S.dma("sp", x1q[u % 3].t[:], s_x1.ap()[t0 + tt * P:t0 + (tt + 1) * P, cg * 256:(cg + 1) * 256],
                              reads=[B_x1], writes=[x1q[u % 3]], sembuf=x1q[u % 3])

                    load_x1(0)
                    load_x1(1)
                    for u, (cg, tt) in enumerate(units):
                        if half == 0 and u == 4:
                            load_h2(1)
                        if half == 0 and cg == 6 and tt == 0:
                            load_gu0()
                        if tt == 0 and cg + 1 < 8:
                            load_wd(cg + 1)
                        if u + 2 < len(units):
                            load_x1(u + 2)
                        ps = pd[u % 4]
                        w = wd[cg % 2]
                        mm_group(S, ps.t[:, 0:256], [(actT.t[:, k, tt * P:(tt + 1) * P], w.t[:, k, :]) for k in range(FKC)],
                                 reads=[w, act_b[tt]], writes=[ps])
                        o_ = oq[u % 3]
                        S.op("dve", lambda e: e.tensor_tensor(o_.t[:], ps.t[:, 0:256], x1q[u % 3].t[:], ALU.add),
                             reads=[ps, x1q[u % 3]], writes=[o_])
                        S.dma("pool", out_d.ap()[t0 + tt * P:t0 + (tt + 1) * P, cg * 256:(cg + 1) * 256], o_.t[:],
                              reads=[o_], writes=[B_out], sembuf=o_)
            S.wait_all("sp", [B_out])
            S.wait_all("pool", [B_out])
    return nc, dbg_outs, S


def host_consts(s):
    pj = np.arange(P)
    cbm = np.zeros((P, CB_N), np.float32)
    cbm[:, CB_IDENT:CB_IDENT + P] = np.eye(P)
    cbm[:, CB_ONES:CB_ONES + P] = 1.0
    prev = (pj[:, None] >= pj[None, :]).astype(np.float32)
    cur = (pj[:, None] <= pj[None, :]).astype(np.float32)
    cbm[:, CB_MASKA:CB_MASKA + P] = prev
    cbm[:, CB_MASKA + P:CB_MASKA + 2 * P] = cur
    cbm[:, CB_MASKC:CB_MASKC + P] = prev * (1.0 if s == 1 else 0.0)
    cbm[:, CB_MASKC + P:CB_MASKC + 2 * P] = cur
    cbm[:, CB_MBA:CB_MBA + 2 * P] = (1.0 - cbm[:, CB_MASKA:CB_MASKA + 2 * P]) * MNEG
    cbm[:, CB_MBC:CB_MBC + 2 * P] = (1.0 - cbm[:, CB_MASKC:CB_MASKC + 2 * P]) * MNEG
    return cbm.astype(ml_dtypes.bfloat16)


def make_in_maps(x, norm_mix_g, w_in, conv_w, conv_b, gate_b, q_norm_g, k_norm_g, mlstm_norm_g,
                 w_out, norm_ffn_g, w_gate, w_up, w_down):
    f = lambda a: np.ascontiguousarray(np.asarray(a, dtype=np.float32))
    x = f(x)
    w_in_, w_out_, w_gate_, w_up_, w_down_ = f(w_in[0]), f(w_out[0]), f(w_gate[0]), f(w_up[0]), f(w_down[0])
    pp = np.zeros((P, PP_N), np.float32)
    cw = f(conv_w[0])
    pp[:, PP_CONVW:PP_CONVW + 64] = cw.reshape(4, 16, P).transpose(2, 1, 0).reshape(P, 64)
    pp[:, PP_CONVB:PP_CONVB + 16] = f(conv_b[0]).reshape(16, P).T
    pp[:, PP_QG] = f(q_norm_g[0])
    pp[:, PP_KG] = f(k_norm_g[0])
    pp[:, PP_EPS] = EPS
    pj = np.arange(P)
    pp[:, PP_IDENT:PP_IDENT + P] = np.eye(P)
    pp[:, PP_TRINEG:PP_TRINEG + P] = -(pj[:, None] <= pj[None, :]).astype(np.float32)
    pp[:, PP_NEGONES:PP_NEGONES + P] = -1.0
    pp[:, PP_ONES:PP_ONES + P] = 1.0
    rp = np.zeros((P, RP_N), np.float32)
    rp[:, RP_G1:RP_G1 + D] = f(norm_mix_g[0])[None]
    rp[:, RP_G2:RP_G2 + D] = f(norm_ffn_g[0])[None]
    rp[:, RP_MG:RP_MG + 1024] = f(mlstm_norm_g[0]).reshape(1, 1024)
    rp[:, RP_GB:RP_GB + 16] = f(gate_b[0])[None]
    rp[:, RP_QG:RP_QG + P] = f(q_norm_g[0])[None]
    rp[:, RP_KG:RP_KG + P] = f(k_norm_g[0])[None]
    cbs = [host_consts(0), host_consts(1)]
    in_maps = []
    for core in range(8):
        b, s = core // 2, core % 2
        if s == 1:
            xl = x[b]
        else:
            xl = np.concatenate([np.zeros((TC, D), np.float32), x[b, :TO]], axis=0)
        ppc = pp.copy()
        ppc[:, PP_CBIAS] = 0.0 if s == 1 else MNEG
        in_maps.append({"x_loc": np.ascontiguousarray(xl), "w_in": w_in_, "w_out": w_out_, "w_gate": w_gate_,
                        "w_up": w_up_, "w_down": w_down_, "pp": ppc, "rp": rp, "cb": cbs[s]})
    return in_maps


_NC_CACHE = {}


def kernel(**inputs):
    in_maps = make_in_maps(**inputs)
    if "nc" not in _NC_CACHE:
        _NC_CACHE["nc"] = build()[0]
    nc = _NC_CACHE["nc"]
    res = run_bass_kernel_spmd(nc, in_maps, core_ids=list(range(8)))
    out = np.empty((4, 4096, D), np.float32)
    for core in range(8):
        b, s = core // 2, core % 2
        out[b, s * TO:(s + 1) * TO] = np.asarray(res.results[core]["out"], dtype=np.float32)
    return out
```

```python
import numpy as np
import ml_dtypes
import concourse.bass as bass
import concourse.mybir as mybir
from concourse.bass_utils import run_bass_kernel_spmd

F32 = mybir.dt.float32
BF16 = mybir.dt.bfloat16
AF = mybir.ActivationFunctionType
ALU = mybir.AluOpType
AX = mybir.AxisListType

P = 128
D = 2048
KC = 16
TO = 2048
TC = 2048
T = TO + TC
NH = 8
HD = 128
INW = 7184
FH = 5632
FKC = 44
EPS = 1e-6
MNEG = -30000.0
QS = HD ** -0.5

PP_CONVW = 0
PP_CONVB = 64
PP_QG = 80
PP_KG = 81
PP_CBIAS = 82
PP_EPS = 83
PP_IDENT = 84
PP_TRINEG = 212
PP_NEGONES = 340
PP_ONES = 468
PP_N = 596
RP_G1 = 0
RP_G2 = 2048
RP_MG = 4096
RP_GB = 5120
RP_QG = 5136
RP_KG = 5264
RP_N = 5392
CB_IDENT = 0
CB_ONES = 128
CB_MASKA = 256
CB_MASKC = 512
CB_MBA = 768
CB_MBC = 1024
CB_N = 1280


class Buf:
    def __init__(self, name, t=None, multi=False):
        self.name = name
        self.t = t
        self.w = []
        self.rs = []
        self.multi = multi
        self.dsem = None
        self.dcount = 0


class Eng:
    def __init__(self, name, h, sem):
        self.name = name
        self.h = h
        self.sem = sem
        self.n = 0
        self.seen = {}


class Sync:
    def __init__(self, nc):
        self.nc = nc
        self.eng = {}
        for name, h in (("pe", nc.tensor), ("act", nc.scalar), ("dve", nc.vector),
                        ("pool", nc.gpsimd), ("sp", nc.sync)):
            self.eng[name] = Eng(name, h, nc.alloc_semaphore("prog_" + name))
        self.nsem = 5
        self.ninst = 0
        self.dma_sems = {}

    def _wait(self, E, ev):
        sem, val, who = ev
        k = id(sem)
        if E.seen.get(k, 0) >= val:
            return
        E.h.wait_ge(sem, val)
        E.seen[k] = val
        self.ninst += 1

    def _deps(self, en, reads, writes):
        deps = []
        for b in reads:
            deps += b.w
        for b in writes:
            deps += b.rs
            if not b.multi:
                deps += b.w
        return deps

    def op(self, en, fn, reads=(), writes=(), inc=True):
        E = self.eng[en]
        for b in reads:
            for ev in b.w:
                if ev[2] == en and en == "pe":
                    continue
                self._wait(E, ev)
        for b in writes:
            for ev in b.rs:
                if ev[2] == en:
                    continue
                self._wait(E, ev)
            for ev in b.w:
                if ev[2] == en:
                    continue
                self._wait(E, ev)
        ins = fn(E.h)
        self.ninst += 1
        if inc:
            E.n += 1
            ins.then_inc(E.sem, 1)
            ev = (E.sem, E.n, en)
        else:
            ev = (E.sem, E.n + 1, en)
        for b in reads:
            b.rs = [r for r in b.rs if r[2] != en] + [ev]
        for b in writes:
            b.w = [ev]
            b.rs = []
        return ins

    def dma(self, qn, out, in_, reads=(), writes=(), sembuf=None):
        E = self.eng[qn]
        for ev in self._deps(qn, reads, writes):
            self._wait(E, ev)
        if sembuf.dsem is None:
            sembuf.dsem = self.nc.alloc_semaphore("d_" + sembuf.name)
            sembuf.dq = qn
            self.nsem += 1
        assert sembuf.dq == qn, f"DMA semaphore of {sembuf.name} shared between queues {sembuf.dq} and {qn}"
        sembuf.dcount += 16
        E.h.dma_start(out=out, in_=in_).then_inc(sembuf.dsem, 16)
        self.ninst += 1
        ev = (sembuf.dsem, sembuf.dcount, "dma:" + sembuf.name)
        self.dma_sems[id(sembuf.dsem)] = (sembuf.dsem, sembuf.dcount)
        for b in reads:
            b.rs = b.rs + [ev]
        for b in writes:
            if b.multi:
                b.w = [w for w in b.w if w[0] is not sembuf.dsem] + [ev]
            else:
                b.w = [ev]
                b.rs = []

    def barrier(self):
        engs = list(self.eng.values())
        for E in engs:
            for O in engs:
                if O.n > 0:
                    self._wait(E, (O.sem, O.n, O.name))
            for sem, cnt in self.dma_sems.values():
                self._wait(E, (sem, cnt, "dma"))

    def wait_all(self, qn, bufs):
        E = self.eng[qn]
        for b in bufs:
            for ev in b.w + b.rs:
                self._wait(E, ev)


def mm_group(S, out, pairs, reads, writes):
    n = len(pairs)
    for i, (l, r) in enumerate(pairs):
        S.op("pe", lambda e, l=l, r=r, i=i: e.matmul(out, l, r, start=(i == 0), stop=(i == n - 1)),
             reads=reads, writes=writes, inc=(i == n - 1))


def build(dbg=False, stop=None):
    nc = bass.Bass("TRN2", target_bir_lowering=False)
    S = Sync(nc)

    def din(name, shape, dt=F32):
        return nc.dram_tensor(name, list(shape), dt, kind="ExternalInput")

    x_loc = din("x_loc", [T, D])
    w_in = din("w_in", [D, INW])
    w_out = din("w_out", [D, D])
    w_gate = din("w_gate", [D, FH])
    w_up = din("w_up", [D, FH])
    w_down = din("w_down", [FH, D])
    pp_d = din("pp", [P, PP_N])
    rp_d = din("rp", [P, RP_N])
    cb_d = din("cb", [P, CB_N], BF16)
    out_d = nc.dram_tensor("out", [TO, D], F32, kind="ExternalOutput")

    sk = "ExternalOutput" if dbg else "Internal"
    s_aq = nc.dram_tensor("s_aq", [NH, P, TO], BF16, kind=sk)
    s_ak = nc.dram_tensor("s_ak", [NH, P, T], BF16, kind=sk)
    s_av = nc.dram_tensor("s_av", [T, NH * HD], BF16, kind=sk)
    s_mq = nc.dram_tensor("s_mq", [NH, P, TO], BF16, kind=sk)
    s_mk = nc.dram_tensor("s_mk", [NH, P, T], BF16, kind=sk)
    s_mv = nc.dram_tensor("s_mv", [T, NH * HD], BF16, kind=sk)
    s_mo = nc.dram_tensor("s_mo", [TO, NH * HD], BF16, kind=sk)
    s_x1 = nc.dram_tensor("s_x1", [TO, D], F32, kind=sk)
    s_h2 = nc.dram_tensor("s_h2", [16, P, KC * P], BF16, kind="Internal")
    dbg_outs = ["s_aq", "s_ak", "s_av", "s_mq", "s_mk", "s_mv", "s_mo", "s_x1"]
    if dbg:
        d_gates = nc.dram_tensor("d_gates", [P, 32 * 16], F32, kind="ExternalOutput")
        d_gp = nc.dram_tensor("d_gp", [P, 6 * 256], F32, kind="ExternalOutput")
        d_mixT = nc.dram_tensor("d_mixT", [P, 16 * TO], BF16, kind="ExternalOutput")
        dbg_outs += ["d_gates", "d_gp", "d_mixT"]
    B_aq, B_ak, B_av = Buf("s_aq", multi=True), Buf("s_ak", multi=True), Buf("s_av", multi=True)
    B_mq, B_mk, B_mv = Buf("s_mq", multi=True), Buf("s_mk", multi=True), Buf("s_mv", multi=True)
    B_mo, B_x1, B_out = Buf("s_mo", multi=True), Buf("s_x1", multi=True), Buf("out", multi=True)
    B_dbg = Buf("dbg", multi=True)
    B_h2 = Buf("s_h2", multi=True)

    def sb(name, shape, dt):
        return nc.sbuf_tensor(name, list(shape), dt)

    uid = [0]

    def SB(stack, name, shape, dt):
        uid[0] += 1
        name = f"{name}_{uid[0]}"
        t = stack.enter_context(nc.sbuf_tensor(name, list(shape), dt))
        return Buf(name, t)

    def PS(stack, name, shape, dt=F32):
        uid[0] += 1
        name = f"{name}_{uid[0]}"
        t = stack.enter_context(nc.psum_tensor(name, list(shape), dt))
        return Buf(name, t)

    from contextlib import ExitStack, contextmanager

    @contextmanager
    def stage():
        with ExitStack() as es:
            yield es
            S.barrier()

    with ExitStack() as top:
        pp = SB(top, "pp_sb", [P, PP_N], F32)
        cb = SB(top, "cbc", [P, CB_N], BF16)
        smallp = SB(top, "smallp", [P, 16 + 128 + 128 + 8], F32)
        gates_all = SB(top, "gates_all", [P, 32, 16], F32)
        halo = SB(top, "halo", [P, 16, 3], F32)
        gpU = SB(top, "gpU", [P, 32, 8], F32)
        gpUS = SB(top, "gpUS", [P, 32, 8], F32)
        gpG = SB(top, "gpG", [P, 32, 8], F32)
        gpGS = SB(top, "gpGS", [P, 32, 8], F32)
        gpEL = SB(top, "gpEL", [P, 32, 8], F32)
        S.dma("sp", pp.t[:], pp_d.ap(), writes=[pp], sembuf=pp)
        S.dma("sp", cb.t[:], cb_d.ap(), writes=[cb], sembuf=cb)
        S.dma("sp", smallp.t[:, 0:272], rp_d.ap()[:, RP_GB:RP_GB + 272], writes=[smallp], sembuf=smallp)
        ident = cb.t[:, CB_IDENT:CB_IDENT + 128]
        ones_b = cb.t[:, CB_ONES:CB_ONES + 128]
        maskA = cb.t[:, CB_MASKA:CB_MASKA + 256]
        maskC = cb.t[:, CB_MASKC:CB_MASKC + 256]
        maskLE = cb.t[:, CB_MASKA + 128:CB_MASKA + 256]
        mbA = cb.t[:, CB_MBA:CB_MBA + 256]
        mbC = cb.t[:, CB_MBC:CB_MBC + 256]
        ident_f = pp.t[:, PP_IDENT:PP_IDENT + 128]
        trineg_f = pp.t[:, PP_TRINEG:PP_TRINEG + 128]
        negones_f = pp.t[:, PP_NEGONES:PP_NEGONES + 128]
        ones_f = pp.t[:, PP_ONES:PP_ONES + 128]
        eps_col = pp.t[:, PP_EPS:PP_EPS + 1]
        qgs = smallp.t[:, 272:273]
        negshift = smallp.t[:, 273:274]
        S.op("dve", lambda e: e.tensor_scalar(qgs, pp.t[:, PP_QG:PP_QG + 1], QS, None, ALU.mult),
             reads=[pp], writes=[smallp])
        S.op("dve", lambda e: e.reduce_max(smallp.t[:, 274:275], smallp.t[:, 16:144], axis=AX.X,
                                           apply_absolute_value=True), reads=[smallp], writes=[smallp])
        S.op("dve", lambda e: e.reduce_max(smallp.t[:, 275:276], smallp.t[:, 144:272], axis=AX.X,
                                           apply_absolute_value=True), reads=[smallp], writes=[smallp])
        S.op("dve", lambda e: e.scalar_tensor_tensor(negshift, smallp.t[:, 274:275], -(HD ** 0.5),
                                                     smallp.t[:, 275:276], ALU.mult, ALU.mult),
             reads=[smallp], writes=[smallp])

        def norm_transpose(st, src_row_ap, src_buf, ntiles, grow, dstT, dst_bufs, tag):
            xb = [SB(st, f"xb{tag}{i}", [P, D], F32) for i in range(3)]
            xn = [SB(st, f"xn{tag}{i}", [P, D], BF16) for i in range(3)]
            junk = SB(st, f"junk{tag}", [P, D], BF16)
            stt = [SB(st, f"st{tag}{i}", [P, 8], F32) for i in range(3)]
            tp = [PS(st, f"tp{tag}{i}", [P, KC, P], BF16) for i in range(2)]
            dst_bufs2 = dst_bufs[len(dst_bufs) // 2:]
            dst_bufs = dst_bufs[:len(dst_bufs) // 2]
            def nA(i):
                j = i % 3
                jn = i % 3
                S.dma("sp", xb[j].t[:], src_row_ap(i), reads=[src_buf] if src_buf else [],
                      writes=[xb[j]], sembuf=xb[j])
                S.op("act", lambda e: e.activation(out=junk.t[:], in_=xb[j].t[:], func=AF.Square,
                                                   accum_out=stt[jn].t[:, 0:1]),
                     reads=[xb[j]], writes=[junk, stt[jn]])
                S.op("dve", lambda e: e.tensor_scalar(stt[jn].t[:, 1:2], stt[jn].t[:, 0:1], 1.0 / D, EPS,
                                                      ALU.mult, ALU.add), reads=[stt[jn]], writes=[stt[jn]])
                S.op("act", lambda e: e.activation(out=stt[jn].t[:, 2:3], in_=stt[jn].t[:, 1:2], func=AF.Sqrt),
                     reads=[stt[jn]], writes=[stt[jn]])
                S.op("dve", lambda e: e.reciprocal(stt[jn].t[:, 3:4], stt[jn].t[:, 2:3]),
                     reads=[stt[jn]], writes=[stt[jn]])
                S.op("dve", lambda e: e.scalar_tensor_tensor(xn[jn].t[:], xb[j].t[:], stt[jn].t[:, 3:4],
                                                             grow.t[:], ALU.mult, ALU.mult),
                     reads=[xb[j], stt[jn], grow], writes=[xn[jn]])

            def nB(i):
                jn = i % 3
                jt = i % 2
                for k in range(KC):
                    S.op("pe", lambda e, k=k: e.transpose(tp[jt].t[:, k, :], xn[jn].t[:, k * P:(k + 1) * P], ident),
                         reads=[xn[jn], cb], writes=[tp[jt]], inc=(k == KC - 1))
                S.op("act", lambda e: e.copy(out=dstT.t[:, :, i * P:(i + 1) * P], in_=tp[jt].t[:]),
                     reads=[tp[jt]], writes=[dst_bufs[i], dst_bufs2[i]])

            for i in range(ntiles + 2):
                if i < ntiles:
                    nA(i)
                if i >= 2:
                    nB(i - 2)

        def w_src(wd, ncols_total, col0, ncols, nk=KC, row0=0):
            return bass.AP(wd, row0 * ncols_total + col0, [[ncols_total, P], [P * ncols_total, nk], [1, ncols]])

        with stage() as st:
            hT = SB(st, "hT", [P, KC, 2048], BF16)
            hT_b = [Buf(f"hTb{i}") for i in range(16)]
            hT_b2 = [Buf(f"hTc{i}") for i in range(16)]
            grow1 = SB(st, "grow1", [P, D], F32)
            S.dma("sp", grow1.t[:], rp_d.ap()[:, RP_G1:RP_G1 + D], writes=[grow1], sembuf=grow1)
            wb = [SB(st, f"wb{i}", [P, KC, 512], BF16) for i in range(2)]
            wg16 = SB(st, "wg16", [P, KC, 16], BF16)
            S.dma("pool", wg16.t[:], w_src(w_in, INW, 7168, 16), writes=[wg16], sembuf=wg16)
            stg = [SB(st, f"stg{i}", [P, 2048], BF16) for i in range(2)]
            vst = [SB(st, f"vst{i}", [P, 512], BF16) for i in range(4)]
            cbuf = [SB(st, f"cbuf{i}", [P, 515], F32) for i in range(2)]
            cacc = [SB(st, f"cacc{i}", [P, 512], F32) for i in range(2)]
            sq = [SB(st, f"sq{i}", [P, 512], BF16) for i in range(2)]
            sv = [SB(st, f"sv{i}", [P, 512], F32) for i in range(2)]
            sigf = [SB(st, f"sigf{i}", [P, 512], F32) for i in range(2)]
            mgrow1 = SB(st, "mgrow1", [P, 1024], F32)
            S.dma("sp", mgrow1.t[:], rp_d.ap()[:, RP_MG:RP_MG + 1024], writes=[mgrow1], sembuf=mgrow1)
            S.op("dve", lambda e: e.memset(halo.t[:], 0.0), writes=[halo])
            cnt = {"stg": 0, "vst": 0, "r": 0}

            for is_ctx in (True, False):
                tok0 = 0 if is_ctx else TC
                pre_cols = (1024, 1536) if is_ctx else (0, 512)
                prefetched = stop != "inproj_small"
                if prefetched:
                    for gi_, col_ in enumerate(pre_cols):
                        S.dma("pool", wb[gi_].t[:], w_src(w_in, INW, col_, 512), writes=[wb[gi_]], sembuf=wb[gi_])
                with stage() as st1:
                    norm_transpose(st1, lambda i: x_loc.ap()[tok0 + i * P: tok0 + (i + 1) * P, :], None, 16,
                                   grow1, hT, hT_b + hT_b2, "a")
                with stage() as st2:
                    psA = [PS(st2, f"psA{i}", [P, 512]) for i in range(4)]
                    psB = [PS(st2, f"psB{i}", [P, 512]) for i in range(2)]
                    psG = PS(st2, "psG", [P, 512])
                    groups = []
                    if not is_ctx:
                        groups += [("aq", 0, 0), ("aq", 512, 4)]
                    groups += [("ak", 1024, 0), ("ak", 1536, 4)]
                    groups += [("mq", 3072, 0), ("mq", 3584, 4)]
                    groups += [("mk", 4096, 0), ("mk", 4608, 4)]
                    groups += [("av", 2048, 0), ("av", 2560, 1), ("mv", 5120, 0), ("mv", 5632, 1)]
                    if not is_ctx:
                        groups += [("mo", 6144, 0), ("mo", 6656, 1)]
                    if stop == "inproj_small":
                        groups = groups[:1] + [g for g in groups if g[0] == "av"][:1]

                    def load_w(gi):
                        kind, col0, _ = groups[gi]
                        S.dma("pool", wb[gi % 2].t[:], w_src(w_in, INW, col0, 512), writes=[wb[gi % 2]],
                              sembuf=wb[gi % 2])

                    if not prefetched:
                        load_w(0)
                    psi = 0
                    pending = []
                    for gi, (kind, col0, hb) in enumerate(groups):
                        if gi + 1 < len(groups) and not (prefetched and gi == 0):
                            load_w(gi + 1)
                        w = wb[gi % 2]
                        if kind in ("aq", "ak", "mq", "mk"):
                            for hc in range(4):
                                head = hb + hc
                                halo_only = (kind == "mq" and is_ctx)
                                sg = stg[cnt["stg"] % 2]
                                if not halo_only:
                                    cnt["stg"] += 1
                                hq = head + (8 if kind == "mk" else 0)
                                tgs = [3] if halo_only else [0, 1, 2, 3]
                                for tg in tgs:
                                    ps = psA[psi % 4]
                                    psi += 1
                                    tsl = slice(2045, 2048) if halo_only else slice(tg * 512, (tg + 1) * 512)
                                    pso_ = ps.t[:, 0:3] if halo_only else ps.t[:]
                                    mm_group(S, pso_,
                                             [(w.t[:, k, hc * P:(hc + 1) * P], hT.t[:, k, tsl])
                                              for k in range(KC)],
                                             reads=[w] + hT_b[tg * 4:(tg + 1) * 4] + hT_b2[tg * 4:(tg + 1) * 4], writes=[ps])
                                    r = cnt["r"] % 2
                                    cnt["r"] += 1
                                    dst = sg.t[:, tg * 512:(tg + 1) * 512]
                                    if kind in ("aq", "ak"):
                                        gcol = qgs if kind == "aq" else pp.t[:, PP_KG:PP_KG + 1]
                                        S.op("act", lambda e: e.activation(out=sq[r].t[:], in_=ps.t[:], func=AF.Square),
                                             reads=[ps], writes=[sq[r]])
                                        for fn_ in pending:
                                            fn_()
                                        pending.clear()

                                        def rest(ps=ps, r=r, dst=dst, gcol=gcol, sg=sg):
                                            S.op("pe", lambda e: e.matmul(psB[r].t[:], ones_b, sq[r].t[:], start=True, stop=True),
                                                 reads=[sq[r], cb], writes=[psB[r]])
                                            S.op("act", lambda e: e.activation(out=sv[r].t[:], in_=psB[r].t[:], func=AF.Ln,
                                                                               bias=eps_col, scale=1.0 / HD),
                                                 reads=[psB[r], pp], writes=[sv[r]])
                                            S.op("act", lambda e: e.activation(out=sv[r].t[:], in_=sv[r].t[:], func=AF.Exp,
                                                                               scale=-0.5),
                                                 reads=[sv[r]], writes=[sv[r]])
                                            S.op("dve", lambda e: e.scalar_tensor_tensor(dst, ps.t[:], gcol, sv[r].t[:],
                                                                                         ALU.mult, ALU.mult),
                                                 reads=[ps, sv[r], smallp, pp], writes=[sg])
                                        pending.append(rest)
                                    else:
                                        cbf_ = cbuf[r]
                                        if tg == tgs[0]:
                                            S.op("dve", lambda e: e.tensor_copy(cbf_.t[:, 0:3], halo.t[:, hq, :]),
                                                 reads=[halo], writes=[cbf_])
                                        S.op("act", lambda e: e.copy(out=cbf_.t[:, 512:515] if halo_only else cbf_.t[:, 3:515],
                                                                     in_=pso_),
                                             reads=[ps], writes=[cbf_])
                                        for fn_ in pending:
                                            fn_()
                                        pending.clear()
                                        if halo_only or (is_ctx and tg == 3):
                                            S.op("dve", lambda e: e.tensor_copy(halo.t[:, hq, :], cbf_.t[:, 512:515]),
                                                 reads=[cbf_], writes=[halo])
                                        if halo_only:
                                            continue
                                        if tg < 3:
                                            nxt = cbuf[(r + 1) % 2]
                                            S.op("dve", lambda e: e.tensor_copy(nxt.t[:, 0:3], cbf_.t[:, 512:515]),
                                                 reads=[cbf_], writes=[nxt])
                                        ac = cacc[r]
                                        S.op("dve", lambda e: e.tensor_scalar(ac.t[:], cbf_.t[:, 0:512],
                                                                              pp.t[:, PP_CONVW + hq * 4:PP_CONVW + hq * 4 + 1],
                                                                              None, ALU.mult),
                                             reads=[cbf_, pp], writes=[ac])
                                        for jj in (1, 2, 3):
                                            S.op("dve", lambda e, jj=jj: e.scalar_tensor_tensor(
                                                ac.t[:], cbf_.t[:, jj:jj + 512],
                                                pp.t[:, PP_CONVW + hq * 4 + jj:PP_CONVW + hq * 4 + jj + 1],
                                                ac.t[:], ALU.mult, ALU.add), reads=[cbf_, pp, ac], writes=[ac])
                                        def silu_(dst=dst, ac=ac, hq=hq, sg=sg):
                                            S.op("act", lambda e: e.activation(out=dst, in_=ac.t[:], func=AF.Silu,
                                                                               bias=pp.t[:, PP_CONVB + hq:PP_CONVB + hq + 1]),
                                                 reads=[ac, pp], writes=[sg])
                                        pending.append(silu_)
                                for fn_ in pending:
                                    fn_()
                                pending.clear()
                                if halo_only:
                                    continue
                                sdst, sB = {"aq": (s_aq, B_aq), "ak": (s_ak, B_ak), "mq": (s_mq, B_mq),
                                            "mk": (s_mk, B_mk)}[kind]
                                t0 = tok0 if kind in ("ak", "mk") else 0
                                S.dma("pool", sdst.ap()[head, :, t0:t0 + 2048], sg.t[:], reads=[sg], writes=[sB],
                                      sembuf=sg)
                        else:
                            for tt in range(16):
                                ps = psA[psi % 4]
                                psi += 1
                                mm_group(S, ps.t[:],
                                         [(hT.t[:, k, tt * P:(tt + 1) * P], w.t[:, k, :]) for k in range(KC)],
                                         reads=[w, hT_b[tt], hT_b2[tt]], writes=[ps])
                                vb = vst[cnt["vst"] % 4]
                                cnt["vst"] += 1
                                if kind == "mo":
                                    sf = sigf[cnt["vst"] % 2]
                                    S.op("act", lambda e: e.activation(out=sf.t[:], in_=ps.t[:], func=AF.Sigmoid),
                                         reads=[ps], writes=[sf])
                                    S.op("dve", lambda e: e.tensor_tensor(vb.t[:], sf.t[:],
                                                                          mgrow1.t[:, hb * 512:(hb + 1) * 512], ALU.mult),
                                         reads=[sf, mgrow1], writes=[vb])
                                elif tt % 2 == 0:
                                    S.op("act", lambda e: e.copy(out=vb.t[:], in_=ps.t[:]), reads=[ps], writes=[vb])
                                else:
                                    S.op("dve", lambda e: e.tensor_copy(vb.t[:], ps.t[:]), reads=[ps], writes=[vb])
                                sdst, sB = {"av": (s_av, B_av), "mv": (s_mv, B_mv), "mo": (s_mo, B_mo)}[kind]
                                t0 = 0 if kind == "mo" else tok0
                                S.dma("pool", sdst.ap()[t0 + tt * P:t0 + (tt + 1) * P, hb * 512:(hb + 1) * 512],
                                      vb.t[:], reads=[vb], writes=[sB], sembuf=vb)
                    for tt in range(16):
                        mm_group(S, psG.t[:, 0:16],
                                 [(hT.t[:, k, tt * P:(tt + 1) * P], wg16.t[:, k, :]) for k in range(KC)],
                                 reads=[wg16, hT_b[tt], hT_b2[tt]], writes=[psG])
                        ch = tok0 // P + tt
                        S.op("dve", lambda e: e.tensor_tensor(gates_all.t[:, ch, :], psG.t[:, 0:16],
                                                              smallp.t[:, 0:16], ALU.add),
                             reads=[psG, smallp], writes=[gates_all])
            S.wait_all("sp", [B_aq, B_ak, B_av, B_mq, B_mk, B_mv, B_mo])
            S.wait_all("pool", [B_aq, B_ak, B_av, B_mq, B_mk, B_mv, B_mo])
            if dbg:
                S.dma("sp", d_gates.ap(), bass.AP(gates_all.t, 0, [[512, P], [1, 512]]), reads=[gates_all],
                      writes=[B_dbg], sembuf=gates_all)

        if stop in ("inproj", "inproj_small"):
            S.wait_all("sp", [B_dbg])
            return nc, dbg_outs, S

        def flat(t, n):
            return bass.AP(t, 0, [[n, P], [1, n]])

        with stage() as st:
            ge = SB(st, "ge", [P, 32, 8], F32)
            lsp = SB(st, "lsp", [P, 32, 8], F32)
            gB = SB(st, "gB", [P, 32, 8], F32)
            gBL = SB(st, "gBL", [P, 32, 8], F32)
            gC = SB(st, "gC", [P, 32, 8], F32)
            gMAXC = SB(st, "gMAXC", [P, 32, 8], F32)
            gM = SB(st, "gM", [P, 32, 8], F32)
            gMP = SB(st, "gMP", [P, 33, 8], F32)
            gtmp = SB(st, "gtmp", [P, 32, 8], F32)
            gcol = SB(st, "gcol", [P, 2], F32)
            gdiag = [SB(st, f"gdiag{i}", [P, P], F32) for i in range(2)]
            p1 = PS(st, "gp1", [P, 512])
            p2 = PS(st, "gp2", [P, 512])
            p3 = PS(st, "gp3", [P, 512])
            p4 = PS(st, "gp4", [P, 512])
            S.op("act", lambda e: e.activation(out=ge.t[:], in_=gates_all.t[:, :, 8:16], func=AF.Exp, scale=-1.0),
                 reads=[gates_all], writes=[ge])
            S.op("act", lambda e: e.activation(out=lsp.t[:], in_=ge.t[:], func=AF.Ln, bias=1.0),
                 reads=[ge], writes=[lsp])
            S.op("pe", lambda e: e.matmul(p1.t[:, 0:256], trineg_f, flat(lsp.t, 256), start=True, stop=True),
                 reads=[lsp, pp], writes=[p1])
            S.op("pe", lambda e: e.matmul(p2.t[:, 0:256], negones_f, flat(lsp.t, 256), start=True, stop=True),
                 reads=[lsp, pp], writes=[p2])
            S.op("dve", lambda e: e.tensor_copy(flat(gB.t, 256), p1.t[:, 0:256]), reads=[p1], writes=[gB])
            S.op("dve", lambda e: e.tensor_copy(flat(gBL.t, 256), p2.t[:, 0:256]), reads=[p2], writes=[gBL])
            S.op("dve", lambda e: e.tensor_tensor(gC.t[:], gates_all.t[:, :, 0:8], gB.t[:], ALU.subtract),
                 reads=[gates_all, gB], writes=[gC])
            for hf in range(2):
                S.op("pe", lambda e: e.transpose(p3.t[:, hf * P:(hf + 1) * P], flat(gC.t, 256)[:, hf * P:(hf + 1) * P],
                                                 ident_f), reads=[gC, pp], writes=[p3])
                S.op("dve", lambda e: e.reduce_max(gcol.t[:, hf:hf + 1], p3.t[:, hf * P:(hf + 1) * P], axis=AX.X),
                     reads=[p3], writes=[gcol])
                S.op("dve", lambda e: e.tensor_scalar(gdiag[hf].t[:], ident_f, gcol.t[:, hf:hf + 1], None, ALU.mult),
                     reads=[gcol, pp], writes=[gdiag[hf]])
                S.op("pe", lambda e: e.matmul(p4.t[:, hf * P:(hf + 1) * P], ones_f, gdiag[hf].t[:], start=True, stop=True),
                     reads=[gdiag[hf], pp], writes=[p4])
            S.op("dve", lambda e: e.tensor_copy(flat(gMAXC.t, 256), p4.t[:, 0:256]), reads=[p4], writes=[gMAXC])
            S.op("dve", lambda e: e.memset(gMP.t[:, 0, :], MNEG), writes=[gMP])
            for c in range(32):
                S.op("dve", lambda e: e.tensor_tensor(gM.t[:, c, :], gMP.t[:, c, :], gMAXC.t[:, c, :], ALU.max),
                     reads=[gMP, gMAXC], writes=[gM])
                S.op("dve", lambda e: e.tensor_tensor(gMP.t[:, c + 1, :], gBL.t[:, c, :], gM.t[:, c, :], ALU.add),
                     reads=[gBL, gM], writes=[gMP])
                if c == 15:
                    S.op("dve", lambda e: e.tensor_scalar(gMP.t[:, 16, :], gMP.t[:, 16, :],
                                                          pp.t[:, PP_CBIAS:PP_CBIAS + 1], None, ALU.add),
                         reads=[gMP, pp], writes=[gMP])
            S.op("dve", lambda e: e.tensor_tensor(gtmp.t[:], gMP.t[:, 0:32, :], gM.t[:], ALU.subtract),
                 reads=[gMP, gM], writes=[gtmp])
            S.op("act", lambda e: e.activation(out=gpG.t[:], in_=gtmp.t[:], func=AF.Exp), reads=[gtmp], writes=[gpG])
            S.op("dve", lambda e: e.tensor_scalar(gpGS.t[:], gpG.t[:], QS, None, ALU.mult), reads=[gpG], writes=[gpGS])
            S.op("dve", lambda e: e.tensor_tensor(gtmp.t[:], gC.t[:], gM.t[:], ALU.subtract),
                 reads=[gC, gM, gpG], writes=[gtmp])
            S.op("act", lambda e: e.activation(out=gpU.t[:], in_=gtmp.t[:], func=AF.Exp), reads=[gtmp], writes=[gpU])
            S.op("dve", lambda e: e.tensor_scalar(gpUS.t[:], gpU.t[:], QS, None, ALU.mult), reads=[gpU], writes=[gpUS])
            S.op("dve", lambda e: e.tensor_tensor(gtmp.t[:], gB.t[:], gM.t[:], ALU.add),
                 reads=[gB, gM, gpU], writes=[gtmp])
            S.op("act", lambda e: e.activation(out=gpEL.t[:], in_=gtmp.t[:], func=AF.Exp, scale=-1.0),
                 reads=[gtmp], writes=[gpEL])
            S.op("dve", lambda e: e.tensor_scalar(gpEL.t[:], gpEL.t[:], 1.0 / QS, None, ALU.mult),
                 reads=[gpEL], writes=[gpEL])
            if dbg:
                for i, bsrc in enumerate([gpU, gpG, gpEL, gM, gB, gC]):
                    S.dma("sp", d_gp.ap()[:, i * 256:(i + 1) * 256], flat(bsrc.t, 256), reads=[bsrc],
                          writes=[B_dbg], sembuf=bsrc)
                S.wait_all("sp", [B_dbg])
        if stop == "gates":
            return nc, dbg_outs, S

        with stage() as mixst:
            mixT = SB(mixst, "mixT", [P, 16, TO], BF16)
            mix_b = [Buf(f"mixb{i}") for i in range(16)]
            with stage() as st:
                qT = [SB(st, f"aqT{i}", [P, TO], BF16) for i in range(2)]
                kT = [SB(st, f"akT{i}", [P, T], BF16) for i in range(2)]
                v1 = [SB(st, f"av1{i}", [P, 17, P], BF16) for i in range(2)]
                v2 = [SB(st, f"av2{i}", [P, 4, 5, P], BF16) for i in range(2)]
                v3 = [SB(st, f"av3{i}", [P, 16, 2, P], BF16) for i in range(2)]
                accs = [SB(st, f"aacc{i}", [P, 2, TO], F32) for i in range(2)]
                pT = [SB(st, f"apT{i}", [P, 256], BF16) for i in range(4)]
                sp_ = [PS(st, f"asp{i}", [P, 512]) for i in range(3)]
                po_ = [PS(st, f"apo{i}", [P, 512]) for i in range(3)]

                def load_head(h):
                    j = h % 2
                    S.dma("sp", qT[j].t[:], s_aq.ap()[h], reads=[B_aq], writes=[qT[j]], sembuf=qT[j])
                    S.dma("sp", kT[j].t[:], s_ak.ap()[h], reads=[B_ak], writes=[kT[j]], sembuf=kT[j])
                    S.dma("sp", v1[j].t[:], bass.AP(s_av, (TC - P) * 1024 + h * P, [[1024, P], [P * 1024, 17], [1, P]]),
                          reads=[B_av], writes=[v1[j]], sembuf=v1[j])
                    S.dma("sp", v2[j].t[:], bass.AP(s_av, (4 * P * 3) * 1024 + h * P,
                                                    [[4 * 1024, P], [1024, 4], [512 * 1024, 5], [1, P]]),
                          reads=[B_av], writes=[v2[j]], sembuf=v2[j])
                    for kb in range(2):
                        S.dma("sp", v3[j].t[:, :, kb, :], bass.AP(s_av, kb * 2048 * 1024 + h * P,
                                                                  [[16 * 1024, P], [1024, 16], [1, P]]),
                              reads=[B_av], writes=[v3[j]], sembuf=v3[j])

                units = []
                for h in range(NH):
                    for pi, dil in enumerate((1, 4, 16)):
                        for r in range(dil):
                            for qb in range(16 // dil):
                                units.append((h, pi, dil, r, qb))

                def phaseA(u):
                    h, pi, dil, r, qb = units[u]
                    j = h % 2
                    if (pi, r, qb) == (0, 0, 3) and h + 1 < NH:
                        load_head(h + 1)
                    q0 = r + dil * P * qb
                    qsl = slice(q0, q0 + dil * (P - 1) + 1, dil)
                    kc0 = TC + q0
                    kp0 = kc0 - dil * P
                    ksl_c = slice(kc0, kc0 + dil * (P - 1) + 1, dil)
                    ksl_p = slice(kp0, kp0 + dil * (P - 1) + 1, dil)
                    sp, pt = sp_[u % 3], pT[u % 4]
                    mb = mbC if qb == 0 else mbA
                    S.op("pe", lambda e: e.matmul(sp.t[:, 0:256], ident, mb, start=True, stop=False),
                         reads=[cb], writes=[sp], inc=False)
                    S.op("pe", lambda e: e.matmul(sp.t[:, 0:P], kT[j].t[:, ksl_p], qT[j].t[:, qsl],
                                                  start=False, stop=False),
                         reads=[kT[j], qT[j]], writes=[sp], inc=False)
                    S.op("pe", lambda e: e.matmul(sp.t[:, P:2 * P], kT[j].t[:, ksl_c], qT[j].t[:, qsl],
                                                  start=False, stop=True),
                         reads=[kT[j], qT[j]], writes=[sp])
                    S.op("act", lambda e: e.activation(out=pt.t[:], in_=sp.t[:, 0:256], func=AF.Exp,
                                                       bias=negshift), reads=[sp, smallp], writes=[pt])

                def phaseB(u):
                    h, pi, dil, r, qb = units[u]
                    j = h % 2
                    acc = accs[j]
                    q0 = r + dil * P * qb
                    if pi == 0:
                        vp, vc, vbuf = v1[j].t[:, qb, :], v1[j].t[:, qb + 1, :], v1[j]
                    elif pi == 1:
                        vp, vc, vbuf = v2[j].t[:, r, qb, :], v2[j].t[:, r, qb + 1, :], v2[j]
                    else:
                        vp, vc, vbuf = v3[j].t[:, r, 0, :], v3[j].t[:, r, 1, :], v3[j]
                    po, pt = po_[u % 3], pT[u % 4]
                    S.op("pe", lambda e: e.matmul(po.t[:, 0:P], vp, pt.t[:, 0:P], start=True, stop=False),
                         reads=[pt, vbuf], writes=[po], inc=False)
                    S.op("pe", lambda e: e.matmul(po.t[:, 0:P], vc, pt.t[:, P:2 * P], start=False, stop=True),
                         reads=[pt, vbuf], writes=[po], inc=False)
                    S.op("pe", lambda e: e.matmul(po.t[:, P:2 * P], ones_b, pt.t[:, 0:P], start=True, stop=False),
                         reads=[pt, cb], writes=[po], inc=False)
                    S.op("pe", lambda e: e.matmul(po.t[:, P:2 * P], ones_b, pt.t[:, P:2 * P], start=False, stop=True),
                         reads=[pt, cb], writes=[po])
                    accv = bass.AP(acc.t, q0, [[2 * TO, P], [TO, 2], [dil, P]])
                    pov = bass.AP(po.t, 0, [[512, P], [P, 2], [1, P]])
                    if pi == 0:
                        S.op("act", lambda e: e.copy(out=accv, in_=pov), reads=[po], writes=[acc])
                    else:
                        S.op("dve", lambda e: e.tensor_tensor(accv, accv, pov, ALU.add),
                             reads=[po, acc], writes=[acc])
                    if (pi, r, qb) == (2, 15, 0):
                        S.op("act", lambda e: e.activation(out=acc.t[:, 1, :], in_=acc.t[:, 1, :], func=AF.Ln),
                             reads=[acc], writes=[acc])
                        S.op("act", lambda e: e.activation(out=acc.t[:, 1, :], in_=acc.t[:, 1, :], func=AF.Exp, scale=-1.0),
                             reads=[acc], writes=[acc])
                        S.op("dve", lambda e: e.tensor_tensor(mixT.t[:, h, :], acc.t[:, 0, :], acc.t[:, 1, :], ALU.mult),
                             reads=[acc], writes=mix_b)

                load_head(0)
                SK = 3
                for idx in range(len(units) + SK):
                    if idx < len(units):
                        phaseA(idx)
                    if idx >= SK:
                        phaseB(idx - SK)
            if stop == "attn":
                if dbg:
                    S.dma("sp", d_mixT.ap(), bass.AP(mixT.t, 0, [[16 * TO, P], [1, 16 * TO]]), reads=mix_b,
                          writes=[B_dbg], sembuf=mixT)
                    S.wait_all("sp", [B_dbg])
                return nc, dbg_outs, S

            with stage() as st:
                kk = [SB(st, f"mkk{i}", [P, NH, 512], BF16) for i in range(2)]
                qq = [SB(st, f"mqq{i}", [P, NH, 512], BF16) for i in range(2)]
                vv = [SB(st, f"mvv{i}", [P, 4, NH, 132], BF16) for i in range(2)]
                mo = [SB(st, f"mmo{i}", [P, 4, 1024], BF16) for i in range(2)]
                Cst = [SB(st, f"mCst{g}", [P, 4, 132], F32) for g in range(2)]
                CGf = [SB(st, f"mCG{i}", [P, 4, 132], F32) for i in range(2)]
                cbf = [SB(st, f"mcbf{i}", [P, 4, 132], BF16) for i in range(2)]
                vu = [SB(st, f"mvu{i}", [P, 4, 132], BF16) for i in range(4)]
                wt = [SB(st, f"mwt{i}", [P, 4, P], BF16) for i in range(2)]
                ktok = [SB(st, f"mktok{i}", [P, 4, P], BF16) for i in range(2)]
                hh = [SB(st, f"mhh{i}", [P, 4, P], F32) for i in range(4)]
                sqj = [SB(st, f"msqj{i}", [P, 4, P], F32) for i in range(4)]
                ot = [SB(st, f"mot{i}", [P, 4, P], BF16) for i in range(4)]
                sm = [SB(st, f"msm{i}", [P, 32], F32) for i in range(4)]
                FA = [PS(st, f"mFA{i}", [P, 4, P]) for i in range(2)]
                FC = [PS(st, f"mFC{i}", [P, 4, P]) for i in range(2)]
                FD = [PS(st, f"mFD{i}", [P, 512]) for i in range(2)]
                FB = [PS(st, f"mFB{i}", [P, 8, P], BF16) for i in range(2)]
                for g in range(2):
                    S.op("dve", lambda e, g=g: e.memset(Cst[g].t[:], 0.0), writes=[Cst[g]])
                for i in range(2):
                    S.op("dve", lambda e, i=i: e.memset(vv[i].t[:, :, :, 128:129], 1.0), writes=[vv[i]])

                def load_grp(g):
                    j = g % 2
                    S.dma("sp", kk[j].t[:], bass.AP(s_mk, g * 512, [[T, P], [P * T, NH], [1, 512]]),
                          reads=[B_mk], writes=[kk[j]], sembuf=kk[j])
                    for ci_ in range(4):
                        S.dma("sp", vv[j].t[:, ci_, :, 0:P],
                              bass.AP(s_mv, (g * 512 + ci_ * P) * 1024, [[1024, P], [P, NH], [1, P]]),
                              reads=[B_mv], writes=[vv[j]], sembuf=vv[j])
                    if g >= 4:
                        S.dma("sp", qq[j].t[:], bass.AP(s_mq, (g - 4) * 512, [[TO, P], [P * TO, NH], [1, 512]]),
                              reads=[B_mq], writes=[qq[j]], sembuf=qq[j])

                def load_mo(g):
                    j = g % 2
                    S.dma("sp", mo[j].t[:], bass.AP(s_mo, (g - 4) * 512 * 1024, [[1024, P], [P * 1024, 4], [1, 1024]]),
                          reads=[B_mo], writes=[mo[j]], sembuf=mo[j])

                munits = [(c, hg) for c in range(32) for hg in range(2)]
                mask_bc = bass.AP(cb.t, CB_MASKA + P, [[CB_N, P], [0, 4], [1, P]])

                def bc4(t, off, n):
                    return bass.AP(t, off, [[256, P], [1, 4], [0, n]])

                def phaseV(u):
                    c, hg = munits[u]
                    g, ci = c // 4, c % 4
                    j = g % 2
                    hs = slice(hg * 4, hg * 4 + 4)
                    if ci == 1 and hg == 1 and g + 1 < 8:
                        load_grp(g + 1)
                    if ci == 3 and hg == 0 and 4 <= g + 1 < 8:
                        load_mo(g + 1)
                    S.op("pool", lambda e: e.tensor_tensor(vu[u % 4].t[:, :, 0:129], vv[j].t[:, ci, hs, 0:129],
                                                           bc4(gpU.t, c * 8 + hg * 4, 129), ALU.mult),
                         reads=[vv[j], gpU], writes=[vu[u % 4]])

                def phaseX(u):
                    c, hg = munits[u]
                    g, ci = c // 4, c % 4
                    j = g % 2
                    s = u % 2
                    vu_ = vu[u % 4]
                    own = c >= 16
                    csl = slice(ci * P, (ci + 1) * P)
                    hs = slice(hg * 4, hg * 4 + 4)
                    if own:
                        for h4 in range(4):
                            S.op("pe", lambda e, h4=h4: e.matmul(FA[s].t[:, h4, :], kk[j].t[:, hg * 4 + h4, csl],
                                                                 qq[j].t[:, hg * 4 + h4, csl], start=True, stop=True),
                                 reads=[kk[j], qq[j]], writes=[FA[s]], inc=(h4 == 3))
                        S.op("dve", lambda e: e.tensor_tensor(wt[s].t[:], FA[s].t[:], mask_bc, ALU.mult),
                             reads=[FA[s], cb], writes=[wt[s]])
                    if c < 31:
                        for h4 in range(4):
                            S.op("pe", lambda e, h4=h4: e.transpose(FB[s].t[:, h4, :], kk[j].t[:, hg * 4 + h4, csl], ident),
                                 reads=[kk[j], cb], writes=[FB[s]], inc=(h4 == 3))
                        S.op("act", lambda e: e.copy(out=ktok[s].t[:], in_=FB[s].t[:, 0:4, :]),
                             reads=[FB[s]], writes=[ktok[s]])
                        for h4 in range(4):
                            S.op("pe", lambda e, h4=h4: e.matmul(FC[s].t[:, h4, :], ktok[s].t[:, h4, :],
                                                                 vu_.t[:, h4, 0:P], start=True, stop=True),
                                 reads=[ktok[s], vu_], writes=[FC[s]], inc=False)
                            S.op("pe", lambda e, h4=h4: e.matmul(FD[s].t[:, 8 + h4:9 + h4], ktok[s].t[:, h4, :],
                                                                 vu_.t[:, h4, P:P + 1], start=True, stop=True),
                                 reads=[ktok[s], vu_], writes=[FD[s]], inc=(h4 == 3))
                    S.op("dve", lambda e: e.tensor_tensor(CGf[s].t[:, :, 0:129], Cst[hg].t[:, :, 0:129],
                                                          bc4(gpG.t, c * 8 + hg * 4, 129), ALU.mult),
                         reads=[Cst[hg], gpG], writes=[CGf[s]])
                    if own:
                        S.op("act", lambda e: e.copy(out=cbf[s].t[:, :, 0:129], in_=CGf[s].t[:, :, 0:129]),
                             reads=[CGf[s]], writes=[cbf[s]])
                    if c < 31:
                        S.op("dve", lambda e: e.tensor_tensor(Cst[hg].t[:, :, 0:P], CGf[s].t[:, :, 0:P], FC[s].t[:],
                                                              ALU.add),
                             reads=[CGf[s], FC[s]], writes=[Cst[hg]])
                        S.op("dve", lambda e: e.tensor_tensor(Cst[hg].t[:, :, P:P + 1], CGf[s].t[:, :, P:P + 1],
                                                              bass.AP(FD[s].t, 8, [[512, P], [1, 4], [1, 1]]), ALU.add),
                             reads=[CGf[s], FD[s]], writes=[Cst[hg]])
                    if own:
                        for h4 in range(4):
                            h = hg * 4 + h4
                            S.op("pe", lambda e, h4=h4: e.matmul(FA[s].t[:, h4, :], wt[s].t[:, h4, :], vu_.t[:, h4, 0:P],
                                                                 start=True, stop=False),
                                 reads=[wt[s], vu_], writes=[FA[s]], inc=False)
                            S.op("pe", lambda e, h4=h4, h=h: e.matmul(FA[s].t[:, h4, :], qq[j].t[:, h, csl],
                                                                      cbf[s].t[:, h4, 0:P], start=False, stop=True),
                                 reads=[qq[j], cbf[s]], writes=[FA[s]], inc=False)
                            S.op("pe", lambda e, h4=h4: e.matmul(FD[s].t[:, h4:h4 + 1], wt[s].t[:, h4, :],
                                                                 vu_.t[:, h4, P:P + 1], start=True, stop=False),
                                 reads=[wt[s], vu_], writes=[FD[s]], inc=False)
                            S.op("pe", lambda e, h4=h4, h=h: e.matmul(FD[s].t[:, h4:h4 + 1], qq[j].t[:, h, csl],
                                                                      cbf[s].t[:, h4, P:P + 1], start=False, stop=True),
                                 reads=[qq[j], cbf[s]], writes=[FD[s]], inc=(h4 == 3))

                def phaseYa(u):
                    c, hg = munits[u]
                    if c < 16:
                        return
                    s = u % 2
                    r4 = u % 4
                    m = sm[r4]
                    S.op("dve", lambda e: e.tensor_copy(m.t[:, 0:4], FD[s].t[:, 0:4]), reads=[FD[s]], writes=[m])
                    S.op("dve", lambda e: e.scalar_tensor_tensor(m.t[:, 4:8], m.t[:, 0:4], -1.0, m.t[:, 0:4],
                                                                 ALU.mult, ALU.max), reads=[m], writes=[m])
                    S.op("dve", lambda e: e.tensor_tensor(m.t[:, 4:8], m.t[:, 4:8], gpEL.t[:, c, hg * 4:hg * 4 + 4],
                                                          ALU.max), reads=[m, gpEL], writes=[m])
                    S.op("dve", lambda e: e.reciprocal(m.t[:, 8:12], m.t[:, 4:8]), reads=[m], writes=[m])
                    S.op("dve", lambda e: e.tensor_tensor(hh[r4].t[:], FA[s].t[:],
                                                          bass.AP(m.t, 8, [[32, P], [1, 4], [0, P]]), ALU.mult),
                         reads=[FA[s], m], writes=[hh[r4]])
                    S.op("act", lambda e: e.activation(out=sqj[r4].t[:], in_=hh[r4].t[:], func=AF.Square),
                         reads=[hh[r4]], writes=[sqj[r4]])

                def phaseYb(u):
                    c, hg = munits[u]
                    if c < 16:
                        return
                    r4 = u % 4
                    m = sm[r4]
                    S.op("dve", lambda e: e.reduce_sum(m.t[:, 12:16], sqj[r4].t[:], axis=AX.X),
                         reads=[sqj[r4], m], writes=[m])
                    S.op("dve", lambda e: e.tensor_scalar(m.t[:, 16:20], m.t[:, 12:16], 1.0 / HD, EPS, ALU.mult, ALU.add),
                         reads=[m], writes=[m])
                    S.op("act", lambda e: e.activation(out=m.t[:, 20:24], in_=m.t[:, 16:20], func=AF.Sqrt),
                         reads=[m], writes=[m])

                def phaseYc(u):
                    c, hg = munits[u]
                    if c < 16:
                        return
                    g, ci = c // 4, c % 4
                    j = g % 2
                    s = u % 2
                    r4 = u % 4
                    m = sm[r4]
                    S.op("dve", lambda e: e.reciprocal(m.t[:, 24:28], m.t[:, 20:24]), reads=[m], writes=[m])
                    S.op("dve", lambda e: e.tensor_tensor(hh[r4].t[:], hh[r4].t[:],
                                                          bass.AP(m.t, 24, [[32, P], [1, 4], [0, P]]), ALU.mult),
                         reads=[hh[r4], m], writes=[hh[r4]])
                    S.op("dve", lambda e: e.tensor_tensor(ot[r4].t[:], hh[r4].t[:],
                                                          bass.AP(mo[j].t, ci * 1024 + hg * 512,
                                                                  [[4096, P], [P, 4], [1, P]]), ALU.mult),
                         reads=[hh[r4], mo[j]], writes=[ot[r4]])
                    for h4 in range(4):
                        S.op("pe", lambda e, h4=h4: e.transpose(FB[s].t[:, 4 + h4, :], ot[r4].t[:, h4, :], ident),
                             reads=[ot[r4], cb], writes=[FB[s]], inc=(h4 == 3))
                    tt = c - 16
                    S.op("act", lambda e: e.copy(out=mixT.t[:, 8 + hg * 4:12 + hg * 4, tt * P:(tt + 1) * P],
                                                 in_=FB[s].t[:, 4:8, :]), reads=[FB[s]], writes=[mix_b[tt]])

                load_grp(0)
                NU = len(munits)
                for idx in range(NU + 5):
                    if idx < NU:
                        phaseV(idx)
                    if 2 <= idx < NU + 2:
                        phaseX(idx - 2)
                    if 3 <= idx < NU + 3:
                        phaseYa(idx - 3)
                    if 4 <= idx < NU + 4:
                        phaseYb(idx - 4)
                    if idx >= 5:
                        phaseYc(idx - 5)
            if dbg:
                S.dma("sp", d_mixT.ap(), bass.AP(mixT.t, 0, [[16 * TO, P], [1, 16 * TO]]), reads=mix_b,
                      writes=[B_dbg], sembuf=mixT)
                S.wait_all("sp", [B_dbg])
            if stop == "mix":
                return nc, dbg_outs, S

            with stage() as st:
                wo = SB(st, "wo", [P, KC, D], BF16)
                wo_b = [Buf(f"wob{i}") for i in range(4)]
                xr = [SB(st, f"xr{i}", [P, D], F32) for i in range(3)]
                xr_st = [Buf(f"xrst{i}") for i in range(3)]
                grow2 = SB(st, "grow2", [P, D], F32)
                xn2 = [SB(st, f"xn2{i}", [P, D], BF16) for i in range(2)]
                junk2 = SB(st, "junk2", [P, D], BF16)
                stt2 = [SB(st, f"stt2{i}", [P, 8], F32) for i in range(3)]
                stg2 = [SB(st, f"stg2{i}", [P, KC, P], BF16) for i in range(2)]
                pso = [PS(st, f"pso{i}", [P, 512]) for i in range(4)]
                tp2 = [PS(st, f"tp2{i}", [P, KC, P], BF16) for i in range(2)]
                S.dma("sp", grow2.t[:], rp_d.ap()[:, RP_G2:RP_G2 + D], writes=[grow2], sembuf=grow2)
                for cg in range(4):
                    S.dma("pool", wo.t[:, :, cg * 512:(cg + 1) * 512], w_src(w_out, D, cg * 512, 512),
                          writes=[wo_b[cg]], sembuf=wo_b[cg])

                def o_load(tt):
                    S.dma("sp", xr[tt % 3].t[:], x_loc.ap()[TC + tt * P:TC + (tt + 1) * P, :], writes=[xr[tt % 3]],
                          sembuf=xr[tt % 3])

                def o_mm(tt, cgs=(0, 1, 2, 3), bank=None):
                    x_ = xr[tt % 3]
                    for cg in cgs:
                        ps = pso[(tt * 4 + cg) % 4 if bank is None else bank(tt, cg)]
                        mm_group(S, ps.t[:], [(mixT.t[:, k, tt * P:(tt + 1) * P], wo.t[:, k, cg * 512:(cg + 1) * 512])
                                              for k in range(KC)], reads=[wo_b[cg], mix_b[tt]], writes=[ps])
                        S.op("dve", lambda e, cg=cg, ps=ps: e.tensor_tensor(x_.t[:, cg * 512:(cg + 1) * 512], ps.t[:],
                                                                            x_.t[:, cg * 512:(cg + 1) * 512], ALU.add),
                             reads=[ps, x_], writes=[x_])

                def o_n1(tt):
                    x_ = xr[tt % 3]
                    m = stt2[tt % 3]
                    S.dma("pool", s_x1.ap()[tt * P:(tt + 1) * P, :], x_.t[:], reads=[x_], writes=[B_x1],
                          sembuf=xr_st[tt % 3])
                    S.op("act", lambda e: e.activation(out=junk2.t[:], in_=x_.t[:], func=AF.Square,
                                                       accum_out=m.t[:, 0:1]), reads=[x_], writes=[junk2, m])
                    S.op("dve", lambda e: e.tensor_scalar(m.t[:, 1:2], m.t[:, 0:1], 1.0 / D, EPS, ALU.mult, ALU.add),
                         reads=[m], writes=[m])
                    S.op("act", lambda e: e.activation(out=m.t[:, 2:3], in_=m.t[:, 1:2], func=AF.Sqrt),
                         reads=[m], writes=[m])
                    S.op("dve", lambda e: e.reciprocal(m.t[:, 3:4], m.t[:, 2:3]), reads=[m], writes=[m])
                    S.op("dve", lambda e: e.scalar_tensor_tensor(xn2[tt % 2].t[:], x_.t[:], m.t[:, 3:4], grow2.t[:],
                                                                 ALU.mult, ALU.mult),
                         reads=[x_, m, grow2], writes=[xn2[tt % 2]])

                def o_n2(tt):
                    xn_, tp_, sg_ = xn2[tt % 2], tp2[tt % 2], stg2[tt % 2]
                    for k in range(KC):
                        S.op("pe", lambda e, k=k: e.transpose(tp_.t[:, k, :], xn_.t[:, k * P:(k + 1) * P], ident),
                             reads=[xn_, cb], writes=[tp_], inc=(k == KC - 1))
                    S.op("act", lambda e: e.copy(out=sg_.t[:], in_=tp_.t[:]), reads=[tp_], writes=[sg_])
                    S.dma("pool", bass.AP(s_h2, tt * P * KC * P, [[KC * P, P], [1, KC * P]]),
                          bass.AP(sg_.t, 0, [[KC * P, P], [1, KC * P]]), reads=[sg_], writes=[B_h2], sembuf=sg_)

                o_load(0)
                o_load(1)
                o_load(2)
                for cg in range(4):
                    o_mm(0, (cg,), bank=lambda t, c: (2 * c + t) % 4)
                    o_mm(1, (cg,), bank=lambda t, c: (2 * c + t) % 4)
                o_n1(0)
                o_n1(1)
                o_n2(0)
                for idx in range(2, 17):
                    if idx + 1 < 16:
                        o_load(idx + 1)
                    if idx < 16:
                        o_mm(idx)
                        o_n1(idx)
                    o_n2(idx - 1)
                S.wait_all("sp", [B_x1, B_h2])
                S.wait_all("pool", [B_x1, B_h2])
        if stop == "x1":
            return nc, dbg_outs, S

        with stage() as st:
            h2T = SB(st, "h2T", [P, KC, 1024], BF16)
            h2_b = [Buf(f"h2b{i}") for i in range(8)]
            h2_b2 = [Buf(f"h2c{i}") for i in range(8)]
            actT = SB(st, "actT", [P, FKC, 1024], BF16)
            wd0 = SB(st, "wd0", [P, FKC, 256], BF16)
            wgu0 = SB(st, "wgu0", [P, 2, KC, 256], BF16)

            def load_gu0():
                S.dma("pool", wgu0.t[:, 0], w_src(w_gate, FH, 0, 256), writes=[wgu0], sembuf=wgu0)
                S.dma("pool", wgu0.t[:, 1], w_src(w_up, FH, 0, 256), writes=[wgu0], sembuf=wgu0)
            act_b = [Buf(f"actb{i}") for i in range(8)]

            def load_h2(half_):
                for t_ in range(8):
                    S.dma("sp", h2T.t[:, :, t_ * P:(t_ + 1) * P],
                          bass.AP(s_h2, (half_ * 8 + t_) * P * KC * P, [[KC * P, P], [P, KC], [1, P]]),
                          reads=[B_h2], writes=[h2_b[t_], h2_b2[t_]], sembuf=h2_b[t_])
            for half in range(2):
                t0 = half * 1024
                if half == 0:
                    load_h2(0)
                with stage() as st2:
                    wgu = [wgu0, SB(st2, "wgu1", [P, 2, KC, 256], BF16)]
                    sgl = [SB(st2, f"sgl{i}", [P, 512], F32) for i in range(2)]
                    pg = [PS(st2, f"pg{i}", [P, 512]) for i in range(3)]
                    pu = [PS(st2, f"pu{i}", [P, 512]) for i in range(3)]

                    def load_gu(gi):
                        j = gi % 2
                        S.dma("pool", wgu[j].t[:, 0], w_src(w_gate, FH, gi * 256, 256), writes=[wgu[j]], sembuf=wgu[j])
                        S.dma("pool", wgu[j].t[:, 1], w_src(w_up, FH, gi * 256, 256), writes=[wgu[j]], sembuf=wgu[j])

                    if half == 0:
                        load_gu0()
                    ui = 0
                    for gi in range(22):
                        if gi + 1 < 22:
                            load_gu(gi + 1)
                        if gi == 17:
                            S.dma("pool", wd0.t[:], w_src(w_down, D, 0, 256, nk=FKC), writes=[wd0], sembuf=wd0)
                        w = wgu[gi % 2]
                        for hc in range(2):
                            fc = gi * 2 + hc
                            for tg in range(2):
                                r = ui % 3
                                ui += 1
                                rds = [w] + h2_b[tg * 4:(tg + 1) * 4] + h2_b2[tg * 4:(tg + 1) * 4]
                                mm_group(S, pg[r].t[:], [(w.t[:, 0, k, hc * P:(hc + 1) * P], h2T.t[:, k, tg * 512:(tg + 1) * 512])
                                                         for k in range(KC)], reads=rds, writes=[pg[r]])
                                mm_group(S, pu[r].t[:], [(w.t[:, 1, k, hc * P:(hc + 1) * P], h2T.t[:, k, tg * 512:(tg + 1) * 512])
                                                         for k in range(KC)], reads=rds, writes=[pu[r]])
                                sg_ = sgl[ui % 2]
                                S.op("act", lambda e: e.activation(out=sg_.t[:], in_=pg[r].t[:], func=AF.Silu),
                                     reads=[pg[r]], writes=[sg_])
                                S.op("dve", lambda e: e.tensor_tensor(actT.t[:, fc, tg * 512:(tg + 1) * 512], sg_.t[:],
                                                                      pu[r].t[:], ALU.mult),
                                     reads=[sg_, pu[r]], writes=act_b[tg * 4:(tg + 1) * 4])
                with stage() as st3:
                    wd = [wd0, SB(st3, "wd1", [P, FKC, 256], BF16)]
                    x1q = [SB(st3, f"x1q{i}", [P, 256], F32) for i in range(3)]
                    oq = [SB(st3, f"oq{i}", [P, 256], F32) for i in range(3)]
                    pd = [PS(st3, f"pd{i}", [P, 512]) for i in range(4)]

                    def load_wd(cg):
                        S.dma("pool", wd[cg % 2].t[:], w_src(w_down, D, cg * 256, 256, nk=FKC), writes=[wd[cg % 2]],
                              sembuf=wd[cg % 2])

                    units = [(cg, tt) for cg in range(8) for tt in range(8)]

                    def load_x1(u):
                        cg, tt = units[u]
                        S.dma("sp", x1q[u % 3].t[:], s_x1.ap()[t0 + tt * P:t0 + (tt + 1) * P, cg * 256:(cg + 1) * 256],
                              reads=[B_x1], writes=[x1q[u % 3]], sembuf=x1q[u % 3])

                    load_x1(0)
                    load_x1(1)
                    for u, (cg, tt) in enumerate(units):
                        if half == 0 and u == 4:
                            load_h2(1)
                        if half == 0 and cg == 6 and tt == 0:
                            load_gu0()
                        if tt == 0 and cg + 1 < 8:
                            load_wd(cg + 1)
                        if u + 2 < len(units):
                            load_x1(u + 2)
                        ps = pd[u % 4]
                        w = wd[cg % 2]
                        mm_group(S, ps.t[:, 0:256], [(actT.t[:, k, tt * P:(tt + 1) * P], w.t[:, k, :]) for k in range(FKC)],
                                 reads=[w, act_b[tt]], writes=[ps])
                        o_ = oq[u % 3]
                        S.op("dve", lambda e: e.tensor_tensor(o_.t[:], ps.t[:, 0:256], x1q[u % 3].t[:], ALU.add),
                             reads=[ps, x1q[u % 3]], writes=[o_])
                        S.dma("pool", out_d.ap()[t0 + tt * P:t0 + (tt + 1) * P, cg * 256:(cg + 1) * 256], o_.t[:],
                              reads=[o_], writes=[B_out], sembuf=o_)
            S.wait_all("sp", [B_out])
            S.wait_all("pool", [B_out])
    return nc, dbg_outs, S


def host_consts(s):
    pj = np.arange(P)
    cbm = np.zeros((P, CB_N), np.float32)
    cbm[:, CB_IDENT:CB_IDENT + P] = np.eye(P)
    cbm[:, CB_ONES:CB_ONES + P] = 1.0
    prev = (pj[:, None] >= pj[None, :]).astype(np.float32)
    cur = (pj[:, None] <= pj[None, :]).astype(np.float32)
    cbm[:, CB_MASKA:CB_MASKA + P] = prev
    cbm[:, CB_MASKA + P:CB_MASKA + 2 * P] = cur
    cbm[:, CB_MASKC:CB_MASKC + P] = prev * (1.0 if s == 1 else 0.0)
    cbm[:, CB_MASKC + P:CB_MASKC + 2 * P] = cur
    cbm[:, CB_MBA:CB_MBA + 2 * P] = (1.0 - cbm[:, CB_MASKA:CB_MASKA + 2 * P]) * MNEG
    cbm[:, CB_MBC:CB_MBC + 2 * P] = (1.0 - cbm[:, CB_MASKC:CB_MASKC + 2 * P]) * MNEG
    return cbm.astype(ml_dtypes.bfloat16)


def make_in_maps(x, norm_mix_g, w_in, conv_w, conv_b, gate_b, q_norm_g, k_norm_g, mlstm_norm_g,
                 w_out, norm_ffn_g, w_gate, w_up, w_down):
    f = lambda a: np.ascontiguousarray(np.asarray(a, dtype=np.float32))
    x = f(x)
    w_in_, w_out_, w_gate_, w_up_, w_down_ = f(w_in[0]), f(w_out[0]), f(w_gate[0]), f(w_up[0]), f(w_down[0])
    pp = np.zeros((P, PP_N), np.float32)
    cw = f(conv_w[0])
    pp[:, PP_CONVW:PP_CONVW + 64] = cw.reshape(4, 16, P).transpose(2, 1, 0).reshape(P, 64)
    pp[:, PP_CONVB:PP_CONVB + 16] = f(conv_b[0]).reshape(16, P).T
    pp[:, PP_QG] = f(q_norm_g[0])
    pp[:, PP_KG] = f(k_norm_g[0])
    pp[:, PP_EPS] = EPS
    pj = np.arange(P)
    pp[:, PP_IDENT:PP_IDENT + P] = np.eye(P)
    pp[:, PP_TRINEG:PP_TRINEG + P] = -(pj[:, None] <= pj[None, :]).astype(np.float32)
    pp[:, PP_NEGONES:PP_NEGONES + P] = -1.0
    pp[:, PP_ONES:PP_ONES + P] = 1.0
    rp = np.zeros((P, RP_N), np.float32)
    rp[:, RP_G1:RP_G1 + D] = f(norm_mix_g[0])[None]
    rp[:, RP_G2:RP_G2 + D] = f(norm_ffn_g[0])[None]
    rp[:, RP_MG:RP_MG + 1024] = f(mlstm_norm_g[0]).reshape(1, 1024)
    rp[:, RP_GB:RP_GB + 16] = f(gate_b[0])[None]
    rp[:, RP_QG:RP_QG + P] = f(q_norm_g[0])[None]
    rp[:, RP_KG:RP_KG + P] = f(k_norm_g[0])[None]
    cbs = [host_consts(0), host_consts(1)]
    in_maps = []
    for core in range(8):
        b, s = core // 2, core % 2
        if s == 1:
            xl = x[b]
        else:
            xl = np.concatenate([np.zeros((TC, D), np.float32), x[b, :TO]], axis=0)
        ppc = pp.copy()
        ppc[:, PP_CBIAS] = 0.0 if s == 1 else MNEG
        in_maps.append({"x_loc": np.ascontiguousarray(xl), "w_in": w_in_, "w_out": w_out_, "w_gate": w_gate_,
                        "w_up": w_up_, "w_down": w_down_, "pp": ppc, "rp": rp, "cb": cbs[s]})
    return in_maps


_NC_CACHE = {}


def kernel(**inputs):
    in_maps = make_in_maps(**inputs)
    if "nc" not in _NC_CACHE:
        _NC_CACHE["nc"] = build()[0]
    nc = _NC_CACHE["nc"]
    res = run_bass_kernel_spmd(nc, in_maps, core_ids=list(range(8)))
    out = np.empty((4, 4096, D), np.float32)
    for core in range(8):
        b, s = core // 2, core % 2
        out[b, s * TO:(s + 1) * TO] = np.asarray(res.results[core]["out"], dtype=np.float32)
    return out
```
